# Optimizing a Trainium2 kernel written in Bass

```python
import math, functools
import jax, jax.numpy as jnp
from jax import lax
import numpy as np

D_MODEL = 2048
BATCH = 4
SEQ = 2048
DEPTH = 2

CTX_LEN = 256
GRID_W = 64
EPS = 1e-6
SHORT_CONV = 3

A_DK = 128
A_DV = 128
A_HEADS = D_MODEL // 128
A_QK = A_HEADS * A_DK
A_WIDTH = A_HEADS * A_DV
DN_CHUNK = 64
B_WIDTH = D_MODEL
C_WIDTH = D_MODEL
C_GROUPS = D_MODEL // 128
C_CHUNK = 128
D_WIDTH = D_MODEL
D_GROUPS = 4
D_GW = D_WIDTH // D_GROUPS
POOL_RADII = (1, 2, 4, 8)

N_EVEN = (DEPTH + 1) // 2
N_ODD = DEPTH // 2

QKV_W = 2 * A_QK + A_WIDTH
EVEN_SPLITS = (QKV_W, QKV_W + A_WIDTH, QKV_W + A_WIDTH + 4 * A_HEADS,
               QKV_W + A_WIDTH + 4 * A_HEADS + B_WIDTH,
               QKV_W + A_WIDTH + 4 * A_HEADS + 2 * B_WIDTH,
               QKV_W + A_WIDTH + 4 * A_HEADS + 3 * B_WIDTH)
EVEN_COLS = QKV_W + A_WIDTH + 4 * A_HEADS + 4 * B_WIDTH
ODD_SPLITS = (2 * C_WIDTH, 3 * C_WIDTH, 3 * C_WIDTH + D_WIDTH)
ODD_COLS = 3 * C_WIDTH + 2 * D_WIDTH

kernel_name = 'hybrid_deltanet_shortconv_chunkmlp_pool_prefix_dit'


def _rmsnorm(x, w):
    xf = x.astype(jnp.float32)
    y = xf * lax.rsqrt(jnp.mean(xf * xf, axis=-1, keepdims=True) + EPS)
    return (y * w).astype(x.dtype)


def _layernorm(x, w, b):
    xf = x.astype(jnp.float32)
    mu = jnp.mean(xf, axis=-1, keepdims=True)
    xc = xf - mu
    var = jnp.mean(xc * xc, axis=-1, keepdims=True)
    return (xc * lax.rsqrt(var + EPS) * w + b).astype(x.dtype)


def _l2norm(x):
    return x * lax.rsqrt(jnp.sum(x * x, axis=-1, keepdims=True) + EPS)


def _dwconv(t, w):
    K = w.shape[0]
    r = K // 2
    L = t.shape[-2]
    tp = jnp.pad(t, [(0, 0)] * (t.ndim - 2) + [(r, r), (0, 0)])
    out = tp[..., 0:L, :] * w[0]
    for i in range(1, K):
        out = out + tp[..., i:i + L, :] * w[i]
    return out


def _centred_mean(t, r):
    L = t.shape[-2]
    cs = jnp.cumsum(t.astype(jnp.float32), axis=-2)
    cs = jnp.concatenate([jnp.zeros_like(cs[..., :1, :]), cs], axis=-2)
    idx = jnp.arange(L)
    hi = jnp.minimum(idx + r + 1, L)
    lo = jnp.maximum(idx - r, 0)
    s = cs[..., hi, :] - cs[..., lo, :]
    cnt = (hi - lo).astype(jnp.float32)[:, None]
    return (s / cnt).astype(t.dtype)


def _on_rows(fn, t, arg):
    Bn, L, C = t.shape
    rows = L // GRID_W
    return fn(t.reshape(Bn, rows, GRID_W, C), arg).reshape(Bn, L, C)


_conv_latent = functools.partial(_on_rows, _dwconv)
_mean_latent = functools.partial(_on_rows, _centred_mean)


def _gdn_chunk_scan(q, k, v, g, beta, s0):
    Bn, H, L, dk = q.shape
    dv = v.shape[-1]
    C = DN_CHUNK
    n = L // C
    q = q * (dk ** -0.5)
    kb = k * beta[..., None]
    vb = v * beta[..., None]
    rs = lambda t: t.reshape((Bn, H, n, C) + t.shape[3:])
    q, k, kb, vb, g = rs(q), rs(k), rs(kb), rs(vb), rs(g)
    gc = jnp.cumsum(g, axis=-1)
    tril = jnp.tril(jnp.ones((C, C), dtype=bool))
    strict = jnp.tril(jnp.ones((C, C), dtype=bool), -1)
    decay = jnp.exp(jnp.where(tril, gc[..., :, None] - gc[..., None, :], -jnp.inf))
    m = jnp.where(strict, jnp.einsum('bhnid,bhnjd->bhnij', kb, k) * decay, 0.0)
    eye = jnp.eye(C, dtype=jnp.float32)
    t_inv = lax.linalg.triangular_solve(eye + m, jnp.broadcast_to(eye, m.shape),
                                        left_side=True, lower=True, unit_diagonal=True)
    w = jnp.einsum('bhnij,bhnjd->bhnid', t_inv, kb * jnp.exp(gc)[..., None])
    u = jnp.einsum('bhnij,bhnjd->bhnid', t_inv, vb)
    a_intra = jnp.where(tril, jnp.einsum('bhnid,bhnjd->bhnij', q, k) * decay, 0.0)
    qg = q * jnp.exp(gc)[..., None]
    kg = k * jnp.exp(gc[..., -1:] - gc)[..., None]
    g_last = jnp.exp(gc[..., -1])

    def step(S, xs):
        w_i, u_i, a_i, qg_i, kg_i, gl_i = xs
        v_new = u_i - w_i @ S
        o = qg_i @ S + a_i @ v_new
        S = S * gl_i[..., None, None] + jnp.swapaxes(kg_i, -1, -2) @ v_new
        return S, o

    mv = lambda t: jnp.moveaxis(t, 2, 0)
    S, o = lax.scan(step, s0, (mv(w), mv(u), mv(a_intra), mv(qg), mv(kg), mv(g_last)))
    o = jnp.moveaxis(o, 0, 2).reshape(Bn, H, L, dv)
    return o, S


def _even_stream(p, conv, w_conv_qkv, a_log, dt_bias, w_conv_b):
    qkv, z_a, ab, b_g, c_g, h_b, z_b = jnp.split(p, EVEN_SPLITS, axis=-1)
    Bn, L, _ = p.shape
    qkv = jax.nn.silu(conv(qkv, w_conv_qkv))
    q, k, v = jnp.split(qkv, (A_QK, 2 * A_QK), axis=-1)
    heads = lambda t, d: jnp.swapaxes(t.reshape(Bn, L, A_HEADS, d), 1, 2).astype(jnp.float32)
    q = _l2norm(heads(q, A_DK))
    k = _l2norm(heads(k, A_DK))
    v = heads(v, A_DV)
    ab = jnp.transpose(ab.astype(jnp.float32).reshape(Bn, L, 4, A_HEADS), (2, 0, 3, 1))
    a_raw, b_raw = ab[:2], ab[2:]
    g = -jnp.exp(a_log.astype(jnp.float32))[:, None, :, None] * jax.nn.softplus(
        a_raw + dt_bias.astype(jnp.float32)[:, None, :, None])
    beta = jax.nn.sigmoid(b_raw)
    y_b = b_g * conv(c_g * h_b, w_conv_b) * jax.nn.silu(z_b)
    return (q, k, v, g, beta), z_a, y_b


def _even_mixer(h_x, h_c, w_in, w_conv_qkv, a_log, dt_bias, head_norm, w_conv_b, w_out):
    (qx, kx, vx, gx, bx), zx, ybx = _even_stream(h_x @ w_in, _conv_latent, w_conv_qkv, a_log, dt_bias, w_conv_b)
    (qc, kc, vc, gcx, bc), zc, ybc = _even_stream(h_c @ w_in, _dwconv, w_conv_qkv, a_log, dt_bias, w_conv_b)
    Bn = h_x.shape[0]
    o_x = 0.0
    o_c = 0.0
    for d in range(2):
        f = (lambda t: jnp.flip(t, axis=2)) if d == 1 else (lambda t: t)
        s0 = jnp.zeros((Bn, A_HEADS, A_DK, A_DV), jnp.float32)
        oc_d, s_ctx = _gdn_chunk_scan(f(qc), f(kc), f(vc), f(gcx[d]), f(bc[d]), s0)
        ox_d, _ = _gdn_chunk_scan(f(qx), f(kx), f(vx), f(gx[d]), f(bx[d]), s_ctx)
        o_c = o_c + f(oc_d)
        o_x = o_x + f(ox_d)

    def finish(o, z, y_b):
        Bq, H, L, dv = o.shape
        o = _rmsnorm(o, head_norm)
        o = jnp.swapaxes(o, 1, 2).reshape(Bq, L, A_WIDTH).astype(z.dtype) * jax.nn.silu(z)
        return jnp.concatenate([o, y_b], axis=-1) @ w_out

    return finish(o_x, zx, ybx), finish(o_c, zc, ybc)


def _chunk_token_mix(u, v, w_s, b_s):
    Bn, L, _ = v.shape
    n = L // C_CHUNK
    vr = v.reshape(Bn, n, C_CHUNK, C_GROUPS, C_WIDTH // C_GROUPS)
    s = jnp.einsum('gpq,bnqgc->bnpgc', w_s, vr) + b_s.T[:, :, None]
    return u * s.reshape(Bn, L, C_WIDTH)


def _multi_scale_pool(p, mean_fn, pool_w, pool_scale):
    groups = jnp.split(p, D_GROUPS, axis=-1)
    diffs = jnp.stack([mean_fn(gp, r) - gp for gp, r in zip(groups, POOL_RADII)], axis=-2)
    y = jnp.einsum('blgc,gcd->blgd', diffs, pool_w).reshape(p.shape)
    return y * pool_scale


def _odd_stream(p, mean_fn, ln_w, ln_b, w_s, b_s, pool_w, pool_scale, w_out):
    uv, z_c, p_d, z_d = jnp.split(p, ODD_SPLITS, axis=-1)
    u, v = jnp.split(jax.nn.gelu(uv), 2, axis=-1)
    y_c = _chunk_token_mix(u, _layernorm(v, ln_w, ln_b), w_s, b_s) * jax.nn.silu(z_c)
    y_d = _multi_scale_pool(p_d, mean_fn, pool_w, pool_scale) * jax.nn.silu(z_d)
    return jnp.concatenate([y_c, y_d], axis=-1) @ w_out


def setup_inputs(seed: int = 0) -> dict:
    key = jax.random.key(seed)
    ks = jax.random.split(key, 24)
    nrm = lambda k, shape, s: jax.random.normal(k, shape, jnp.float32) * s
    D = D_MODEL
    x = nrm(ks[0], (BATCH, SEQ, D), 1.0)
    c = nrm(ks[1], (BATCH, D), 1.0)
    ctx = nrm(ks[2], (BATCH, CTX_LEN, D), 1.0)
    c_ctx = nrm(ks[3], (D,), 1.0)
    ada_w = nrm(ks[4], (DEPTH, D, 3 * D), 0.5 * D ** -0.5)
    ada_b = nrm(ks[5], (DEPTH, 3 * D), 0.02)
    norm_w = 1.0 + nrm(ks[6], (DEPTH, D), 0.02)
    e_w_in = nrm(ks[7], (N_EVEN, D, EVEN_COLS), D ** -0.5)
    e_conv_qkv = nrm(ks[8], (N_EVEN, SHORT_CONV, QKV_W), SHORT_CONV ** -0.5)
    e_a_log = jnp.log(jax.random.uniform(ks[9], (N_EVEN, 2, A_HEADS), jnp.float32, 1.0, 16.0))
    dt = jnp.exp(jax.random.uniform(ks[10], (N_EVEN, 2, A_HEADS), jnp.float32, math.log(1e-3), math.log(0.1)))
    e_dt_bias = dt + jnp.log(-jnp.expm1(-dt))
    e_head_norm = 1.0 + nrm(ks[11], (N_EVEN, A_DV), 0.02)
    e_conv_b = nrm(ks[12], (N_EVEN, SHORT_CONV, B_WIDTH), SHORT_CONV ** -0.5)
    e_w_out = nrm(ks[13], (N_EVEN, A_WIDTH + B_WIDTH, D), (A_WIDTH + B_WIDTH) ** -0.5)
    o_w_in = nrm(ks[14], (N_ODD, D, ODD_COLS), D ** -0.5)
    o_ln_w = 1.0 + nrm(ks[15], (N_ODD, C_WIDTH), 0.02)
    o_ln_b = nrm(ks[16], (N_ODD, C_WIDTH), 0.02)
    o_w_s = nrm(ks[17], (N_ODD, C_GROUPS, C_CHUNK, C_CHUNK), C_CHUNK ** -0.5)
    o_b_s = nrm(ks[18], (N_ODD, C_GROUPS, C_CHUNK), 0.02)
    o_pool_w = nrm(ks[19], (N_ODD, D_GROUPS, D_GW, D_GW), D_GW ** -0.5)
    o_pool_scale = 1.0 + nrm(ks[20], (N_ODD, D_WIDTH), 0.1)
    o_w_out = nrm(ks[21], (N_ODD, C_WIDTH + D_WIDTH, D), (C_WIDTH + D_WIDTH) ** -0.5)
    final_norm_w = 1.0 + nrm(ks[22], (D,), 0.02)
    return {'x': x, 'c': c, 'ctx': ctx, 'c_ctx': c_ctx, 'ada_w': ada_w, 'ada_b': ada_b, 'norm_w': norm_w,
            'e_w_in': e_w_in, 'e_conv_qkv': e_conv_qkv, 'e_a_log': e_a_log, 'e_dt_bias': e_dt_bias,
            'e_head_norm': e_head_norm, 'e_conv_b': e_conv_b, 'e_w_out': e_w_out,
            'o_w_in': o_w_in, 'o_ln_w': o_ln_w, 'o_ln_b': o_ln_b, 'o_w_s': o_w_s, 'o_b_s': o_b_s,
            'o_pool_w': o_pool_w, 'o_pool_scale': o_pool_scale, 'o_w_out': o_w_out,
            'final_norm_w': final_norm_w}


def reference(x, c, ctx, c_ctx, ada_w, ada_b, norm_w, e_w_in, e_conv_qkv, e_a_log, e_dt_bias, e_head_norm,
              e_conv_b, e_w_out, o_w_in, o_ln_w, o_ln_b, o_w_s, o_b_s, o_pool_w, o_pool_scale, o_w_out,
              final_norm_w):
    for layer in range(DEPTH):
        last = layer == DEPTH - 1
        even = layer % 2 == 0
        i = layer // 2
        need_ctx = even or not last
        mod_x = jax.nn.silu(c) @ ada_w[layer] + ada_b[layer]
        sh_x, sc_x, gt_x = jnp.split(mod_x[:, None, :], 3, axis=-1)
        h_x = _rmsnorm(x, norm_w[layer]) * (1.0 + sc_x) + sh_x
        if need_ctx:
            mod_c = jax.nn.silu(c_ctx) @ ada_w[layer] + ada_b[layer]
            sh_c, sc_c, gt_c = jnp.split(mod_c, 3, axis=-1)
            h_c = _rmsnorm(ctx, norm_w[layer]) * (1.0 + sc_c) + sh_c
        if even:
            y_x, y_c = _even_mixer(h_x, h_c, e_w_in[i], e_conv_qkv[i], e_a_log[i], e_dt_bias[i],
                                   e_head_norm[i], e_conv_b[i], e_w_out[i])
        else:
            odd_args = (o_ln_w[i], o_ln_b[i], o_w_s[i], o_b_s[i], o_pool_w[i], o_pool_scale[i], o_w_out[i])
            y_x = _odd_stream(h_x @ o_w_in[i], _mean_latent, *odd_args)
            if need_ctx:
                y_c = _odd_stream(h_c @ o_w_in[i], _centred_mean, *odd_args)
        x = x + gt_x * y_x
        if not last:
            ctx = ctx + gt_c * y_c
    return _rmsnorm(x, final_norm_w)
```

```python
import numpy as np
from contextlib import ExitStack
import concourse.bass as bass
import concourse.mybir as mybir
from concourse.bass_utils import run_bass_kernel_spmd

F32 = mybir.dt.float32
BF16 = mybir.dt.bfloat16
AF = mybir.ActivationFunctionType
ALU = mybir.AluOpType

D = 2048
NT = 1024
NCTX = 256
EPS = 1e-6
BIG = 30000.0
EVEN_COLS = 16448
ODD_COLS = 10240
WSPLIT = 2

C_ID, C_TA, C_TD, C_MA, C_MD, C_BAND, C_ONES, C_BM, NCST = 0, 128, 256, 384, 768, 1152, 1664, 1792, 2432
V_C, V_CC, V_AB0, V_AB1, V_NW0, V_NW1, V_LNW, V_LNB, V_PS, V_FNW = 0, 16, 32, 80, 128, 144, 160, 176, 192, 208
V_CQ, V_CB, V_HN, V_SEL, V_DTB, V_ALOG, NV = 224, 368, 416, 417, 419, 451, 512


class Buf:
    __slots__ = ("name", "w", "r", "excl")

    def __init__(self, name, excl=False):
        self.name = name
        self.w = None
        self.r = []
        self.excl = excl


class Prog:
    ENGS = ("pe", "act", "dve", "pool", "sp")

    def __init__(self, nc, es):
        self.nc = nc
        self.es = es
        self.q = {k: [] for k in self.ENGS}
        self.sems = {}
        self.cnt = {}
        self.known = {k: {} for k in self.ENGS}
        self.nbank = 0
        self.nslot = 0
        self.nw = 3
        self.rings = {}

    def _sem(self, key):
        if key not in self.sems:
            self.sems[key] = self.es.enter_context(self.nc.semaphore("s_" + key))
            self.cnt[key] = 0
        return self.sems[key]

    def op(self, eng, fn, R=(), W=(), key=None, inc=1):
        deps = []
        for b in R:
            if b.w is not None:
                deps.append(b.w)
            if b.excl:
                deps.extend(b.r)
        for b in W:
            if b.w is not None:
                deps.append(b.w)
            deps.extend(b.r)
        waits = {}
        kn = self.known[eng]
        for (k, v) in deps:
            if k == "pe" and eng == "pe" and key is None:
                continue
            if kn.get(k, 0) >= v:
                continue
            if waits.get(k, 0) < v:
                waits[k] = v
        for k, v in waits.items():
            kn[k] = v
        if key is None:
            key = eng
        self._sem(key)
        self.cnt[key] += inc
        t = (key, self.cnt[key])
        self.q[eng].append((tuple(waits.items()), fn, key, inc))
        for b in R:
            b.r.append(t)
        for b in W:
            b.w = t
            b.r = []
        return t

    def barrier(self):
        snap = {k: v for k, v in self.cnt.items() if v > 0}
        for eng in self.ENGS:
            kn = self.known[eng]
            waits = tuple((k, v) for k, v in snap.items() if kn.get(k, 0) < v)
            for k, v in waits:
                kn[k] = v
            self.q[eng].append((waits, None, None, 0))

    def mm(self, out, lhsT, rhs, start, stop, R, W):
        return self.op("pe", lambda e: e.matmul(out, lhsT, rhs, start=start, stop=stop, skip_group_check=True), R, W)

    def tr(self, out, in_, ident, R, W):
        return self.op("pe", lambda e: e.transpose(out, in_, ident), R, W)

    def act(self, out, in_, func, R, W, bias=None, scale=None, accum_out=None):
        kw = {}
        if bias is not None:
            kw["bias"] = bias
        if scale is not None:
            kw["scale"] = scale
        if accum_out is not None:
            kw["accum_out"] = accum_out
        return self.op("act", lambda e: e.activation(out=out, in_=in_, func=func, **kw), R, W)

    def tt(self, eng, out, in0, in1, op, R, W):
        return self.op(eng, lambda e: e.tensor_tensor(out=out, in0=in0, in1=in1, op=op), R, W)

    def ts(self, eng, out, in0, s1, s2, op0, op1, R, W, accum_out=None):
        if accum_out is not None:
            return self.op(eng, lambda e: e.tensor_scalar(out=out, in0=in0, scalar1=s1, scalar2=s2, op0=op0, op1=op1,
                                                          accum_out=accum_out), R, W)
        if op1 is None:
            return self.op(eng, lambda e: e.tensor_scalar(out=out, in0=in0, scalar1=s1, scalar2=None, op0=op0), R, W)
        return self.op(eng, lambda e: e.tensor_scalar(out=out, in0=in0, scalar1=s1, scalar2=s2, op0=op0, op1=op1), R, W)

    def stt(self, out, in0, scalar, in1, op0, op1, R, W, accum_out=None):
        if accum_out is not None:
            return self.op("dve", lambda e: e.scalar_tensor_tensor(out=out, in0=in0, scalar=scalar, in1=in1, op0=op0,
                                                                   op1=op1, accum_out=accum_out), R, W)
        return self.op("dve", lambda e: e.scalar_tensor_tensor(out=out, in0=in0, scalar=scalar, in1=in1, op0=op0,
                                                               op1=op1), R, W)

    def copy(self, eng, out, in_, R, W):
        if eng == "act":
            return self.op("act", lambda e: e.copy(out=out, in_=in_), R, W)
        return self.op(eng, lambda e: e.tensor_copy(out=out, in_=in_), R, W)

    def recip(self, out, in_, R, W):
        return self.op("dve", lambda e: e.reciprocal(out=out, in_=in_), R, W)

    def memset(self, eng, ap, val, W):
        return self.op(eng, lambda e: e.memset(ap, val), (), W)

    def dma(self, eng, out, in_, R, W, key, slow=False):
        if key == "cld":
            self.ncld = getattr(self, "ncld", 0) + 1
            key = f"cld{self.ncld}"
        if slow:
            return self.op(eng, lambda e: e.dma_start(out=out, in_=in_, allow_slow_non_contiguous=True), R, W, key=key, inc=16)
        return self.op(eng, lambda e: e.dma_start(out=out, in_=in_), R, W, key=key, inc=16)

    def flush(self, block):
        engs = {"pe": block.tensor, "act": block.scalar, "dve": block.vector, "pool": block.gpsimd, "sp": block.sync}
        for name in self.ENGS:
            items = self.q[name]
            sems = self.sems
            final = []
            if name == "sp":
                final = [(k, self.cnt[k]) for k in self.cnt if k.startswith("out")]

            def body(e, items=items, final=final):
                for waits, fn, key, inc in items:
                    for k, v in waits:
                        e.wait_ge(sems[k], v)
                    if fn is not None:
                        fn(e).then_inc(sems[key], inc)
                for k, v in final:
                    e.wait_ge(sems[k], v)

            engs[name](body)


class Ring:
    def __init__(self, aps, name):
        self.aps = aps
        self.bufs = [Buf(f"{name}{i}") for i in range(len(aps))]
        self.i = 0

    def next(self):
        k = self.i % len(self.aps)
        self.i += 1
        return self.aps[k], self.bufs[k]


def build(mode="full", nheads=16, groups=None, stop_after=None):
    nc = bass.Bass("TRN2", target_bir_lowering=False)
    dr = {}

    def din(name, shape):
        dr[name] = nc.dram_tensor(name, list(shape), F32, kind="ExternalInput").ap()
        return dr[name]

    vec_d = din("vec", [128, NV])
    cst_d = din("cst", [128, NCST])
    adaw_d = [din("ada_w0", [D, 3 * D]), din("ada_w1", [D, 3 * D])]
    owin_d = din("o_w_in", [D, ODD_COLS])
    ws_d = din("ws", [128, 2048])
    bsb_d = din("bsb", [128, 2048])
    opw_d = din("o_pool_w", [2048, 512])
    owout_d = din("o_w_out", [2 * D, D])
    if mode in ("full", "L0"):
        xs_d = din("xs", [NT, D])
        ctx_d = din("ctxs", [NCTX, D])
        ewhd_d = din("e_w_hd", [16 * D, 512])
        ewmb_d = din("e_w_mb", [16 * D, 512])
        wab_d = din("w_ab", [D, 64])
        ewout_d = din("e_w_out", [2 * D, D])
    if mode == "L1":
        x1in_d = din("x1T_in", [128, 16 * NT])
    if mode == "L0":
        out_d = nc.dram_tensor("out", [128, 16 * NT], F32, kind="ExternalOutput").ap()
    else:
        out_d = nc.dram_tensor("out", [NT, D], F32, kind="ExternalOutput").ap()

    with ExitStack() as es:
        P = Prog(nc, es)

        def sb(name, shape, dt):
            return es.enter_context(nc.sbuf_tensor("sb_" + name, list(shape), dt))

        x1T_t = sb("x1T", [128, 16 * NT], F32)
        G1_t = sb("G1", [128, 8192], F32)
        G2_t = sb("G2", [128, 10240], F32)
        W_t = [sb(f"W{i}", [128, 16 * 512], BF16) for i in range(3)]
        cst = sb("cst", [128, C_BAND], F32)
        vec = sb("vec", [128, NV], F32)
        cbf = sb("cbf", [128, 12 * 128], BF16)
        mod_t = sb("mod", [128, 2 * 96], F32)
        der_t = sb("der", [128, 2 * 5 * 16], F32)
        small = sb("small", [128, 256], F32)
        scin = sb("scin", [128, 32], BF16)
        aux = sb("aux", [128, 2048], F32)
        mb_ring = Ring([aux[:, i * 512:(i + 1) * 512] for i in range(4)], "mb")
        ps_all = es.enter_context(nc.psum_tensor("ps_all", [128, 4096], F32))
        psum = [ps_all[:, i * 512:(i + 1) * 512] for i in range(8)]
        pbuf = [Buf(f"ps{i}", excl=True) for i in range(8)]

        x1T = x1T_t[:].rearrange("p (c t) -> p c t", c=16)
        x1b = [[Buf(f"x1_{dc}_{h}") for h in range(2)] for dc in range(16)]
        Wap = [w[:].rearrange("p (c n) -> p c n", c=16) for w in W_t]
        Wbuf = [Buf(f"W{i}") for i in range(3)]
        cstb = Buf("cst")
        vecb = Buf("vec")
        cbfb = Buf("cbf")
        modb = Buf("mod")
        derb = Buf("der")
        smallb = Buf("small")
        ident_f = cst[:, C_ID:C_ID + 128]
        ident_b = cbf[:, 0:128]
        nident_b = cbf[:, 128:256]
        ones_b = cbf[:, 256:384]
        band_b = [cbf[:, 384 + 128 * i: 512 + 128 * i] for i in range(4)]
        bm_b = [cbf[:, 896 + 128 * i: 1024 + 128 * i] for i in range(5)]

        P.bank_lo = 0

        def bank():
            n = 8 - P.bank_lo
            i = P.bank_lo + (P.nbank % n)
            P.nbank += 1
            return psum[i], pbuf[i]

        def banks(n):
            i = P.nbank % 8
            if i + n > 8:
                P.nbank += 8 - i
                i = 0
            P.nbank += n
            return ps_all[:, i * 512:(i + n) * 512], [pbuf[i + k] for k in range(n)]

        def wslot():
            i = P.nslot % P.nw
            P.nslot += 1
            return Wap[i], Wbuf[i], f"w{i}"

        def load_w(src2d, r0, nrow_chunks, c0, ncols):
            ap, b, key = wslot()
            src = src2d[r0:r0 + 128 * nrow_chunks, c0:c0 + ncols].rearrange("(c p) n -> p c n", p=128)
            nsp = WSPLIT if nrow_chunks % WSPLIT == 0 else 1
            step = nrow_chunks // nsp
            for i in range(nsp):
                P.dma("pool", ap[:, i * step:(i + 1) * step, 0:ncols], src[:, i * step:(i + 1) * step, :], R=(), W=[b], key=key)
            return ap, b

        P.dma("sp", cst[:], cst_d[:, 0:C_BAND], (), [cstb], "cld")
        ctmp = G1_t[:, 0:NCST - C_BAND]
        ctmpb = Buf("ctmp")
        P.dma("sp", ctmp, cst_d[:, C_BAND:NCST], (), [ctmpb], "cld")
        P.dma("sp", vec[:], vec_d[:, :], (), [vecb], "cld")
        P.copy("dve", ident_b, ident_f, [cstb], [cbfb])
        P.ts("dve", nident_b, ident_f, -1.0, None, ALU.mult, None, [cstb], [cbfb])
        P.copy("dve", ones_b, ctmp[:, C_ONES - C_BAND:C_ONES - C_BAND + 128], [ctmpb], [cbfb])
        for i in range(4):
            P.copy("dve", band_b[i], ctmp[:, 128 * i:128 * (i + 1)], [ctmpb], [cbfb])
        for i in range(5):
            P.copy("dve", bm_b[i], ctmp[:, C_BM - C_BAND + 128 * i:C_BM - C_BAND + 128 * (i + 1)], [ctmpb], [cbfb])
        P.barrier()

        def emit_mod_gen(l):
            sc3 = scin[:].rearrange("p (c k) -> p c k", k=2)
            if l == 0:
                P.act(sc3[:, :, 0], vec[:, V_C:V_C + 16], AF.Silu, [vecb], [smallb])
                P.act(sc3[:, :, 1], vec[:, V_CC:V_CC + 16], AF.Silu, [vecb], [smallb])
            m3 = mod_t[:, l * 96:(l + 1) * 96].rearrange("p (g k) -> p g k", k=2)
            vab = V_AB0 if l == 0 else V_AB1
            for cb in range(12):
                wap, wb = load_w(adaw_d[l], 0, 16, cb * 512, 512)
                if l == 1:
                    yield
                mps, mpb = bank()
                for j in range(4):
                    for dc in range(16):
                        P.mm(mps[:, 2 * j:2 * j + 2], wap[:, dc, j * 128:(j + 1) * 128], sc3[:, dc, :],
                             dc == 0, dc == 15, [wb, smallb], [mpb])
                mp3 = mps[:, 0:8].rearrange("p (g k) -> p g k", k=2)
                for k in range(2):
                    P.tt("dve", m3[:, cb * 4:cb * 4 + 4, k], mp3[:, :, k], vec[:, vab + cb * 4:vab + cb * 4 + 4], ALU.add,
                         [mpb, vecb], [modb])
                yield
            vnw = V_NW0 if l == 0 else V_NW1
            dd = der_t[:, l * 80:(l + 1) * 80]
            P.stt(dd[:, 0:16], m3[:, 16:32, 0], 1.0, vec[:, vnw:vnw + 16], ALU.add, ALU.mult, [modb, vecb], [derb])
            P.copy("dve", dd[:, 16:32], m3[:, 0:16, 0], [modb], [derb])
            P.copy("dve", dd[:, 32:48], m3[:, 32:48, 0], [modb], [derb])
            P.stt(dd[:, 48:64], m3[:, 16:32, 1], 1.0, vec[:, vnw:vnw + 16], ALU.add, ALU.mult, [modb, vecb], [derb])
            P.copy("dve", dd[:, 64:80], m3[:, 0:16, 1], [modb], [derb])

        def emit_mod(l):
            for _ in emit_mod_gen(l):
                pass
            return der_t[:, l * 80:(l + 1) * 80]

        def emit_L1_prelude(mod_done=False):
            dd = der_t[:, 80:160] if mod_done else emit_mod(1)
            wsT_t = aux[:, 0:1024].bitcast(BF16)
            bias2_t = aux[:, 1024:2048].bitcast(BF16)
            wsTb = Buf("wsT")
            bias2b = Buf("bias2")
            wap, wb, key = wslot()
            ws_sb = wap[:, 0:4, :].rearrange("p c n -> p (c n)")
            P.dma("pool", ws_sb, ws_d[:, :], (), [wb], key)
            for q4 in range(4):
                ps, pb = bank()
                psb = ps[:].bitcast(BF16)
                for j in range(4):
                    g = q4 * 4 + j
                    P.tr(psb[:, j * 128:(j + 1) * 128], ws_sb[:, g * 128:(g + 1) * 128], ident_b, [wb, cbfb], [pb])
                P.copy("dve", wsT_t[:, q4 * 512:(q4 + 1) * 512], psb[:, 0:512], [pb], [wsTb])
            bs_sb = G1_t[:, 0:2048]
            g1b = Buf("g1tmp")
            P.dma("sp", bs_sb, bsb_d[:, :], (), [g1b], "cld")
            for q4 in range(4):
                ps, pb = bank()
                P.mm(ps[:, :], ones_b, wsT_t[:, q4 * 512:(q4 + 1) * 512], True, True, [cbfb, wsTb], [pb])
                for j in range(4):
                    g = q4 * 4 + j
                    P.stt(bias2_t[:, g * 128:(g + 1) * 128], ps[:, j * 128:(j + 1) * 128],
                          vec[:, V_LNB + g:V_LNB + g + 1], bs_sb[:, g * 128:(g + 1) * 128], ALU.mult, ALU.add,
                          [pb, vecb, g1b], [bias2b])
            return dd, wsT_t, wsTb, bias2_t, bias2b

        def emit_rstd_bc(src_of_dc, srcbufs_of_dc, ntok, tmp_ring, out_ap, outb, inv_n):
            nb = (ntok + 511) // 512
            for bi in range(nb):
                n0 = bi * 512
                n1 = min(ntok, n0 + 512)
                ps, pb = bank()
                for dc in range(16):
                    sq, sqb = tmp_ring.next()
                    P.act(sq[:, 0:n1 - n0], src_of_dc(dc)[:, n0:n1], AF.Square, srcbufs_of_dc(dc), [sqb])
                    P.mm(ps[:, 0:n1 - n0], ones_b, sq[:, 0:n1 - n0], dc == 0, dc == 15, [cbfb, sqb], [pb])
                P.ts("dve", out_ap[:, n0:n1], ps[:, 0:n1 - n0], inv_n, EPS, ALU.mult, ALU.add, [pb], [outb])
                P.act(out_ap[:, n0:n1], out_ap[:, n0:n1], AF.Sqrt, [outb], [outb])
                P.recip(out_ap[:, n0:n1], out_ap[:, n0:n1], [outb], [outb])

        def emit_L1_half(half, dd, wsT_t, wsTb, bias2_t, bias2b):
            T0 = half * 512
            h1T = G2_t[:, 0:4096].bitcast(BF16).rearrange("p (c t) -> p c t", c=16)
            vtok = G2_t[:, 4096:8192].bitcast(BF16).rearrange("p (t n) -> p t n", t=4)
            gat = G1_t[:, 0:4096].bitcast(BF16).rearrange("p (c t) -> p c t", c=16)
            tmpA = G1_t[:, 4096:8192]
            rs = G2_t[:, 8192:8704]
            misc = G2_t[:, 8704:10240]
            hb = [Buf(f"h1_{dc}") for dc in range(16)]
            vb_ = [Buf(f"vt_{t}") for t in range(4)]
            gb = [Buf(f"gat_{c}") for c in range(16)]
            rsb = Buf("rs")
            miscb = Buf("misc")
            sq_ring = Ring([tmpA[:, i * 256:(i + 1) * 256].bitcast(BF16) for i in range(3)], "sq")
            f_ring = Ring([tmpA[:, 768 + i * 512:768 + (i + 1) * 512] for i in range(6)], "f")
            emit_rstd_bc(lambda dc: x1T[:, dc, T0:T0 + 512], lambda dc: [x1b[dc][half]], 512, sq_ring, rs, rsb, 1.0 / D)
            for dc in range(16):
                t, tb = f_ring.next()
                P.tt("dve", t, x1T[:, dc, T0:T0 + 512], rs, ALU.mult, [x1b[dc][half], rsb], [tb])
                P.act(h1T[:, dc, :], t, AF.Identity, [tb, derb], [hb[dc]], bias=dd[:, 16 + dc:17 + dc],
                      scale=dd[:, dc:dc + 1])
            st = misc[:, 0:64]
            stb = [Buf(f"st{i}") for i in range(32)]
            for vbk in range(4):
                wap, wb = load_w(owin_d, 0, 16, 2048 + vbk * 512, 512)
                for t4 in range(4):
                    ps, pb = bank()
                    for dc in range(16):
                        P.mm(ps[:, :], h1T[:, dc, t4 * 128:(t4 + 1) * 128], wap[:, dc, :], dc == 0, dc == 15,
                             [hb[dc], wb], [pb])
                    vblk = vtok[:, t4, vbk * 512:(vbk + 1) * 512]
                    P.act(vblk, ps[:, :], AF.Gelu_apprx_tanh, [pb], [vb_[t4]])
                    j1, j1b = f_ring.next()
                    P.act(j1, vblk, AF.Square, [vb_[t4]], [j1b, stb[16 + t4 * 4 + vbk]],
                          accum_out=st[:, 16 + t4 * 4 + vbk:17 + t4 * 4 + vbk])
                    j2, j2b = f_ring.next()
                    P.ts("dve", j2, vblk, 1.0, 0.0, ALU.mult, ALU.add, [vb_[t4]], [j2b, stb[t4 * 4 + vbk]],
                         accum_out=st[:, t4 * 4 + vbk:t4 * 4 + vbk + 1])
            st3 = st[:, 0:32].rearrange("p (a t v) -> p a t v", a=2, v=4)
            red = misc[:, 64:72].rearrange("p (a t) -> p a t", a=2)
            P.tt("dve", red, st3[:, :, :, 0], st3[:, :, :, 1], ALU.add, stb, [miscb])
            P.tt("dve", red, red, st3[:, :, :, 2], ALU.add, [miscb], [miscb])
            P.tt("dve", red, red, st3[:, :, :, 3], ALU.add, [miscb], [miscb])
            mu = misc[:, 72:76]
            var = misc[:, 76:80]
            rstd = misc[:, 80:84]
            nmr = misc[:, 84:88]
            P.ts("dve", mu, red[:, 0, :], 1.0 / 2048, None, ALU.mult, None, [miscb], [miscb])
            P.ts("dve", var, red[:, 1, :], 1.0 / 2048, EPS, ALU.mult, ALU.add, [miscb], [miscb])
            P.tt("dve", nmr, mu, mu, ALU.mult, [miscb], [miscb])
            P.tt("dve", var, var, nmr, ALU.subtract, [miscb], [miscb])
            P.act(var, var, AF.Sqrt, [miscb], [miscb])
            P.recip(rstd, var, [miscb], [miscb])
            P.stt(nmr, mu, -1.0, rstd, ALU.mult, ALU.mult, [miscb], [miscb])
            for t4 in range(4):
                P.ts("dve", vtok[:, t4, :], vtok[:, t4, :], rstd[:, t4:t4 + 1], nmr[:, t4:t4 + 1], ALU.mult, ALU.add,
                     [vb_[t4], miscb], [vb_[t4]])
            for g4 in range(4):
                wu, wub = load_w(owin_d, 0, 16, g4 * 512, 512)
                wz, wzb = load_w(owin_d, 0, 16, 4096 + g4 * 512, 512)
                for j in range(4):
                    g = g4 * 4 + j
                    pu, pub = bank()
                    for dc in range(16):
                        P.mm(pu[:, :], wu[:, dc, j * 128:(j + 1) * 128], h1T[:, dc, :], dc == 0, dc == 15,
                             [wub, hb[dc]], [pub])
                    pz, pzb = bank()
                    for dc in range(16):
                        P.mm(pz[:, :], wz[:, dc, j * 128:(j + 1) * 128], h1T[:, dc, :], dc == 0, dc == 15,
                             [wzb, hb[dc]], [pzb])
                    pss, pssb = bank()
                    for t4 in range(4):
                        P.mm(pss[:, t4 * 128:(t4 + 1) * 128], vtok[:, t4, g * 128:(g + 1) * 128],
                             wsT_t[:, g * 128:(g + 1) * 128], True, True, [vb_[t4], wsTb], [pssb])
                    gu, gub = f_ring.next()
                    P.act(gu, pu[:, :], AF.Gelu_apprx_tanh, [pub], [gub])
                    sz, szb = f_ring.next()
                    P.act(sz, pz[:, :], AF.Silu, [pzb], [szb])
                    s2, s2b = f_ring.next()
                    b2 = bias2_t[:, g * 128:(g + 1) * 128]
                    for t4 in range(4):
                        P.stt(s2[:, t4 * 128:(t4 + 1) * 128], pss[:, t4 * 128:(t4 + 1) * 128],
                              vec[:, V_LNW + g:V_LNW + g + 1], b2, ALU.mult, ALU.add, [pssb, vecb, bias2b], [s2b])
                    P.tt("pool", gu, gu, sz, ALU.mult, [gub, szb], [gub])
                    P.tt("dve", gat[:, g, :], s2, gu, ALU.mult, [s2b, gub], [gb[g]])
            emit_wout_pass(owout_d, 0, gat, gb, half, dd)
            pt = G2_t[:, 4096:5120].bitcast(BF16).rearrange("p (t n) -> p t n", t=4)
            dfT = G2_t[:, 5120:6144].bitcast(BF16).rearrange("p (c t) -> p c t", c=4)
            ptb = [Buf(f"pt_{t}") for t in range(4)]
            dfb = [Buf(f"df_{c}") for c in range(4)]
            for pg in range(4):
                wp, wpb = load_w(owin_d, 0, 16, 6144 + pg * 512, 512)
                for t4 in range(4):
                    ps, pb = bank()
                    for dc in range(16):
                        P.mm(ps[:, :], h1T[:, dc, t4 * 128:(t4 + 1) * 128], wp[:, dc, :], dc == 0, dc == 15,
                             [hb[dc], wpb], [pb])
                    P.copy("act", pt[:, t4, 0:512], ps[:, :], [pb], [ptb[t4]] + (vb_ if pg == 0 else []))
                for cc in range(4):
                    ps, pb = bank()
                    for t4 in range(4):
                        P.mm(ps[:, t4 * 128:(t4 + 1) * 128], pt[:, t4, cc * 128:(cc + 1) * 128], band_b[pg], True, True,
                             [ptb[t4], cbfb], [pb])
                    P.copy("dve", dfT[:, cc, :], ps[:, :], [pb], [dfb[cc]] + (vb_ if pg == 0 else []))
                wz, wzb = load_w(owin_d, 0, 16, 8192 + pg * 512, 512)
                wq, wqb = load_w(opw_d, pg * 512, 4, 0, 512)
                for j in range(4):
                    g = pg * 4 + j
                    py, pyb = bank()
                    for cc in range(4):
                        P.mm(py[:, :], wq[:, cc, j * 128:(j + 1) * 128], dfT[:, cc, :], cc == 0, cc == 3,
                             [wqb, dfb[cc]], [pyb])
                    pz, pzb = bank()
                    for dc in range(16):
                        P.mm(pz[:, :], wz[:, dc, j * 128:(j + 1) * 128], h1T[:, dc, :], dc == 0, dc == 15,
                             [wzb, hb[dc]], [pzb])
                    sz, szb = f_ring.next()
                    P.act(sz, pz[:, :], AF.Silu, [pzb], [szb])
                    P.stt(gat[:, g, :], py[:, :], vec[:, V_PS + g:V_PS + g + 1], sz, ALU.mult, ALU.mult,
                          [pyb, vecb, szb], [gb[g]])
            emit_wout_pass(owout_d, 2048, gat, gb, half, dd)

        def emit_wout_pass(wsrc, r0, gat, gb, half, dd, first=False, ntok=512, tok0=None):
            T0 = half * 512 if tok0 is None else tok0
            for db in range(4):
                ww, wwb = load_w(wsrc, r0, 16, db * 512, 512)
                for j in range(4):
                    dch = db * 4 + j
                    for n0 in range(0, ntok, 512):
                        ps, pb = bank()
                        for fc in range(16):
                            P.mm(ps[:, :], ww[:, fc, j * 128:(j + 1) * 128], gat[:, fc, n0:n0 + 512], fc == 0, fc == 15,
                                 [wwb, gb[fc]], [pb])
                        hh = (T0 + n0) // 512
                        dst = x1T[:, dch, T0 + n0:T0 + n0 + 512]
                        if first:
                            P.ts("dve", dst, ps[:, :], dd[:, 32 + dch:33 + dch], None, ALU.mult, None, [pb, derb],
                                 [x1b[dch][hh]])
                        else:
                            P.stt(dst, ps[:, :], dd[:, 32 + dch:33 + dch], dst, ALU.mult, ALU.add, [pb, derb, x1b[dch][hh]],
                                  [x1b[dch][hh]])

        def emit_final(half):
            T0 = half * 512
            tmpA = G1_t[:, 0:4096]
            rs = G2_t[:, 8192:8704]
            rsb = Buf("rs_f")
            sq_ring = Ring([tmpA[:, i * 256:(i + 1) * 256].bitcast(BF16) for i in range(3)], "sqf")
            f_ring = Ring([tmpA[:, 768 + i * 512:768 + (i + 1) * 512] for i in range(4)], "ff")
            stg = [G2_t[:, 0:2048], G2_t[:, 2048:4096], G2_t[:, 4096:6144], G2_t[:, 6144:8192]]
            stgb = [Buf(f"stg{i}") for i in range(4)]
            emit_rstd_bc(lambda dc: x1T[:, dc, T0:T0 + 512], lambda dc: [x1b[dc][half]], 512, sq_ring, rs, rsb, 1.0 / D)
            xn = [None] * 16
            for d4 in range(4):
                tl = []
                for j in range(4):
                    dc = d4 * 4 + j
                    t, tb = f_ring.next()
                    P.stt(t, x1T[:, dc, T0:T0 + 512], vec[:, V_FNW + dc:V_FNW + dc + 1], rs, ALU.mult, ALU.mult,
                          [x1b[dc][half], vecb, rsb], [tb])
                    tl.append((t, tb))
                for t4 in range(4):
                    ps, pb = bank()
                    for j in range(4):
                        P.tr(ps[:, j * 128:(j + 1) * 128], tl[j][0][:, t4 * 128:(t4 + 1) * 128], ident_f,
                             [tl[j][1], cstb], [pb])
                    eng = "act" if (t4 % 2 == 0) else "dve"
                    P.copy(eng, stg[t4][:, d4 * 512:(d4 + 1) * 512], ps[:, :], [pb], [stgb[t4]])
            for t4 in range(4):
                r = T0 + t4 * 128
                P.dma("sp", out_d[r:r + 128, :], stg[t4], [stgb[t4]], [], f"out{t4}")


        def emit_L0(with_mod1=False):
            P.nw = 2
            dd = emit_mod(0)
            P.barrier()
            NTA = NCTX + NT
            hT = G2_t[:].bitcast(BF16).rearrange("p (c t) -> p c t", c=16)
            hTb = Buf("hT")
            gat = G1_t[:].bitcast(BF16).rearrange("p (c t) -> p c t", c=16)
            gb = [Buf(f"g0_{c}") for c in range(16)]
            XR = x1T_t
            XB = W_t[2][:].bitcast(F32)
            stg_ring = Ring([G1_t[:, i * 2048:(i + 1) * 2048] for i in range(4)], "xstg")
            ssr = small[:, 0:32]
            ssb = [Buf(f"ss{i}") for i in range(10)]
            for t in range(10):
                stg, stgb = stg_ring.next()
                src = ctx_d[t * 128:(t + 1) * 128, :] if t < 2 else xs_d[(t - 2) * 128:(t - 1) * 128, :]
                P.dma("sp", stg, src, (), [stgb], f"xl{t % 4}")
                junk = XR[:, 0:2048]
                junkb = Buf("junk")
                P.act(junk, stg, AF.Square, [stgb], [junkb, ssb[t]], accum_out=ssr[:, t:t + 1])
                P.ts("dve", ssr[:, t:t + 1], ssr[:, t:t + 1], 1.0 / D, EPS, ALU.mult, ALU.add, [ssb[t]], [ssb[t]])
                P.act(ssr[:, t:t + 1], ssr[:, t:t + 1], AF.Sqrt, [ssb[t]], [ssb[t]])
                P.recip(ssr[:, t:t + 1], ssr[:, t:t + 1], [ssb[t]], [ssb[t]])
                P.ts("dve", stg, stg, ssr[:, t:t + 1], None, ALU.mult, None, [stgb, ssb[t]], [stgb])
                so, bo = (48, 64) if t < 2 else (0, 16)
                for d4 in range(4):
                    ps, pb = bank()
                    for j in range(4):
                        dc = d4 * 4 + j
                        P.tr(ps[:, j * 128:(j + 1) * 128], stg[:, dc * 128:(dc + 1) * 128], ident_f, [stgb, cstb], [pb])
                    for j in range(4):
                        dc = d4 * 4 + j
                        P.act(hT[:, dc, t * 128:(t + 1) * 128], ps[:, j * 128:(j + 1) * 128], AF.Identity, [pb, derb], [hTb],
                              bias=dd[:, bo + dc:bo + dc + 1], scale=dd[:, so + dc:so + dc + 1])
            P.barrier()
            o = 0
            ob = 0
            def carve(n):
                nonlocal o
                a = XR[:, o:o + n]
                o += n
                assert o <= 16384, o
                return a
            def carveB(n):
                nonlocal ob
                a = XB[:, ob:ob + n]
                ob += n
                assert ob <= 4096, ob
                return a
            gc3 = carve(320).rearrange("p (c k) -> p c k", c=10)
            glb3 = carve(320).rearrange("p (c k) -> p c k", c=10)
            ngc3 = carve(320).rearrange("p (c k) -> p c k", c=10)
            bexp3 = carve(320).rearrange("p (c k) -> p c k", c=10)
            beta3 = carve(320).rearrange("p (c k) -> p c k", c=10)
            egl3 = carve(320).rearrange("p (c k) -> p c k", c=10)
            gl3 = carve(320).rearrange("p (c k) -> p c k", c=10)
            o_save = o
            o = 2240 + 7552
            Graw = carve(640).rearrange("p (c k) -> p c k", c=10)
            ones_f = carve(128)
            gtmp = carve(320).rearrange("p (c k) -> p c k", c=10)
            o = o_save
            gatesb = Buf("gates")
            P.memset("dve", ones_f, 1.0, [gatesb])
            P.memset("dve", small[:, 64:65], EPS, [smallb])
            wap, wb, key = wslot()
            P.dma("pool", wap[:, :, 0:64], wab_d[:, :].rearrange("(c p) n -> p c n", p=128), (), [wb], key)
            pg2, pg2b = banks(2)
            for t in range(10):
                for dc in range(16):
                    P.mm(pg2[:, t * 64:(t + 1) * 64], hT[:, dc, t * 128:(t + 1) * 128], wap[:, dc, 0:64], dc == 0, dc == 15,
                         [hTb, wb], [pg2b[(t * 64) // 512]])
            pg3 = pg2[:, 0:640].rearrange("p (c k) -> p c k", c=10)
            dtb_bc = vec[:, V_DTB:V_DTB + 32]
            nA = small[:, 32:64]
            P.act(nA, vec[:, V_ALOG:V_ALOG + 32], AF.Exp, [vecb], [smallb])
            P.ts("dve", nA, nA, -1.0, None, ALU.mult, None, [smallb], [smallb])
            for t in range(10):
                P.tt("dve", Graw[:, t, 0:32], pg3[:, t, 0:32], dtb_bc, ALU.add, pg2b + [vecb], [gatesb])
            P.act(Graw[:, :, 0:32], Graw[:, :, 0:32], AF.Exp, [gatesb], [gatesb])
            P.act(Graw[:, :, 0:32], Graw[:, :, 0:32], AF.Ln, [gatesb], [gatesb], bias=1.0)
            for t in range(10):
                P.tt("dve", Graw[:, t, 0:32], Graw[:, t, 0:32], nA, ALU.mult, [gatesb, smallb], [gatesb])
            P.act(Graw[:, :, 32:64], pg3[:, :, 32:64], AF.Exp, pg2b, [gatesb], scale=-1.0)
            P.act(Graw[:, :, 32:64], Graw[:, :, 32:64], AF.Ln, [gatesb], [gatesb], bias=1.0)
            P.ts("dve", Graw[:, :, 32:64], Graw[:, :, 32:64], -1.0, None, ALU.mult, None, [gatesb], [gatesb])
            pcs_, pcsb = bank()
            ptt_, pttb = bank()
            pc3 = pcs_[:, 0:320].rearrange("p (c k) -> p c k", c=10)
            pt3 = ptt_[:, 0:320].rearrange("p (c k) -> p c k", c=10)
            tri_a = cst[:, C_TA:C_TA + 128]
            tri_d = cst[:, C_TD:C_TD + 128]
            for t in range(10):
                P.mm(pc3[:, t, 0:16], tri_a, Graw[:, t, 0:16], True, True, [cstb, gatesb], [pcsb])
                P.mm(pc3[:, t, 16:32], tri_d, Graw[:, t, 16:32], True, True, [cstb, gatesb], [pcsb])
                P.mm(pt3[:, t, :], ones_f, Graw[:, t, 0:32], True, True, [gatesb], [pttb])
            P.copy("act", gc3, pc3, [pcsb], [gatesb])
            P.ts("dve", ngc3, pc3, -1.0, None, ALU.mult, None, [pcsb], [gatesb])
            P.tt("dve", glb3, pc3, Graw[:, :, 32:64], ALU.add, [pcsb, gatesb], [gatesb])
            P.act(bexp3, glb3, AF.Exp, [gatesb], [gatesb])
            P.act(beta3, Graw[:, :, 32:64], AF.Exp, [gatesb], [gatesb])
            P.tt("dve", gtmp, pt3, gc3, ALU.subtract, [pttb, gatesb], [gatesb])
            P.act(egl3, gtmp, AF.Exp, [gatesb], [gatesb])
            P.act(gl3, pt3, AF.Exp, [pttb], [gatesb])
            P.barrier()
            slots = []
            for i in range(2):
                sl = {}
                sl["kqT"] = carve(1280).bitcast(BF16).rearrange("p (c w t) -> p c w t", c=10, w=2)
                sl["ktok"] = carve(640).bitcast(BF16).rearrange("p (c d) -> p c d", c=10)
                sl["vtok"] = carve(640).bitcast(BF16).rearrange("p (c d) -> p c d", c=10)
                sl["zAs"] = carve(512).bitcast(BF16)
                sl["o1"] = carve(512).bitcast(BF16)
                sl["S"] = carve(128)
                sl["Sb"] = carve(64).bitcast(BF16)
                for nm in ("kqT", "ktok", "vtok", "zAs", "o1", "S", "Sb"):
                    sl[nm + "_b"] = Buf(f"{nm}{i}")
                slots.append(sl)
            oc = 0
            def carveC(n):
                nonlocal oc
                a = aux[:, oc:oc + n]
                oc += n
                assert oc <= 2048, oc
                return a
            CH = 1664
            chain_bases = [carve(CH), carve(CH), carve(CH), carveB(CH), carveC(CH)]
            vn_ring = [Ring([carve(64).bitcast(BF16) for _ in range(2)], f"vn{p}") for p in range(2)]
            fsq = carve(512).bitcast(BF16)
            fsqb = Buf("fsq")
            oacc = carveB(1024)
            oaccb = Buf("oacc")
            Gt = carveB(256)
            Gtb = Buf("Gt")

            def mk_chain_slot(base, nm):
                bfv = lambda a: a.bitcast(BF16)
                cs = dict(
                    em=base[:, 0:384], xyrtA=bfv(base[:, 0:256]), ab=bfv(base[:, 256:384]),
                    F=bfv(base[:, 384:576]), r1t1=bfv(base[:, 384:512]), a2=bfv(base[:, 512:576]),
                    egc=bfv(base[:, 576:640]), kbg=bfv(base[:, 576:640]),
                    y0=bfv(base[:, 640:704]), nw=bfv(base[:, 640:704]),
                    xa=bfv(base[:, 704:832]), msk=bfv(base[:, 832:1152]), xyrtB=bfv(base[:, 1152:1408]),
                    rf=bfv(base[:, 1408:1472]), vb=bfv(base[:, 1472:1536]), kg=bfv(base[:, 1536:1600]),
                    qg=bfv(base[:, 1600:1664]), buf=Buf("chain_" + nm))
                return cs
            chain_slots = [[mk_chain_slot(chain_bases[i], f"p0_{i}") for i in range(3)],
                           [mk_chain_slot(chain_bases[3], "p1_0"), mk_chain_slot(chain_bases[4], "p1_1")]]
            xA = carve(832)
            xBt = carveB(832)
            bfv_ = lambda a: a.bitcast(BF16)
            cs6 = dict(
                em=xA[:, 0:384], xyrtA=bfv_(xA[:, 0:256]), ab=bfv_(xA[:, 256:384]),
                F=bfv_(xA[:, 384:576]), r1t1=bfv_(xA[:, 384:512]), a2=bfv_(xA[:, 512:576]),
                egc=bfv_(xA[:, 576:640]), kbg=bfv_(xA[:, 576:640]),
                y0=bfv_(xA[:, 640:704]), nw=bfv_(xA[:, 640:704]),
                xa=bfv_(xA[:, 704:832]), msk=bfv_(xBt[:, 0:320]), xyrtB=bfv_(xBt[:, 320:576]),
                rf=bfv_(xBt[:, 576:640]), vb=bfv_(xBt[:, 640:704]), kg=bfv_(xBt[:, 704:768]),
                qg=bfv_(xBt[:, 768:832]), buf=Buf("chain_p1_2"))
            chain_slots[1].append(cs6)
            for j, cs_ in enumerate(chain_slots[0] + chain_slots[1]):
                cs_["pcs"] = psum[j // 2][:, (j % 2) * 256:(j % 2) * 256 + 256]
                cs_["pcb"] = pbuf[j // 2]
            cin_d = [nc.dram_tensor(f"cin{h}", [128, 128], F32) for h in range(16)]
            cout_d = [nc.dram_tensor(f"cout{h}", [256, 128], F32) for h in range(16)]
            cinb = [Buf(f"cin{h}") for h in range(16)]
            coutb = [Buf(f"cout{h}") for h in range(16)]
            BLK = ((0, 512), (512, 1024), (1024, 1280))

            cvq, cvk, cvv = chain_bases[0][:, 0:1280], chain_bases[1][:, 0:1280], chain_bases[2][:, 0:1280]
            sqq = chain_bases[3][:, 0:640].bitcast(BF16)
            sqk = chain_bases[3][:, 640:1280].bitcast(BF16)
            vTt = chain_bases[4][:, 0:640].bitcast(BF16)

            head_w = {}

            def load_head_w(h):
                head_w[h] = load_w(ewhd_d, h * D, 16, 0, 512)

            def prep_head(h):
                sl = slots[h % 2]
                wap, wb = head_w.pop(h)

                def proj_conv(which, cv, cvb):
                    ps3, pb3 = banks(3)
                    for bi, (n0, n1) in enumerate(BLK):
                        for dc in range(16):
                            P.mm(ps3[:, n0:n1], wap[:, dc, which * 128:(which + 1) * 128], hT[:, dc, n0:n1], dc == 0, dc == 15,
                                 [wb, hTb], [pb3[bi]])
                    grp = which * 16 + h
                    w0 = vec[:, V_CQ + grp * 3:V_CQ + grp * 3 + 1]
                    w1 = vec[:, V_CQ + grp * 3 + 1:V_CQ + grp * 3 + 2]
                    w2 = vec[:, V_CQ + grp * 3 + 2:V_CQ + grp * 3 + 3]
                    P.act(cv, ps3[:, 0:NTA], AF.Identity, pb3 + [vecb], [cvb], scale=w1)
                    P.stt(cv[:, 1:256], ps3[:, 0:255], w0, cv[:, 1:256], ALU.mult, ALU.add, pb3 + [vecb, cvb], [cvb])
                    P.stt(cv[:, 0:255], ps3[:, 1:256], w2, cv[:, 0:255], ALU.mult, ALU.add, pb3 + [vecb, cvb], [cvb])
                    cvx = cv[:, 256:NTA].rearrange("p (r t) -> p r t", t=64)
                    psx = ps3[:, 256:NTA].rearrange("p (r t) -> p r t", t=64)
                    P.stt(cvx[:, :, 1:64], psx[:, :, 0:63], w0, cvx[:, :, 1:64], ALU.mult, ALU.add, pb3 + [vecb, cvb], [cvb])
                    P.stt(cvx[:, :, 0:63], psx[:, :, 1:64], w2, cvx[:, :, 0:63], ALU.mult, ALU.add, pb3 + [vecb, cvb], [cvb])

                def to_tokmajor(src_T, srcb, dst, dstb):
                    pa, pab = bank()
                    pa_b = pa[:].bitcast(BF16)
                    for c in range(8):
                        P.tr(pa_b[:, c * 128:(c + 1) * 128], src_T(c), ident_b, [srcb, cbfb], [pab])
                    P.copy("act", dst[:, 0:8, :], pa_b[:, 0:1024].rearrange("p (c d) -> p c d", c=8), [pab], [dstb])
                    pa2, pa2b = bank()
                    pa2_b = pa2[:].bitcast(BF16)
                    for c in range(8, 10):
                        P.tr(pa2_b[:, (c - 8) * 128:(c - 7) * 128], src_T(c), ident_b, [srcb, cbfb], [pa2b])
                    P.copy("dve", dst[:, 8:10, :], pa2_b[:, 0:256].rearrange("p (c d) -> p c d", c=2), [pa2b], [dstb])

                def chain_qk(which, cv, sq):
                    cvb, sqb = Buf("cv"), Buf("sq")
                    proj_conv(which, cv, cvb)
                    yield
                    P.act(cv, cv, AF.Silu, [cvb], [cvb])
                    P.tt("pool", sq, cv, cv, ALU.mult, [cvb], [sqb])
                    yield
                    ss3, ssb3 = banks(3)
                    for bi, (n0, n1) in enumerate(BLK):
                        P.mm(ss3[:, n0:n1], ones_b, sq[:, n0:n1], True, True, [cbfb, sqb], [ssb3[bi]])
                    P.act(ss3[:, 0:NTA], ss3[:, 0:NTA], AF.Ln, ssb3 + [smallb], ssb3, bias=small[:, 64:65])
                    P.act(ss3[:, 0:NTA], ss3[:, 0:NTA], AF.Exp, ssb3, ssb3, scale=-0.5)
                    cv3 = cv.rearrange("p (c t) -> p c t", c=10)
                    ri3 = ss3[:, 0:NTA].rearrange("p (c t) -> p c t", c=10)
                    if which == 0:
                        P.stt(sl["kqT"][:, :, 1, :], cv3, 128.0 ** -0.5, ri3, ALU.mult, ALU.mult, [cvb] + ssb3, [sl["kqT_b"]])
                    else:
                        P.tt("dve", sl["kqT"][:, :, 0, :], cv3, ri3, ALU.mult, [cvb] + ssb3, [sl["kqT_b"]])
                        yield
                        to_tokmajor(lambda c: sl["kqT"][:, c, 0, :], sl["kqT_b"], sl["ktok"], sl["ktok_b"])

                def chain_v():
                    cvb, vTb = Buf("cvv"), Buf("vT")
                    proj_conv(2, cvv, cvb)
                    yield
                    P.act(vTt, cvv, AF.Silu, [cvb], [vTb])
                    yield
                    to_tokmajor(lambda c: vTt[:, c * 128:(c + 1) * 128], vTb, sl["vtok"], sl["vtok_b"])

                def chain_z():
                    pz2, pz2b = banks(2)
                    for bi in range(2):
                        n0 = 256 + bi * 512
                        for dc in range(16):
                            P.mm(pz2[:, bi * 512:(bi + 1) * 512], wap[:, dc, 384:512], hT[:, dc, n0:n0 + 512], dc == 0, dc == 15,
                                 [wb, hTb], [pz2b[bi]])
                    P.act(sl["zAs"], pz2[:, 0:1024], AF.Silu, pz2b, [sl["zAs_b"]])
                    P.memset("pool", sl["S"], 0.0, [sl["S_b"]])
                    P.memset("pool", sl["Sb"], 0.0, [sl["Sb_b"]])
                    return
                    yield

                gens = [chain_qk(0, cvq, sqq), chain_qk(1, cvk, sqk), chain_v(), chain_z()]
                while gens:
                    keep = []
                    for g in gens:
                        try:
                            next(g)
                            keep.append(g)
                        except StopIteration:
                            pass
                    gens = keep

            def intra_gen(h, ph, c, cs):
                sl = slots[h % 2]
                gcol = ph * 16 + h
                kq = sl["kqT"]
                kqb = sl["kqT_b"]
                cb = cs["buf"]
                pcs, pcb = cs["pcs"], cs["pcb"]
                P.mm(pcs[:, 0:256], kq[:, c, 0, :], kq[:, c, :, :].rearrange("p w t -> p (w t)"), True, True, [kqb], [pcb])
                pes, peb = bank()
                P.tr(pes[:, 0:128], glb3[:, c, gcol:gcol + 1].to_broadcast([128, 128]), ident_f, [gatesb, cstb], [peb])
                P.tr(pes[:, 128:256], gc3[:, c, gcol:gcol + 1].to_broadcast([128, 128]), ident_f, [gatesb, cstb], [peb])
                P.tr(pes[:, 256:384], gc3[:, c, gcol:gcol + 1].to_broadcast([128, 128]), ident_f, [gatesb, cstb], [peb])
                mbase = C_MA if ph == 0 else C_MD
                P.act(cs["egc"], pes[:, 128:256], AF.Exp, [peb], [cb])
                P.tt("dve", cs["em"], pes[:, 0:384], cst[:, mbase:mbase + 384], ALU.add, [peb, cstb], [cb])
                P.tt("pool", cs["qg"], kq[:, c, 1, :], cs["egc"], ALU.mult, [kqb], [cb])
                yield
                P.act(cs["F"][:, 0:256], cs["em"][:, 0:256], AF.Exp, [gatesb], [cb], bias=ngc3[:, c, gcol:gcol + 1])
                P.act(cs["F"][:, 256:384], cs["em"][:, 256:384], AF.Exp, [gatesb], [cb], bias=glb3[:, c, gcol:gcol + 1],
                      scale=-1.0)
                yield
                P.tt("dve", cs["xa"], pcs[:, 0:256], cs["F"][:, 0:256], ALU.mult, [pcb], [cb])
                P.tt("dve", cs["y0"], pcs[:, 0:128], cs["F"][:, 256:384], ALU.mult, [pcb], [cb])
                P.tt("pool", cs["vb"], sl["vtok"][:, c, :], beta3[:, c, gcol:gcol + 1].to_broadcast([128, 128]), ALU.mult,
                     [sl["vtok_b"], gatesb], [cb])
                P.tt("pool", cs["kg"], sl["ktok"][:, c, :], egl3[:, c, gcol:gcol + 1].to_broadcast([128, 128]), ALU.mult,
                     [sl["ktok_b"], gatesb], [cb])
                yield
                msk = cs["msk"]
                m1x, m1y, m2y = (bm_b[2], bm_b[1], bm_b[3]) if ph == 0 else (bm_b[1], bm_b[2], bm_b[4])
                P.tt("pool", msk[:, 0:128], cs["xa"][:, 0:128], bm_b[0], ALU.mult, [cbfb], [cb])
                P.tt("pool", msk[:, 128:256], cs["y0"], bm_b[0], ALU.mult, [cbfb], [cb])
                yield
                P.tt("pool", msk[:, 256:384], cs["xa"][:, 0:128], m1x, ALU.mult, [cbfb], [cb])
                P.tt("pool", msk[:, 384:512], cs["y0"], m1y, ALU.mult, [cbfb], [cb])
                P.tt("pool", msk[:, 512:640], cs["y0"], m2y, ALU.mult, [cbfb], [cb])
                Xk, Yk = msk[:, 0:128], msk[:, 128:256]
                Rk = ident_b
                for k in range(5):
                    prs, prb = bank()
                    if k <= 3:
                        P.mm(prs[:, 0:128], Yk, Xk, True, True, [cb], [prb])
                        P.mm(prs[:, 128:256], Xk, Yk, True, True, [cb], [prb])
                    P.mm(prs[:, 256:384], ident_b, Rk, True, False, [cbfb, cb], [prb])
                    P.mm(prs[:, 256:384], Yk, nident_b if k == 0 else Rk, False, True, [cb, cbfb], [prb])
                    nx = cs["xyrtA"] if k % 2 == 0 else cs["xyrtB"]
                    lo = 0 if k <= 3 else 256
                    P.copy("act" if k % 2 == 0 else "dve", nx[:, lo:384], prs[:, lo:384], [prb], [cb])
                    Xk, Yk, Rk = nx[:, 0:128], nx[:, 128:256], nx[:, 256:384]
                    yield
                ptr_, ptrb = bank()
                ptr_b = ptr_[:].bitcast(BF16)
                P.tr(ptr_b[:, 0:128], Rk, ident_b, [cb, cbfb], [ptrb])
                P.copy("dve", cs["xyrtA"][:, 384:512], ptr_b[:, 0:128], [ptrb], [cb])
                Tk = cs["xyrtA"][:, 384:512]
                yield
                pl, plb = bank()
                P.mm(pl[:, 0:128], msk[:, 384:512], Rk, True, True, [cb], [plb])
                P.mm(pl[:, 128:256], msk[:, 256:384], Tk, True, True, [cb], [plb])
                P.copy("act", cs["ab"], pl[:, 0:256], [plb], [cb])
                yield
                pl, plb = bank()
                P.mm(pl[:, 0:128], Tk, cs["ab"][:, 0:128], True, True, [cb], [plb])
                P.mm(pl[:, 128:256], Rk, cs["ab"][:, 128:256], True, True, [cb], [plb])
                P.tt("dve", cs["r1t1"], cs["xyrtA"][:, 256:512], pl[:, 0:256], ALU.subtract, [plb], [cb])
                yield
                pl, plb = bank()
                P.mm(pl[:, 0:128], msk[:, 512:640], cs["r1t1"][:, 0:128], True, True, [cb], [plb])
                P.copy("act", cs["a2"], pl[:, 0:128], [plb], [cb])
                yield
                pl, plb = bank()
                P.mm(pl[:, 0:128], cs["r1t1"][:, 128:256], cs["a2"], True, True, [cb], [plb])
                P.tt("dve", cs["rf"], cs["r1t1"][:, 0:128], pl[:, 0:128], ALU.subtract, [plb], [cb])
                P.tt("pool", cs["kbg"], sl["ktok"][:, c, :], bexp3[:, c, gcol:gcol + 1].to_broadcast([128, 128]), ALU.mult,
                     [sl["ktok_b"], gatesb], [cb])
                yield
                pw, pwb = bank()
                P.mm(pw[:, 0:128], cs["kbg"], cs["rf"], True, True, [cb], [pwb])
                P.act(cs["nw"], pw[:, 0:128], AF.Identity, [pwb], [cb], scale=-1.0)

            def scan_gen(h, ph, c, cs, last):
                sl = slots[h % 2]
                gcol = ph * 16 + h
                cb = cs["buf"]
                S, Sb_ = sl["S"], sl["S_b"]
                Sb, Sbb = sl["Sb"], sl["Sb_b"]
                pv, pvb = bank()
                P.mm(pv[:, 0:128], cs["rf"], cs["vb"], True, False, [cb], [pvb])
                P.mm(pv[:, 0:128], cs["nw"], Sb, False, True, [cb, Sbb], [pvb])
                vn, vnb = vn_ring[ph].next()
                P.copy("act", vn, pv[:, 0:128], [pvb], [vnb])
                yield
                po, pob = bank()
                if c >= 2:
                    P.mm(po[:, 0:128], Sb, cs["qg"], True, False, [Sbb, cb], [pob])
                    P.mm(po[:, 0:128], vn, cs["xa"][:, 128:256], False, True, [vnb, cb], [pob])
                P.mm(po[:, 128:256], cs["kg"], vn, True, True, [cb, vnb], [pob])
                P.stt(S, S, gl3[:, c, gcol:gcol + 1], po[:, 128:256], ALU.mult, ALU.add, [Sb_, gatesb, pob], [Sb_])
                if not last:
                    P.copy("act", Sb, S, [Sb_], [Sbb])
                if c >= 2:
                    tk = (c - 2) * 128
                    if ph == 0:
                        P.copy("dve", sl["o1"][:, tk:tk + 128], po[:, 0:128], [pob], [sl["o1_b"]])
                    else:
                        P.tt("dve", oacc[:, tk:tk + 128], po[:, 0:128], sl["o1"][:, tk:tk + 128], ALU.add,
                             [pob, sl["o1_b"]], [oaccb])

            def run_block(phases):
                st = []
                for (h, ph) in phases:
                    order = list(range(10)) if ph == 0 else list(range(9, 1, -1))
                    st.append(dict(h=h, ph=ph, order=order, istart=0, idone=set(), sdone=0, scan=None, intras=[]))
                while True:
                    progressed = False
                    for p in st:
                        n = len(p["order"])
                        ns = len(chain_slots[p["ph"]])
                        while p["istart"] < n and p["istart"] - p["sdone"] < ns:
                            i = p["istart"]
                            g = intra_gen(p["h"], p["ph"], p["order"][i], chain_slots[p["ph"]][i % ns])
                            p["intras"].append((i, g))
                            p["istart"] += 1
                        if p["scan"] is None and p["sdone"] < n and p["sdone"] in p["idone"]:
                            i = p["sdone"]
                            if i == 0 and p["ph"] == 1:
                                exchange_finish(p["h"])
                            p["scan"] = scan_gen(p["h"], p["ph"], p["order"][i], chain_slots[p["ph"]][i % ns], i == n - 1)
                    for p in st:
                        keep = []
                        for (i, g) in p["intras"]:
                            try:
                                next(g)
                                keep.append((i, g))
                            except StopIteration:
                                p["idone"].add(i)
                            progressed = True
                        p["intras"] = keep
                        if p["scan"] is not None:
                            try:
                                next(p["scan"])
                            except StopIteration:
                                p["scan"] = None
                                p["sdone"] += 1
                            progressed = True
                    if not progressed:
                        break

            def exchange(h):
                sl = slots[h % 2]
                P.dma("sp", cin_d[h][:, :], sl["S"], [sl["S_b"]], [cinb[h]], f"xi{h}")
                P.op("pool", lambda e, h=h: e.collective_compute(
                    "AllGather", ALU.bypass, replica_groups=groups or [[0, 1], [2, 3], [4, 5], [6, 7]],
                    ins=[cin_d[h].ap().opt()], outs=[cout_d[h].ap().opt()]), [cinb[h]], [coutb[h]], key="cc", inc=1)
                P.dma("sp", Gt.rearrange("p (r n) -> p r n", r=2), cout_d[h][:, :].rearrange("(r p) n -> p r n", p=128),
                      [coutb[h]], [Gtb], f"xo{h}")

            def exchange_finish(h):
                sl = slots[h % 2]
                P.ts("dve", sl["S"], Gt[:, 0:128], vec[:, V_SEL:V_SEL + 1], None, ALU.mult, None, [Gtb, vecb], [sl["S_b"]])
                P.stt(sl["S"], Gt[:, 128:256], vec[:, V_SEL + 1:V_SEL + 2], sl["S"], ALU.mult, ALU.add,
                      [Gtb, vecb, sl["S_b"]], [sl["S_b"]])
                P.copy("act", sl["Sb"], sl["S"], [sl["S_b"]], [sl["Sb_b"]])

            def finish_head(h):
                sl = slots[h % 2]
                sq, sqb = fsq, fsqb
                P.tt("pool", sq[:, 0:1024], oacc, oacc, ALU.mult, [oaccb], [sqb])
                ss2, ssb2 = banks(2)
                for bi in range(2):
                    P.mm(ss2[:, bi * 512:(bi + 1) * 512], ones_b, sq[:, bi * 512:(bi + 1) * 512], True, True, [cbfb, sqb],
                         [ssb2[bi]])
                P.act(ss2[:, 0:1024], ss2[:, 0:1024], AF.Ln, ssb2 + [smallb], ssb2, bias=small[:, 64:65], scale=1.0 / 128)
                P.act(ss2[:, 0:1024], ss2[:, 0:1024], AF.Exp, ssb2, ssb2, scale=-0.5)
                P.stt(oacc, oacc, vec[:, V_HN:V_HN + 1], ss2[:, 0:1024], ALU.mult, ALU.mult, [oaccb, vecb] + ssb2, [oaccb])
                P.tt("pool", gat[:, h, :], oacc, sl["zAs"], ALU.mult, [oaccb, sl["zAs_b"]], [gb[h]])

            mod1_gen = emit_mod_gen(1) if with_mod1 else iter(())
            P.nw = 3 if False else 2
            load_head_w(0)
            for h in range(nheads + 1):
                if h < nheads:
                    prep_head(h)
                    P.barrier()
                if h >= 1:
                    exchange(h - 1)
                next(mod1_gen, None)
                if h + 1 < nheads:
                    load_head_w(h + 1)
                ph_list = ([(h, 0)] if h < nheads else []) + ([(h - 1, 1)] if h >= 1 else [])
                P.bank_lo = 3
                run_block(ph_list)
                P.bank_lo = 0
                if h >= 1:
                    finish_head(h - 1)
                next(mod1_gen, None)
                P.barrier()
            for _ in mod1_gen:
                pass
            P.barrier()
            if stop_after == "gdn":
                return
            emit_wout_pass(ewout_d, 0, gat, gb, 0, dd, first=True, ntok=1024, tok0=0)
            P.barrier()
            emit_mixer_B(dd, hT, hTb, gat, gb)
            emit_wout_pass(ewout_d, 2048, gat, gb, 0, dd, first=False, ntok=1024, tok0=0)
            P.barrier()
            stg_ring2 = Ring([G2_t[:, i * 2048:(i + 1) * 2048] for i in range(4)], "xstg2")
            for t in range(8):
                stg, stgb = stg_ring2.next()
                P.dma("sp", stg, xs_d[t * 128:(t + 1) * 128, :], (), [stgb], f"xl{t % 4}")
                for d4 in range(4):
                    ps, pb = bank()
                    for j in range(4):
                        dc = d4 * 4 + j
                        P.tr(ps[:, j * 128:(j + 1) * 128], stg[:, dc * 128:(dc + 1) * 128], ident_f, [stgb, cstb], [pb])
                    for j in range(4):
                        dc = d4 * 4 + j
                        dst = x1T[:, dc, t * 128:(t + 1) * 128]
                        P.tt("dve", dst, ps[:, j * 128:(j + 1) * 128], dst, ALU.add, [pb, x1b[dc][t // 4]], [x1b[dc][t // 4]])
            P.barrier()
            P.nw = 3
            P.nslot = 0

        def emit_mixer_B(dd, hT, hTb, gat, gb):
            BASE = 6144 + 2048 + 64
            for cg in range(16):
                wap, wb = load_w(ewmb_d, cg * D, 16, 0, 512)
                w0 = vec[:, V_CB + cg * 3:V_CB + cg * 3 + 1]
                w1 = vec[:, V_CB + cg * 3 + 1:V_CB + cg * 3 + 2]
                w2 = vec[:, V_CB + cg * 3 + 2:V_CB + cg * 3 + 3]
                for half in range(2):
                    n0 = 256 + half * 512
                    pp = []
                    for j in range(4):
                        ps, pb = bank()
                        for dc in range(16):
                            P.mm(ps[:, :], wap[:, dc, j * 128:(j + 1) * 128], hT[:, dc, n0:n0 + 512], dc == 0, dc == 15,
                                 [wb, hTb], [pb])
                        pp.append((ps, pb))
                    (pbg, pbgb), (pcg, pcgb), (phb, phbb), (pzb, pzbb) = pp
                    cgs, cgsb = mb_ring.next()
                    P.copy("act", cgs, pcg[:, :], [pcgb], [cgsb])
                    t1, t1b = mb_ring.next()
                    P.tt("dve", t1, phb[:, :], cgs, ALU.mult, [phbb, cgsb], [t1b])
                    cv, cvb = mb_ring.next()
                    P.act(cv, t1, AF.Identity, [t1b, vecb], [cvb], scale=w1)
                    cv3 = cv.rearrange("p (r t) -> p r t", t=64)
                    t13 = t1.rearrange("p (r t) -> p r t", t=64)
                    P.stt(cv3[:, :, 1:64], t13[:, :, 0:63], w0, cv3[:, :, 1:64], ALU.mult, ALU.add, [t1b, vecb, cvb], [cvb])
                    P.stt(cv3[:, :, 0:63], t13[:, :, 1:64], w2, cv3[:, :, 0:63], ALU.mult, ALU.add, [t1b, vecb, cvb], [cvb])
                    sz, szb = mb_ring.next()
                    P.act(sz, pzb[:, :], AF.Silu, [pzbb], [szb])
                    P.tt("dve", cv, pbg[:, :], cv, ALU.mult, [pbgb, cvb], [cvb])
                    P.tt("pool", gat[:, cg, half * 512:(half + 1) * 512], cv, sz, ALU.mult, [cvb, szb], [gb[cg]])

        if mode == "L0":
            emit_L0()
            for dc in range(16):
                P.dma("sp", out_d[:, dc * NT:(dc + 1) * NT], x1T[:, dc, :], [x1b[dc][0], x1b[dc][1]], [], f"out{dc % 4}")
        if mode == "full":
            emit_L0(True)
            pre = emit_L1_prelude(True)
            for half in range(2):
                P.barrier()
                emit_L1_half(half, *pre)
            for half in range(2):
                P.barrier()
                emit_final(half)
        if mode == "L1":
            x1all = Buf("x1all")
            for dc in range(16):
                P.dma("sp", x1T[:, dc, :], x1in_d[:, dc * NT:(dc + 1) * NT], (), [x1b[dc][0], x1b[dc][1]], "cld")
            pre = emit_L1_prelude()
            for half in range(2):
                P.barrier()
                emit_L1_half(half, *pre)
            for half in range(2):
                P.barrier()
                emit_final(half)

        with nc.Block() as block:
            P.flush(block)
    return nc


def make_consts():
    c = np.zeros((128, NCST), np.float32)
    idx = np.arange(128)
    c[:, C_ID:C_ID + 128] = np.eye(128)
    c[:, C_TA:C_TA + 128] = (idx[:, None] <= idx[None, :])
    c[:, C_TD:C_TD + 128] = (idx[:, None] >= idx[None, :])
    P_, F_ = idx[:, None], idx[None, :]
    for base, asc in ((C_MA, True), (C_MD, False)):
        if asc:
            m1 = F_ > P_; m2 = F_ >= P_; m3 = P_ > F_
        else:
            m1 = F_ < P_; m2 = F_ <= P_; m3 = P_ < F_
        c[:, base:base + 128] = np.where(m1, 0.0, -BIG)
        c[:, base + 128:base + 256] = np.where(m2, 0.0, -BIG)
        c[:, base + 256:base + 384] = np.where(m3, 0.0, BIG)
    for k, r in enumerate((1, 2, 4, 8)):
        Dm = np.zeros((128, 128), np.float64)
        for i in range(128):
            row = i // 64
            lo = max(i - r, row * 64)
            hi = min(i + r + 1, row * 64 + 64)
            Dm[i, lo:hi] = 1.0 / (hi - lo)
            Dm[i, i] -= 1.0
        c[:, C_BAND + 128 * k:C_BAND + 128 * (k + 1)] = Dm.T
    c[:, C_ONES:C_ONES + 128] = 1.0
    pb, fb = P_ // 32, F_ // 32
    m1_lo = ((pb == 1) & (fb == 0)) | ((pb == 3) & (fb == 2))
    m2_lo = (P_ >= 64) & (F_ < 64)
    for i, m in enumerate((pb == fb, m1_lo, m1_lo.T, m2_lo, m2_lo.T)):
        c[:, C_BM + 128 * i:C_BM + 128 * (i + 1)] = m
    return c


def fm(v, n):
    return np.ascontiguousarray(np.asarray(v, np.float32).reshape(n, 128).T)


def make_vec(inp, b, s):
    v = np.zeros((128, NV), np.float32)
    v[:, V_C:V_C + 16] = fm(inp["c"][b], 16)
    v[:, V_CC:V_CC + 16] = fm(inp["c_ctx"], 16)
    v[:, V_AB0:V_AB0 + 48] = fm(inp["ada_b"][0], 48)
    v[:, V_AB1:V_AB1 + 48] = fm(inp["ada_b"][1], 48)
    v[:, V_NW0:V_NW0 + 16] = fm(inp["norm_w"][0], 16)
    v[:, V_NW1:V_NW1 + 16] = fm(inp["norm_w"][1], 16)
    v[:, V_LNW:V_LNW + 16] = fm(inp["o_ln_w"][0], 16)
    v[:, V_LNB:V_LNB + 16] = fm(inp["o_ln_b"][0], 16)
    v[:, V_PS:V_PS + 16] = fm(inp["o_pool_scale"][0], 16)
    v[:, V_FNW:V_FNW + 16] = fm(inp["final_norm_w"], 16)
    cq = np.asarray(inp["e_conv_qkv"][0], np.float32)
    cb = np.asarray(inp["e_conv_b"][0], np.float32)
    if s == 1:
        cq = cq[::-1]
        cb = cb[::-1]
    v[:, V_CQ:V_CQ + 144] = np.stack([fm(cq[t], 48) for t in range(3)], axis=2).reshape(128, 144)
    v[:, V_CB:V_CB + 48] = np.stack([fm(cb[t], 16) for t in range(3)], axis=2).reshape(128, 48)
    v[:, V_HN] = np.asarray(inp["e_head_norm"][0], np.float32)
    v[:, V_SEL] = 1.0 if s == 1 else 0.0
    v[:, V_SEL + 1] = 1.0 if s == 0 else 0.0
    dirs = (0, 1) if s == 0 else (1, 0)
    dtb = np.asarray(inp["e_dt_bias"][0], np.float32)
    alog = np.asarray(inp["e_a_log"][0], np.float32)
    v[:, V_DTB:V_DTB + 32] = np.concatenate([dtb[dirs[0]], dtb[dirs[1]]])[None, :]
    v[:, V_ALOG:V_ALOG + 32] = np.concatenate([alog[dirs[0]], alog[dirs[1]]])[None, :]
    return v


def common_maps(inp):
    f = lambda a: np.ascontiguousarray(np.asarray(a, np.float32))
    return {
        "cst": make_consts(),
        "ada_w0": f(inp["ada_w"][0]), "ada_w1": f(inp["ada_w"][1]),
        "o_w_in": f(inp["o_w_in"][0]),
        "o_pool_w": f(np.asarray(inp["o_pool_w"][0]).reshape(2048, 512)),
        "o_w_out": f(inp["o_w_out"][0]),
    }


def core_maps_L1(inp, b, s):
    ws = np.asarray(inp["o_w_s"][0], np.float32)
    bs = np.asarray(inp["o_b_s"][0], np.float32)
    if s == 1:
        ws = ws[:, ::-1, ::-1]
        bs = bs[:, ::-1]
    return {
        "vec": make_vec(inp, b, s),
        "ws": np.ascontiguousarray(ws.transpose(1, 0, 2).reshape(128, 2048)),
        "bsb": np.ascontiguousarray(np.broadcast_to(bs.reshape(1, 2048), (128, 2048))),
    }


_NC_CACHE = {}


def kernel(**inputs):
    inp = {k: np.asarray(v) for k, v in inputs.items()}
    if "full" not in _NC_CACHE:
        _NC_CACHE["full"] = build("full")
    nc = _NC_CACHE["full"]
    com = common_maps(inp)
    com.update(common_maps_L0(inp))
    maps = []
    for core in range(8):
        b, s = core // 2, core % 2
        m = dict(com)
        m.update(core_maps_L1(inp, b, s))
        m.update(core_maps_L0(inp, b, s))
        maps.append(m)
    res = run_bass_kernel_spmd(nc, maps, core_ids=list(range(8)))
    out = np.empty((4, 2048, D), np.float32)
    for core in range(8):
        b, s = core // 2, core % 2
        o = np.asarray(res.results[core]["out"], np.float32)
        if s == 1:
            o = o[::-1]
        out[b, s * NT:(s + 1) * NT] = o
    return out


def common_maps_L0(inp):
    f = lambda a: np.ascontiguousarray(np.asarray(a, np.float32))
    w = np.asarray(inp["e_w_in"][0], np.float32)
    hd = np.empty((16, D, 512), np.float32)
    mb = np.empty((16, D, 512), np.float32)
    base = 6144 + 2048 + 64
    for h in range(16):
        for j, c0 in enumerate((h * 128, 2048 + h * 128, 4096 + h * 128, 6144 + h * 128)):
            hd[h, :, j * 128:(j + 1) * 128] = w[:, c0:c0 + 128]
        for j in range(4):
            c0 = base + j * 2048 + h * 128
            mb[h, :, j * 128:(j + 1) * 128] = w[:, c0:c0 + 128]
    return {"e_w_hd": hd.reshape(16 * D, 512), "e_w_mb": mb.reshape(16 * D, 512), "e_w_out": f(inp["e_w_out"][0])}


def core_maps_L0(inp, b, s):
    xs = np.asarray(inp["x"][b, s * NT:(s + 1) * NT], np.float32)
    cx = np.asarray(inp["ctx"][b], np.float32)
    if s == 1:
        xs = xs[::-1]
        cx = cx[::-1]
    d1, d2 = (0, 1) if s == 0 else (1, 0)
    base = 8192
    cols = np.concatenate([np.arange(base + d1 * 16, base + d1 * 16 + 16), np.arange(base + d2 * 16, base + d2 * 16 + 16),
                           np.arange(base + 32 + d1 * 16, base + 32 + d1 * 16 + 16),
                           np.arange(base + 32 + d2 * 16, base + 32 + d2 * 16 + 16)])
    wab = np.asarray(inp["e_w_in"][0], np.float32)[:, cols]
    return {"xs": np.ascontiguousarray(xs), "ctxs": np.ascontiguousarray(cx), "w_ab": np.ascontiguousarray(wab)}
```

```python
import numpy as np
from contextlib import ExitStack
import concourse.bass as bass
import concourse.mybir as mybir
from concourse.bass_utils import run_bass_kernel_spmd

F32 = mybir.dt.float32
BF16 = mybir.dt.bfloat16
AF = mybir.ActivationFunctionType
ALU = mybir.AluOpType

D = 2048
NT = 1024
NCTX = 256
EPS = 1e-6
BIG = 30000.0
EVEN_COLS = 16448
ODD_COLS = 10240
WSPLIT = 2

C_ID, C_TA, C_TD, C_MA, C_MD, C_BAND, C_ONES, C_BM, NCST = 0, 128, 256, 384, 768, 1152, 1664, 1792, 2432
V_C, V_CC, V_AB0, V_AB1, V_NW0, V_NW1, V_LNW, V_LNB, V_PS, V_FNW = 0, 16, 32, 80, 128, 144, 160, 176, 192, 208
V_CQ, V_CB, V_HN, V_SEL, V_DTB, V_ALOG, NV = 224, 368, 416, 417, 419, 451, 512


class Buf:
    __slots__ = ("name", "w", "r", "excl")

    def __init__(self, name, excl=False):
        self.name = name
        self.w = None
        self.r = []
        self.excl = excl


class Prog:
    ENGS = ("pe", "act", "dve", "pool", "sp")

    def __init__(self, nc, es):
        self.nc = nc
        self.es = es
        self.q = {k: [] for k in self.ENGS}
        self.sems = {}
        self.cnt = {}
        self.known = {k: {} for k in self.ENGS}
        self.nbank = 0
        self.nslot = 0
        self.nw = 3
        self.rings = {}

    def _sem(self, key):
        if key not in self.sems:
            self.sems[key] = self.es.enter_context(self.nc.semaphore("s_" + key))
            self.cnt[key] = 0
        return self.sems[key]

    def op(self, eng, fn, R=(), W=(), key=None, inc=1):
        deps = []
        for b in R:
            if b.w is not None:
                deps.append(b.w)
            if b.excl:
                deps.extend(b.r)
        for b in W:
            if b.w is not None:
                deps.append(b.w)
            deps.extend(b.r)
        waits = {}
        kn = self.known[eng]
        for (k, v) in deps:
            if k == "pe" and eng == "pe" and key is None:
                continue
            if kn.get(k, 0) >= v:
                continue
            if waits.get(k, 0) < v:
                waits[k] = v
        for k, v in waits.items():
            kn[k] = v
        if key is None:
            key = eng
        self._sem(key)
        self.cnt[key] += inc
        t = (key, self.cnt[key])
        self.q[eng].append((tuple(waits.items()), fn, key, inc))
        for b in R:
            b.r.append(t)
        for b in W:
            b.w = t
            b.r = []
        return t

    def barrier(self):
        snap = {k: v for k, v in self.cnt.items() if v > 0}
        for eng in self.ENGS:
            kn = self.known[eng]
            waits = tuple((k, v) for k, v in snap.items() if kn.get(k, 0) < v)
            for k, v in waits:
                kn[k] = v
            self.q[eng].append((waits, None, None, 0))

    def mm(self, out, lhsT, rhs, start, stop, R, W):
        return self.op("pe", lambda e: e.matmul(out, lhsT, rhs, start=start, stop=stop, skip_group_check=True), R, W)

    def tr(self, out, in_, ident, R, W):
        return self.op("pe", lambda e: e.transpose(out, in_, ident), R, W)

    def act(self, out, in_, func, R, W, bias=None, scale=None, accum_out=None):
        kw = {}
        if bias is not None:
            kw["bias"] = bias
        if scale is not None:
            kw["scale"] = scale
        if accum_out is not None:
            kw["accum_out"] = accum_out
        return self.op("act", lambda e: e.activation(out=out, in_=in_, func=func, **kw), R, W)

    def tt(self, eng, out, in0, in1, op, R, W):
        return self.op(eng, lambda e: e.tensor_tensor(out=out, in0=in0, in1=in1, op=op), R, W)

    def ts(self, eng, out, in0, s1, s2, op0, op1, R, W, accum_out=None):
        if accum_out is not None:
            return self.op(eng, lambda e: e.tensor_scalar(out=out, in0=in0, scalar1=s1, scalar2=s2, op0=op0, op1=op1,
                                                          accum_out=accum_out), R, W)
        if op1 is None:
            return self.op(eng, lambda e: e.tensor_scalar(out=out, in0=in0, scalar1=s1, scalar2=None, op0=op0), R, W)
        return self.op(eng, lambda e: e.tensor_scalar(out=out, in0=in0, scalar1=s1, scalar2=s2, op0=op0, op1=op1), R, W)

    def stt(self, out, in0, scalar, in1, op0, op1, R, W, accum_out=None):
        if accum_out is not None:
            return self.op("dve", lambda e: e.scalar_tensor_tensor(out=out, in0=in0, scalar=scalar, in1=in1, op0=op0,
                                                                   op1=op1, accum_out=accum_out), R, W)
        return self.op("dve", lambda e: e.scalar_tensor_tensor(out=out, in0=in0, scalar=scalar, in1=in1, op0=op0,
                                                               op1=op1), R, W)

    def copy(self, eng, out, in_, R, W):
        if eng == "act":
            return self.op("act", lambda e: e.copy(out=out, in_=in_), R, W)
        return self.op(eng, lambda e: e.tensor_copy(out=out, in_=in_), R, W)

    def recip(self, out, in_, R, W):
        return self.op("dve", lambda e: e.reciprocal(out=out, in_=in_), R, W)

    def memset(self, eng, ap, val, W):
        return self.op(eng, lambda e: e.memset(ap, val), (), W)

    def dma(self, eng, out, in_, R, W, key, slow=False):
        if key == "cld":
            self.ncld = getattr(self, "ncld", 0) + 1
            key = f"cld{self.ncld}"
        if slow:
            return self.op(eng, lambda e: e.dma_start(out=out, in_=in_, allow_slow_non_contiguous=True), R, W, key=key, inc=16)
        return self.op(eng, lambda e: e.dma_start(out=out, in_=in_), R, W, key=key, inc=16)

    def flush(self, block):
        engs = {"pe": block.tensor, "act": block.scalar, "dve": block.vector, "pool": block.gpsimd, "sp": block.sync}
        for name in self.ENGS:
            items = self.q[name]
            sems = self.sems
            final = []
            if name == "sp":
                final = [(k, self.cnt[k]) for k in self.cnt if k.startswith("out")]

            def body(e, items=items, final=final):
                for waits, fn, key, inc in items:
                    for k, v in waits:
                        e.wait_ge(sems[k], v)
                    if fn is not None:
                        fn(e).then_inc(sems[key], inc)
                for k, v in final:
                    e.wait_ge(sems[k], v)

            engs[name](body)


class Ring:
    def __init__(self, aps, name):
        self.aps = aps
        self.bufs = [Buf(f"{name}{i}") for i in range(len(aps))]
        self.i = 0

    def next(self):
        k = self.i % len(self.aps)
        self.i += 1
        return self.aps[k], self.bufs[k]


def build(mode="full", nheads=16, groups=None, stop_after=None):
    nc = bass.Bass("TRN2", target_bir_lowering=False)
    dr = {}

    def din(name, shape):
        dr[name] = nc.dram_tensor(name, list(shape), F32, kind="ExternalInput").ap()
        return dr[name]

    vec_d = din("vec", [128, NV])
    cst_d = din("cst", [128, NCST])
    adaw_d = [din("ada_w0", [D, 3 * D]), din("ada_w1", [D, 3 * D])]
    owin_d = din("o_w_in", [D, ODD_COLS])
    ws_d = din("ws", [128, 2048])
    bsb_d = din("bsb", [128, 2048])
    opw_d = din("o_pool_w", [2048, 512])
    owout_d = din("o_w_out", [2 * D, D])
    if mode in ("full", "L0"):
        xs_d = din("xs", [NT, D])
        ctx_d = din("ctxs", [NCTX, D])
        ewhd_d = din("e_w_hd", [16 * D, 512])
        ewmb_d = din("e_w_mb", [16 * D, 512])
        wab_d = din("w_ab", [D, 64])
        ewout_d = din("e_w_out", [2 * D, D])
    if mode == "L1":
        x1in_d = din("x1T_in", [128, 16 * NT])
    if mode == "L0":
        out_d = nc.dram_tensor("out", [128, 16 * NT], F32, kind="ExternalOutput").ap()
    else:
        out_d = nc.dram_tensor("out", [NT, D], F32, kind="ExternalOutput").ap()

    with ExitStack() as es:
        P = Prog(nc, es)

        def sb(name, shape, dt):
            return es.enter_context(nc.sbuf_tensor("sb_" + name, list(shape), dt))

        x1T_t = sb("x1T", [128, 16 * NT], F32)
        G1_t = sb("G1", [128, 8192], F32)
        G2_t = sb("G2", [128, 10240], F32)
        W_t = [sb(f"W{i}", [128, 16 * 512], BF16) for i in range(3)]
        cst = sb("cst", [128, C_BAND], F32)
        vec = sb("vec", [128, NV], F32)
        cbf = sb("cbf", [128, 12 * 128], BF16)
        mod_t = sb("mod", [128, 2 * 96], F32)
        der_t = sb("der", [128, 2 * 5 * 16], F32)
        small = sb("small", [128, 256], F32)
        scin = sb("scin", [128, 32], BF16)
        aux = sb("aux", [128, 2048], F32)
        mb_ring = Ring([aux[:, i * 512:(i + 1) * 512] for i in range(4)], "mb")
        ps_all = es.enter_context(nc.psum_tensor("ps_all", [128, 4096], F32))
        psum = [ps_all[:, i * 512:(i + 1) * 512] for i in range(8)]
        pbuf = [Buf(f"ps{i}", excl=True) for i in range(8)]

        x1T = x1T_t[:].rearrange("p (c t) -> p c t", c=16)
        x1b = [[Buf(f"x1_{dc}_{h}") for h in range(2)] for dc in range(16)]
        Wap = [w[:].rearrange("p (c n) -> p c n", c=16) for w in W_t]
        Wbuf = [Buf(f"W{i}") for i in range(3)]
        cstb = Buf("cst")
        vecb = Buf("vec")
        cbfb = Buf("cbf")
        modb = Buf("mod")
        derb = Buf("der")
        smallb = Buf("small")
        ident_f = cst[:, C_ID:C_ID + 128]
        ident_b = cbf[:, 0:128]
        nident_b = cbf[:, 128:256]
        ones_b = cbf[:, 256:384]
        band_b = [cbf[:, 384 + 128 * i: 512 + 128 * i] for i in range(4)]
        bm_b = [cbf[:, 896 + 128 * i: 1024 + 128 * i] for i in range(5)]

        P.bank_lo = 0

        def bank():
            n = 8 - P.bank_lo
            i = P.bank_lo + (P.nbank % n)
            P.nbank += 1
            return psum[i], pbuf[i]

        def banks(n):
            i = P.nbank % 8
            if i + n > 8:
                P.nbank += 8 - i
                i = 0
            P.nbank += n
            return ps_all[:, i * 512:(i + n) * 512], [pbuf[i + k] for k in range(n)]

        def wslot():
            i = P.nslot % P.nw
            P.nslot += 1
            return Wap[i], Wbuf[i], f"w{i}"

        def load_w(src2d, r0, nrow_chunks, c0, ncols):
            ap, b, key = wslot()
            src = src2d[r0:r0 + 128 * nrow_chunks, c0:c0 + ncols].rearrange("(c p) n -> p c n", p=128)
            nsp = WSPLIT if nrow_chunks % WSPLIT == 0 else 1
            step = nrow_chunks // nsp
            t = None
            for i in range(nsp):
                t = P.dma("pool", ap[:, i * step:(i + 1) * step, 0:ncols], src[:, i * step:(i + 1) * step, :], R=(),
                          W=[b] if i == 0 else [], key=key)
            b.w = t
            return ap, b

        P.dma("sp", cst[:], cst_d[:, 0:C_BAND], (), [cstb], "cld")
        ctmp = G1_t[:, 0:NCST - C_BAND]
        ctmpb = Buf("ctmp")
        P.dma("sp", ctmp, cst_d[:, C_BAND:NCST], (), [ctmpb], "cld")
        P.dma("sp", vec[:], vec_d[:, :], (), [vecb], "cld")
        P.copy("dve", ident_b, ident_f, [cstb], [cbfb])
        P.ts("dve", nident_b, ident_f, -1.0, None, ALU.mult, None, [cstb], [cbfb])
        P.copy("dve", ones_b, ctmp[:, C_ONES - C_BAND:C_ONES - C_BAND + 128], [ctmpb], [cbfb])
        for i in range(4):
            P.copy("dve", band_b[i], ctmp[:, 128 * i:128 * (i + 1)], [ctmpb], [cbfb])
        for i in range(5):
            P.copy("dve", bm_b[i], ctmp[:, C_BM - C_BAND + 128 * i:C_BM - C_BAND + 128 * (i + 1)], [ctmpb], [cbfb])
        P.barrier()

        def emit_mod_gen(l):
            sc3 = scin[:].rearrange("p (c k) -> p c k", k=2)
            if l == 0:
                P.act(sc3[:, :, 0], vec[:, V_C:V_C + 16], AF.Silu, [vecb], [smallb])
                P.act(sc3[:, :, 1], vec[:, V_CC:V_CC + 16], AF.Silu, [vecb], [smallb])
            m3 = mod_t[:, l * 96:(l + 1) * 96].rearrange("p (g k) -> p g k", k=2)
            vab = V_AB0 if l == 0 else V_AB1
            for cb in range(12):
                wap, wb = load_w(adaw_d[l], 0, 16, cb * 512, 512)
                if l == 1:
                    yield
                mps, mpb = bank()
                for j in range(4):
                    for dc in range(16):
                        P.mm(mps[:, 2 * j:2 * j + 2], wap[:, dc, j * 128:(j + 1) * 128], sc3[:, dc, :],
                             dc == 0, dc == 15, [wb, smallb], [mpb])
                mp3 = mps[:, 0:8].rearrange("p (g k) -> p g k", k=2)
                for k in range(2):
                    P.tt("dve", m3[:, cb * 4:cb * 4 + 4, k], mp3[:, :, k], vec[:, vab + cb * 4:vab + cb * 4 + 4], ALU.add,
                         [mpb, vecb], [modb])
                yield
            vnw = V_NW0 if l == 0 else V_NW1
            dd = der_t[:, l * 80:(l + 1) * 80]
            P.stt(dd[:, 0:16], m3[:, 16:32, 0], 1.0, vec[:, vnw:vnw + 16], ALU.add, ALU.mult, [modb, vecb], [derb])
            P.copy("dve", dd[:, 16:32], m3[:, 0:16, 0], [modb], [derb])
            P.copy("dve", dd[:, 32:48], m3[:, 32:48, 0], [modb], [derb])
            P.stt(dd[:, 48:64], m3[:, 16:32, 1], 1.0, vec[:, vnw:vnw + 16], ALU.add, ALU.mult, [modb, vecb], [derb])
            P.copy("dve", dd[:, 64:80], m3[:, 0:16, 1], [modb], [derb])

        def emit_mod(l):
            for _ in emit_mod_gen(l):
                pass
            return der_t[:, l * 80:(l + 1) * 80]

        def emit_L1_prelude(mod_done=False):
            dd = der_t[:, 80:160] if mod_done else emit_mod(1)
            wsT_t = aux[:, 0:1024].bitcast(BF16)
            bias2_t = aux[:, 1024:2048].bitcast(BF16)
            wsTb = Buf("wsT")
            bias2b = Buf("bias2")
            wap, wb, key = wslot()
            ws_sb = wap[:, 0:4, :].rearrange("p c n -> p (c n)")
            P.dma("pool", ws_sb, ws_d[:, :], (), [wb], key)
            for q4 in range(4):
                ps, pb = bank()
                psb = ps[:].bitcast(BF16)
                for j in range(4):
                    g = q4 * 4 + j
                    P.tr(psb[:, j * 128:(j + 1) * 128], ws_sb[:, g * 128:(g + 1) * 128], ident_b, [wb, cbfb], [pb])
                P.copy("dve", wsT_t[:, q4 * 512:(q4 + 1) * 512], psb[:, 0:512], [pb], [wsTb])
            bs_sb = G1_t[:, 0:2048]
            g1b = Buf("g1tmp")
            P.dma("sp", bs_sb, bsb_d[:, :], (), [g1b], "cld")
            for q4 in range(4):
                ps, pb = bank()
                P.mm(ps[:, :], ones_b, wsT_t[:, q4 * 512:(q4 + 1) * 512], True, True, [cbfb, wsTb], [pb])
                for j in range(4):
                    g = q4 * 4 + j
                    P.stt(bias2_t[:, g * 128:(g + 1) * 128], ps[:, j * 128:(j + 1) * 128],
                          vec[:, V_LNB + g:V_LNB + g + 1], bs_sb[:, g * 128:(g + 1) * 128], ALU.mult, ALU.add,
                          [pb, vecb, g1b], [bias2b])
            return dd, wsT_t, wsTb, bias2_t, bias2b

        def emit_rstd_bc(src_of_dc, srcbufs_of_dc, ntok, tmp_ring, out_ap, outb, inv_n):
            nb = (ntok + 511) // 512
            for bi in range(nb):
                n0 = bi * 512
                n1 = min(ntok, n0 + 512)
                ps, pb = bank()
                for dc in range(16):
                    sq, sqb = tmp_ring.next()
                    P.act(sq[:, 0:n1 - n0], src_of_dc(dc)[:, n0:n1], AF.Square, srcbufs_of_dc(dc), [sqb])
                    P.mm(ps[:, 0:n1 - n0], ones_b, sq[:, 0:n1 - n0], dc == 0, dc == 15, [cbfb, sqb], [pb])
                P.ts("dve", out_ap[:, n0:n1], ps[:, 0:n1 - n0], inv_n, EPS, ALU.mult, ALU.add, [pb], [outb])
                P.act(out_ap[:, n0:n1], out_ap[:, n0:n1], AF.Sqrt, [outb], [outb])
                P.recip(out_ap[:, n0:n1], out_ap[:, n0:n1], [outb], [outb])

        def emit_L1_half(half, dd, wsT_t, wsTb, bias2_t, bias2b):
            T0 = half * 512
            h1T = G2_t[:, 0:4096].bitcast(BF16).rearrange("p (c t) -> p c t", c=16)
            vtok = G2_t[:, 4096:8192].bitcast(BF16).rearrange("p (t n) -> p t n", t=4)
            gat = G1_t[:, 0:4096].bitcast(BF16).rearrange("p (c t) -> p c t", c=16)
            tmpA = G1_t[:, 4096:8192]
            rs = G2_t[:, 8192:8704]
            misc = G2_t[:, 8704:10240]
            hb = [Buf(f"h1_{dc}") for dc in range(16)]
            vb_ = [Buf(f"vt_{t}") for t in range(4)]
            gb = [Buf(f"gat_{c}") for c in range(16)]
            rsb = Buf("rs")
            miscb = Buf("misc")
            sq_ring = Ring([tmpA[:, i * 256:(i + 1) * 256].bitcast(BF16) for i in range(3)], "sq")
            f_ring = Ring([tmpA[:, 768 + i * 512:768 + (i + 1) * 512] for i in range(6)], "f")
            emit_rstd_bc(lambda dc: x1T[:, dc, T0:T0 + 512], lambda dc: [x1b[dc][half]], 512, sq_ring, rs, rsb, 1.0 / D)
            for dc in range(16):
                t, tb = f_ring.next()
                P.tt("dve", t, x1T[:, dc, T0:T0 + 512], rs, ALU.mult, [x1b[dc][half], rsb], [tb])
                P.act(h1T[:, dc, :], t, AF.Identity, [tb, derb], [hb[dc]], bias=dd[:, 16 + dc:17 + dc],
                      scale=dd[:, dc:dc + 1])
            st = misc[:, 0:64]
            stb = [Buf(f"st{i}") for i in range(32)]
            for vbk in range(4):
                wap, wb = load_w(owin_d, 0, 16, 2048 + vbk * 512, 512)
                for t4 in range(4):
                    ps, pb = bank()
                    for dc in range(16):
                        P.mm(ps[:, :], h1T[:, dc, t4 * 128:(t4 + 1) * 128], wap[:, dc, :], dc == 0, dc == 15,
                             [hb[dc], wb], [pb])
                    vblk = vtok[:, t4, vbk * 512:(vbk + 1) * 512]
                    P.act(vblk, ps[:, :], AF.Gelu_apprx_tanh, [pb], [vb_[t4]])
                    j1, j1b = f_ring.next()
                    P.act(j1, vblk, AF.Square, [vb_[t4]], [j1b, stb[16 + t4 * 4 + vbk]],
                          accum_out=st[:, 16 + t4 * 4 + vbk:17 + t4 * 4 + vbk])
                    j2, j2b = f_ring.next()
                    P.ts("dve", j2, vblk, 1.0, 0.0, ALU.mult, ALU.add, [vb_[t4]], [j2b, stb[t4 * 4 + vbk]],
                         accum_out=st[:, t4 * 4 + vbk:t4 * 4 + vbk + 1])
            st3 = st[:, 0:32].rearrange("p (a t v) -> p a t v", a=2, v=4)
            red = misc[:, 64:72].rearrange("p (a t) -> p a t", a=2)
            P.tt("dve", red, st3[:, :, :, 0], st3[:, :, :, 1], ALU.add, stb, [miscb])
            P.tt("dve", red, red, st3[:, :, :, 2], ALU.add, [miscb], [miscb])
            P.tt("dve", red, red, st3[:, :, :, 3], ALU.add, [miscb], [miscb])
            mu = misc[:, 72:76]
            var = misc[:, 76:80]
            rstd = misc[:, 80:84]
            nmr = misc[:, 84:88]
            P.ts("dve", mu, red[:, 0, :], 1.0 / 2048, None, ALU.mult, None, [miscb], [miscb])
            P.ts("dve", var, red[:, 1, :], 1.0 / 2048, EPS, ALU.mult, ALU.add, [miscb], [miscb])
            P.tt("dve", nmr, mu, mu, ALU.mult, [miscb], [miscb])
            P.tt("dve", var, var, nmr, ALU.subtract, [miscb], [miscb])
            P.act(var, var, AF.Sqrt, [miscb], [miscb])
            P.recip(rstd, var, [miscb], [miscb])
            P.stt(nmr, mu, -1.0, rstd, ALU.mult, ALU.mult, [miscb], [miscb])
            for t4 in range(4):
                P.ts("dve", vtok[:, t4, :], vtok[:, t4, :], rstd[:, t4:t4 + 1], nmr[:, t4:t4 + 1], ALU.mult, ALU.add,
                     [vb_[t4], miscb], [vb_[t4]])
            for g4 in range(4):
                wu, wub = load_w(owin_d, 0, 16, g4 * 512, 512)
                wz, wzb = load_w(owin_d, 0, 16, 4096 + g4 * 512, 512)
                for j in range(4):
                    g = g4 * 4 + j
                    pu, pub = bank()
                    for dc in range(16):
                        P.mm(pu[:, :], wu[:, dc, j * 128:(j + 1) * 128], h1T[:, dc, :], dc == 0, dc == 15,
                             [wub, hb[dc]], [pub])
                    pz, pzb = bank()
                    for dc in range(16):
                        P.mm(pz[:, :], wz[:, dc, j * 128:(j + 1) * 128], h1T[:, dc, :], dc == 0, dc == 15,
                             [wzb, hb[dc]], [pzb])
                    pss, pssb = bank()
                    for t4 in range(4):
                        P.mm(pss[:, t4 * 128:(t4 + 1) * 128], vtok[:, t4, g * 128:(g + 1) * 128],
                             wsT_t[:, g * 128:(g + 1) * 128], True, True, [vb_[t4], wsTb], [pssb])
                    gu, gub = f_ring.next()
                    P.act(gu, pu[:, :], AF.Gelu_apprx_tanh, [pub], [gub])
                    sz, szb = f_ring.next()
                    P.act(sz, pz[:, :], AF.Silu, [pzb], [szb])
                    s2, s2b = f_ring.next()
                    b2 = bias2_t[:, g * 128:(g + 1) * 128]
                    for t4 in range(4):
                        P.stt(s2[:, t4 * 128:(t4 + 1) * 128], pss[:, t4 * 128:(t4 + 1) * 128],
                              vec[:, V_LNW + g:V_LNW + g + 1], b2, ALU.mult, ALU.add, [pssb, vecb, bias2b], [s2b])
                    P.tt("pool", gu, gu, sz, ALU.mult, [gub, szb], [gub])
                    P.tt("dve", gat[:, g, :], s2, gu, ALU.mult, [s2b, gub], [gb[g]])
            emit_wout_pass(owout_d, 0, gat, gb, half, dd)
            pt = G2_t[:, 4096:5120].bitcast(BF16).rearrange("p (t n) -> p t n", t=4)
            dfT = G2_t[:, 5120:6144].bitcast(BF16).rearrange("p (c t) -> p c t", c=4)
            ptb = [Buf(f"pt_{t}") for t in range(4)]
            dfb = [Buf(f"df_{c}") for c in range(4)]
            for pg in range(4):
                wp, wpb = load_w(owin_d, 0, 16, 6144 + pg * 512, 512)
                for t4 in range(4):
                    ps, pb = bank()
                    for dc in range(16):
                        P.mm(ps[:, :], h1T[:, dc, t4 * 128:(t4 + 1) * 128], wp[:, dc, :], dc == 0, dc == 15,
                             [hb[dc], wpb], [pb])
                    P.copy("act", pt[:, t4, 0:512], ps[:, :], [pb], [ptb[t4]] + (vb_ if pg == 0 else []))
                for cc in range(4):
                    ps, pb = bank()
                    for t4 in range(4):
                        P.mm(ps[:, t4 * 128:(t4 + 1) * 128], pt[:, t4, cc * 128:(cc + 1) * 128], band_b[pg], True, True,
                             [ptb[t4], cbfb], [pb])
                    P.copy("dve", dfT[:, cc, :], ps[:, :], [pb], [dfb[cc]] + (vb_ if pg == 0 else []))
                wz, wzb = load_w(owin_d, 0, 16, 8192 + pg * 512, 512)
                wq, wqb = load_w(opw_d, pg * 512, 4, 0, 512)
                for j in range(4):
                    g = pg * 4 + j
                    py, pyb = bank()
                    for cc in range(4):
                        P.mm(py[:, :], wq[:, cc, j * 128:(j + 1) * 128], dfT[:, cc, :], cc == 0, cc == 3,
                             [wqb, dfb[cc]], [pyb])
                    pz, pzb = bank()
                    for dc in range(16):
                        P.mm(pz[:, :], wz[:, dc, j * 128:(j + 1) * 128], h1T[:, dc, :], dc == 0, dc == 15,
                             [wzb, hb[dc]], [pzb])
                    sz, szb = f_ring.next()
                    P.act(sz, pz[:, :], AF.Silu, [pzb], [szb])
                    P.stt(gat[:, g, :], py[:, :], vec[:, V_PS + g:V_PS + g + 1], sz, ALU.mult, ALU.mult,
                          [pyb, vecb, szb], [gb[g]])
            emit_wout_pass(owout_d, 2048, gat, gb, half, dd)

        def emit_wout_pass(wsrc, r0, gat, gb, half, dd, first=False, ntok=512, tok0=None):
            T0 = half * 512 if tok0 is None else tok0
            for db in range(4):
                ww, wwb = load_w(wsrc, r0, 16, db * 512, 512)
                for j in range(4):
                    dch = db * 4 + j
                    for n0 in range(0, ntok, 512):
                        ps, pb = bank()
                        for fc in range(16):
                            P.mm(ps[:, :], ww[:, fc, j * 128:(j + 1) * 128], gat[:, fc, n0:n0 + 512], fc == 0, fc == 15,
                                 [wwb, gb[fc]], [pb])
                        hh = (T0 + n0) // 512
                        dst = x1T[:, dch, T0 + n0:T0 + n0 + 512]
                        if first:
                            P.ts("dve", dst, ps[:, :], dd[:, 32 + dch:33 + dch], None, ALU.mult, None, [pb, derb],
                                 [x1b[dch][hh]])
                        else:
                            P.stt(dst, ps[:, :], dd[:, 32 + dch:33 + dch], dst, ALU.mult, ALU.add, [pb, derb, x1b[dch][hh]],
                                  [x1b[dch][hh]])

        def emit_final(half):
            T0 = half * 512
            tmpA = G1_t[:, 0:4096]
            rs = G2_t[:, 8192:8704]
            rsb = Buf("rs_f")
            sq_ring = Ring([tmpA[:, i * 256:(i + 1) * 256].bitcast(BF16) for i in range(3)], "sqf")
            f_ring = Ring([tmpA[:, 768 + i * 512:768 + (i + 1) * 512] for i in range(4)], "ff")
            stg = [G2_t[:, 0:2048], G2_t[:, 2048:4096], G2_t[:, 4096:6144], G2_t[:, 6144:8192]]
            stgb = [Buf(f"stg{i}") for i in range(4)]
            emit_rstd_bc(lambda dc: x1T[:, dc, T0:T0 + 512], lambda dc: [x1b[dc][half]], 512, sq_ring, rs, rsb, 1.0 / D)
            xn = [None] * 16
            for d4 in range(4):
                tl = []
                for j in range(4):
                    dc = d4 * 4 + j
                    t, tb = f_ring.next()
                    P.stt(t, x1T[:, dc, T0:T0 + 512], vec[:, V_FNW + dc:V_FNW + dc + 1], rs, ALU.mult, ALU.mult,
                          [x1b[dc][half], vecb, rsb], [tb])
                    tl.append((t, tb))
                for t4 in range(4):
                    ps, pb = bank()
                    for j in range(4):
                        P.tr(ps[:, j * 128:(j + 1) * 128], tl[j][0][:, t4 * 128:(t4 + 1) * 128], ident_f,
                             [tl[j][1], cstb], [pb])
                    eng = "act" if (t4 % 2 == 0) else "dve"
                    P.copy(eng, stg[t4][:, d4 * 512:(d4 + 1) * 512], ps[:, :], [pb], [stgb[t4]])
            for t4 in range(4):
                r = T0 + t4 * 128
                P.dma("sp", out_d[r:r + 128, :], stg[t4], [stgb[t4]], [], f"out{t4}")


        def emit_L0(with_mod1=False):
            P.nw = 2
            dd = emit_mod(0)
            P.barrier()
            NTA = NCTX + NT
            hT = G2_t[:].bitcast(BF16).rearrange("p (c t) -> p c t", c=16)
            hTb = Buf("hT")
            gat = G1_t[:].bitcast(BF16).rearrange("p (c t) -> p c t", c=16)
            gb = [Buf(f"g0_{c}") for c in range(16)]
            XR = x1T_t
            XB = W_t[2][:].bitcast(F32)
            stg_ring = Ring([G1_t[:, i * 2048:(i + 1) * 2048] for i in range(4)], "xstg")
            ssr = small[:, 0:32]
            ssb = [Buf(f"ss{i}") for i in range(10)]
            for t in range(10):
                stg, stgb = stg_ring.next()
                src = ctx_d[t * 128:(t + 1) * 128, :] if t < 2 else xs_d[(t - 2) * 128:(t - 1) * 128, :]
                P.dma("sp", stg, src, (), [stgb], f"xl{t % 4}")
                junk = XR[:, 0:2048]
                junkb = Buf("junk")
                P.act(junk, stg, AF.Square, [stgb], [junkb, ssb[t]], accum_out=ssr[:, t:t + 1])
                P.ts("dve", ssr[:, t:t + 1], ssr[:, t:t + 1], 1.0 / D, EPS, ALU.mult, ALU.add, [ssb[t]], [ssb[t]])
                P.act(ssr[:, t:t + 1], ssr[:, t:t + 1], AF.Sqrt, [ssb[t]], [ssb[t]])
                P.recip(ssr[:, t:t + 1], ssr[:, t:t + 1], [ssb[t]], [ssb[t]])
                P.ts("dve", stg, stg, ssr[:, t:t + 1], None, ALU.mult, None, [stgb, ssb[t]], [stgb])
                so, bo = (48, 64) if t < 2 else (0, 16)
                for d4 in range(4):
                    ps, pb = bank()
                    for j in range(4):
                        dc = d4 * 4 + j
                        P.tr(ps[:, j * 128:(j + 1) * 128], stg[:, dc * 128:(dc + 1) * 128], ident_f, [stgb, cstb], [pb])
                    for j in range(4):
                        dc = d4 * 4 + j
                        P.act(hT[:, dc, t * 128:(t + 1) * 128], ps[:, j * 128:(j + 1) * 128], AF.Identity, [pb, derb], [hTb],
                              bias=dd[:, bo + dc:bo + dc + 1], scale=dd[:, so + dc:so + dc + 1])
            P.barrier()
            o = 0
            ob = 0
            def carve(n):
                nonlocal o
                a = XR[:, o:o + n]
                o += n
                assert o <= 16384, o
                return a
            def carveB(n):
                nonlocal ob
                a = XB[:, ob:ob + n]
                ob += n
                assert ob <= 4096, ob
                return a
            gc3 = carve(320).rearrange("p (c k) -> p c k", c=10)
            glb3 = carve(320).rearrange("p (c k) -> p c k", c=10)
            ngc3 = carve(320).rearrange("p (c k) -> p c k", c=10)
            bexp3 = carve(320).rearrange("p (c k) -> p c k", c=10)
            beta3 = carve(320).rearrange("p (c k) -> p c k", c=10)
            egl3 = carve(320).rearrange("p (c k) -> p c k", c=10)
            gl3 = carve(320).rearrange("p (c k) -> p c k", c=10)
            o_save = o
            o = 2240 + 7552
            Graw = carve(640).rearrange("p (c k) -> p c k", c=10)
            ones_f = carve(128)
            gtmp = carve(320).rearrange("p (c k) -> p c k", c=10)
            o = o_save
            gatesb = Buf("gates")
            P.memset("dve", ones_f, 1.0, [gatesb])
            P.memset("dve", small[:, 64:65], EPS, [smallb])
            wap, wb, key = wslot()
            P.dma("pool", wap[:, :, 0:64], wab_d[:, :].rearrange("(c p) n -> p c n", p=128), (), [wb], key)
            pg2, pg2b = banks(2)
            for t in range(10):
                for dc in range(16):
                    P.mm(pg2[:, t * 64:(t + 1) * 64], hT[:, dc, t * 128:(t + 1) * 128], wap[:, dc, 0:64], dc == 0, dc == 15,
                         [hTb, wb], [pg2b[(t * 64) // 512]])
            pg3 = pg2[:, 0:640].rearrange("p (c k) -> p c k", c=10)
            dtb_bc = vec[:, V_DTB:V_DTB + 32]
            nA = small[:, 32:64]
            P.act(nA, vec[:, V_ALOG:V_ALOG + 32], AF.Exp, [vecb], [smallb])
            P.ts("dve", nA, nA, -1.0, None, ALU.mult, None, [smallb], [smallb])
            for t in range(10):
                P.tt("dve", Graw[:, t, 0:32], pg3[:, t, 0:32], dtb_bc, ALU.add, pg2b + [vecb], [gatesb])
            P.act(Graw[:, :, 0:32], Graw[:, :, 0:32], AF.Exp, [gatesb], [gatesb])
            P.act(Graw[:, :, 0:32], Graw[:, :, 0:32], AF.Ln, [gatesb], [gatesb], bias=1.0)
            for t in range(10):
                P.tt("dve", Graw[:, t, 0:32], Graw[:, t, 0:32], nA, ALU.mult, [gatesb, smallb], [gatesb])
            P.act(Graw[:, :, 32:64], pg3[:, :, 32:64], AF.Exp, pg2b, [gatesb], scale=-1.0)
            P.act(Graw[:, :, 32:64], Graw[:, :, 32:64], AF.Ln, [gatesb], [gatesb], bias=1.0)
            P.ts("dve", Graw[:, :, 32:64], Graw[:, :, 32:64], -1.0, None, ALU.mult, None, [gatesb], [gatesb])
            pcs_, pcsb = bank()
            ptt_, pttb = bank()
            pc3 = pcs_[:, 0:320].rearrange("p (c k) -> p c k", c=10)
            pt3 = ptt_[:, 0:320].rearrange("p (c k) -> p c k", c=10)
            tri_a = cst[:, C_TA:C_TA + 128]
            tri_d = cst[:, C_TD:C_TD + 128]
            for t in range(10):
                P.mm(pc3[:, t, 0:16], tri_a, Graw[:, t, 0:16], True, True, [cstb, gatesb], [pcsb])
                P.mm(pc3[:, t, 16:32], tri_d, Graw[:, t, 16:32], True, True, [cstb, gatesb], [pcsb])
                P.mm(pt3[:, t, :], ones_f, Graw[:, t, 0:32], True, True, [gatesb], [pttb])
            P.copy("act", gc3, pc3, [pcsb], [gatesb])
            P.ts("dve", ngc3, pc3, -1.0, None, ALU.mult, None, [pcsb], [gatesb])
            P.tt("dve", glb3, pc3, Graw[:, :, 32:64], ALU.add, [pcsb, gatesb], [gatesb])
            P.act(bexp3, glb3, AF.Exp, [gatesb], [gatesb])
            P.act(beta3, Graw[:, :, 32:64], AF.Exp, [gatesb], [gatesb])
            P.tt("dve", gtmp, pt3, gc3, ALU.subtract, [pttb, gatesb], [gatesb])
            P.act(egl3, gtmp, AF.Exp, [gatesb], [gatesb])
            P.act(gl3, pt3, AF.Exp, [pttb], [gatesb])
            P.barrier()
            slots = []
            for i in range(2):
                sl = {}
                sl["kqT"] = carve(1280).bitcast(BF16).rearrange("p (c w t) -> p c w t", c=10, w=2)
                sl["ktok"] = carve(640).bitcast(BF16).rearrange("p (c d) -> p c d", c=10)
                sl["vtok"] = carve(640).bitcast(BF16).rearrange("p (c d) -> p c d", c=10)
                sl["zAs"] = carve(512).bitcast(BF16)
                sl["o1"] = carve(512).bitcast(BF16)
                sl["S"] = carve(128)
                sl["Sb"] = carve(64).bitcast(BF16)
                for nm in ("kqT", "ktok", "vtok", "zAs", "o1", "S", "Sb"):
                    sl[nm + "_b"] = Buf(f"{nm}{i}")
                slots.append(sl)
            oc = 0
            def carveC(n):
                nonlocal oc
                a = aux[:, oc:oc + n]
                oc += n
                assert oc <= 2048, oc
                return a
            CH = 1664
            chain_bases = [carve(CH), carve(CH), carve(CH), carveB(CH), carveC(CH)]
            vn_ring = [Ring([carve(64).bitcast(BF16) for _ in range(2)], f"vn{p}") for p in range(2)]
            fsq = carve(512).bitcast(BF16)
            fsqb = Buf("fsq")
            oacc = carveB(1024)
            oaccb = Buf("oacc")
            Gt = carveB(256)
            Gtb = Buf("Gt")

            def mk_chain_slot(base, nm):
                bfv = lambda a: a.bitcast(BF16)
                cs = dict(
                    em=base[:, 0:384], xyrtA=bfv(base[:, 0:256]), ab=bfv(base[:, 256:384]),
                    F=bfv(base[:, 384:576]), r1t1=bfv(base[:, 384:512]), a2=bfv(base[:, 512:576]),
                    egc=bfv(base[:, 576:640]), kbg=bfv(base[:, 576:640]),
                    y0=bfv(base[:, 640:704]), nw=bfv(base[:, 640:704]),
                    xa=bfv(base[:, 704:832]), msk=bfv(base[:, 832:1152]), xyrtB=bfv(base[:, 1152:1408]),
                    rf=bfv(base[:, 1408:1472]), vb=bfv(base[:, 1472:1536]), kg=bfv(base[:, 1536:1600]),
                    qg=bfv(base[:, 1600:1664]), buf=Buf("chain_" + nm))
                return cs
            chain_slots = [[mk_chain_slot(chain_bases[i], f"p0_{i}") for i in range(3)],
                           [mk_chain_slot(chain_bases[3], "p1_0"), mk_chain_slot(chain_bases[4], "p1_1")]]
            xA = carve(832)
            xBt = carveB(832)
            bfv_ = lambda a: a.bitcast(BF16)
            cs6 = dict(
                em=xA[:, 0:384], xyrtA=bfv_(xA[:, 0:256]), ab=bfv_(xA[:, 256:384]),
                F=bfv_(xA[:, 384:576]), r1t1=bfv_(xA[:, 384:512]), a2=bfv_(xA[:, 512:576]),
                egc=bfv_(xA[:, 576:640]), kbg=bfv_(xA[:, 576:640]),
                y0=bfv_(xA[:, 640:704]), nw=bfv_(xA[:, 640:704]),
                xa=bfv_(xA[:, 704:832]), msk=bfv_(xBt[:, 0:320]), xyrtB=bfv_(xBt[:, 320:576]),
                rf=bfv_(xBt[:, 576:640]), vb=bfv_(xBt[:, 640:704]), kg=bfv_(xBt[:, 704:768]),
                qg=bfv_(xBt[:, 768:832]), buf=Buf("chain_p1_2"))
            chain_slots[1].append(cs6)
            for j, cs_ in enumerate(chain_slots[0] + chain_slots[1]):
                cs_["pcs"] = psum[j // 2][:, (j % 2) * 256:(j % 2) * 256 + 256]
                cs_["pcb"] = pbuf[j // 2]
            cin_d = [nc.dram_tensor(f"cin{h}", [128, 128], F32) for h in range(16)]
            cout_d = [nc.dram_tensor(f"cout{h}", [256, 128], F32) for h in range(16)]
            cinb = [Buf(f"cin{h}") for h in range(16)]
            coutb = [Buf(f"cout{h}") for h in range(16)]
            BLK = ((0, 512), (512, 1024), (1024, 1280))

            cvq, cvk, cvv = chain_bases[0][:, 0:1280], chain_bases[1][:, 0:1280], chain_bases[2][:, 0:1280]
            sqq = chain_bases[3][:, 0:640].bitcast(BF16)
            sqk = chain_bases[3][:, 640:1280].bitcast(BF16)
            vTt = chain_bases[4][:, 0:640].bitcast(BF16)

            head_w = {}

            def load_head_w(h):
                head_w[h] = load_w(ewhd_d, h * D, 16, 0, 512)

            def prep_head(h):
                sl = slots[h % 2]
                wap, wb = head_w.pop(h)

                def proj_conv(which, cv, cvb):
                    ps3, pb3 = banks(3)
                    for bi, (n0, n1) in enumerate(BLK):
                        for dc in range(16):
                            P.mm(ps3[:, n0:n1], wap[:, dc, which * 128:(which + 1) * 128], hT[:, dc, n0:n1], dc == 0, dc == 15,
                                 [wb, hTb], [pb3[bi]])
                    grp = which * 16 + h
                    w0 = vec[:, V_CQ + grp * 3:V_CQ + grp * 3 + 1]
                    w1 = vec[:, V_CQ + grp * 3 + 1:V_CQ + grp * 3 + 2]
                    w2 = vec[:, V_CQ + grp * 3 + 2:V_CQ + grp * 3 + 3]
                    P.act(cv, ps3[:, 0:NTA], AF.Identity, pb3 + [vecb], [cvb], scale=w1)
                    P.stt(cv[:, 1:256], ps3[:, 0:255], w0, cv[:, 1:256], ALU.mult, ALU.add, pb3 + [vecb, cvb], [cvb])
                    P.stt(cv[:, 0:255], ps3[:, 1:256], w2, cv[:, 0:255], ALU.mult, ALU.add, pb3 + [vecb, cvb], [cvb])
                    cvx = cv[:, 256:NTA].rearrange("p (r t) -> p r t", t=64)
                    psx = ps3[:, 256:NTA].rearrange("p (r t) -> p r t", t=64)
                    P.stt(cvx[:, :, 1:64], psx[:, :, 0:63], w0, cvx[:, :, 1:64], ALU.mult, ALU.add, pb3 + [vecb, cvb], [cvb])
                    P.stt(cvx[:, :, 0:63], psx[:, :, 1:64], w2, cvx[:, :, 0:63], ALU.mult, ALU.add, pb3 + [vecb, cvb], [cvb])

                def to_tokmajor(src_T, srcb, dst, dstb):
                    pa, pab = bank()
                    pa_b = pa[:].bitcast(BF16)
                    for c in range(8):
                        P.tr(pa_b[:, c * 128:(c + 1) * 128], src_T(c), ident_b, [srcb, cbfb], [pab])
                    P.copy("act", dst[:, 0:8, :], pa_b[:, 0:1024].rearrange("p (c d) -> p c d", c=8), [pab], [dstb])
                    pa2, pa2b = bank()
                    pa2_b = pa2[:].bitcast(BF16)
                    for c in range(8, 10):
                        P.tr(pa2_b[:, (c - 8) * 128:(c - 7) * 128], src_T(c), ident_b, [srcb, cbfb], [pa2b])
                    P.copy("dve", dst[:, 8:10, :], pa2_b[:, 0:256].rearrange("p (c d) -> p c d", c=2), [pa2b], [dstb])

                def chain_qk(which, cv, sq):
                    cvb, sqb = Buf("cv"), Buf("sq")
                    proj_conv(which, cv, cvb)
                    yield
                    P.act(cv, cv, AF.Silu, [cvb], [cvb])
                    P.tt("pool", sq, cv, cv, ALU.mult, [cvb], [sqb])
                    yield
                    ss3, ssb3 = banks(3)
                    for bi, (n0, n1) in enumerate(BLK):
                        P.mm(ss3[:, n0:n1], ones_b, sq[:, n0:n1], True, True, [cbfb, sqb], [ssb3[bi]])
                    P.act(ss3[:, 0:NTA], ss3[:, 0:NTA], AF.Ln, ssb3 + [smallb], ssb3, bias=small[:, 64:65])
                    P.act(ss3[:, 0:NTA], ss3[:, 0:NTA], AF.Exp, ssb3, ssb3, scale=-0.5)
                    cv3 = cv.rearrange("p (c t) -> p c t", c=10)
                    ri3 = ss3[:, 0:NTA].rearrange("p (c t) -> p c t", c=10)
                    if which == 0:
                        P.stt(sl["kqT"][:, :, 1, :], cv3, 128.0 ** -0.5, ri3, ALU.mult, ALU.mult, [cvb] + ssb3, [sl["kqT_b"]])
                    else:
                        P.tt("dve", sl["kqT"][:, :, 0, :], cv3, ri3, ALU.mult, [cvb] + ssb3, [sl["kqT_b"]])
                        yield
                        to_tokmajor(lambda c: sl["kqT"][:, c, 0, :], sl["kqT_b"], sl["ktok"], sl["ktok_b"])

                def chain_v():
                    cvb, vTb = Buf("cvv"), Buf("vT")
                    proj_conv(2, cvv, cvb)
                    yield
                    P.act(vTt, cvv, AF.Silu, [cvb], [vTb])
                    yield
                    to_tokmajor(lambda c: vTt[:, c * 128:(c + 1) * 128], vTb, sl["vtok"], sl["vtok_b"])

                def chain_z():
                    pz2, pz2b = banks(2)
                    for bi in range(2):
                        n0 = 256 + bi * 512
                        for dc in range(16):
                            P.mm(pz2[:, bi * 512:(bi + 1) * 512], wap[:, dc, 384:512], hT[:, dc, n0:n0 + 512], dc == 0, dc == 15,
                                 [wb, hTb], [pz2b[bi]])
                    P.act(sl["zAs"], pz2[:, 0:1024], AF.Silu, pz2b, [sl["zAs_b"]])
                    P.memset("pool", sl["S"], 0.0, [sl["S_b"]])
                    P.memset("pool", sl["Sb"], 0.0, [sl["Sb_b"]])
                    return
                    yield

                gens = [chain_qk(0, cvq, sqq), chain_qk(1, cvk, sqk), chain_v(), chain_z()]
                while gens:
                    keep = []
                    for g in gens:
                        try:
                            next(g)
                            keep.append(g)
                        except StopIteration:
                            pass
                    gens = keep

            def intra_gen(h, ph, c, cs):
                sl = slots[h % 2]
                gcol = ph * 16 + h
                kq = sl["kqT"]
                kqb = sl["kqT_b"]
                cb = cs["buf"]
                pcs, pcb = cs["pcs"], cs["pcb"]
                P.mm(pcs[:, 0:256], kq[:, c, 0, :], kq[:, c, :, :].rearrange("p w t -> p (w t)"), True, True, [kqb], [pcb])
                pes, peb = bank()
                P.tr(pes[:, 0:128], glb3[:, c, gcol:gcol + 1].to_broadcast([128, 128]), ident_f, [gatesb, cstb], [peb])
                P.tr(pes[:, 128:256], gc3[:, c, gcol:gcol + 1].to_broadcast([128, 128]), ident_f, [gatesb, cstb], [peb])
                P.tr(pes[:, 256:384], gc3[:, c, gcol:gcol + 1].to_broadcast([128, 128]), ident_f, [gatesb, cstb], [peb])
                mbase = C_MA if ph == 0 else C_MD
                P.act(cs["egc"], pes[:, 128:256], AF.Exp, [peb], [cb])
                P.tt("dve", cs["em"], pes[:, 0:384], cst[:, mbase:mbase + 384], ALU.add, [peb, cstb], [cb])
                P.tt("pool", cs["qg"], kq[:, c, 1, :], cs["egc"], ALU.mult, [kqb], [cb])
                yield
                P.act(cs["F"][:, 0:256], cs["em"][:, 0:256], AF.Exp, [gatesb], [cb], bias=ngc3[:, c, gcol:gcol + 1])
                P.act(cs["F"][:, 256:384], cs["em"][:, 256:384], AF.Exp, [gatesb], [cb], bias=glb3[:, c, gcol:gcol + 1],
                      scale=-1.0)
                yield
                P.tt("dve", cs["xa"], pcs[:, 0:256], cs["F"][:, 0:256], ALU.mult, [pcb], [cb])
                P.tt("dve", cs["y0"], pcs[:, 0:128], cs["F"][:, 256:384], ALU.mult, [pcb], [cb])
                P.tt("pool", cs["vb"], sl["vtok"][:, c, :], beta3[:, c, gcol:gcol + 1].to_broadcast([128, 128]), ALU.mult,
                     [sl["vtok_b"], gatesb], [cb])
                P.tt("pool", cs["kg"], sl["ktok"][:, c, :], egl3[:, c, gcol:gcol + 1].to_broadcast([128, 128]), ALU.mult,
                     [sl["ktok_b"], gatesb], [cb])
                yield
                msk = cs["msk"]
                m1x, m1y, m2y = (bm_b[2], bm_b[1], bm_b[3]) if ph == 0 else (bm_b[1], bm_b[2], bm_b[4])
                P.tt("pool", msk[:, 0:128], cs["xa"][:, 0:128], bm_b[0], ALU.mult, [cbfb], [cb])
                P.tt("pool", msk[:, 128:256], cs["y0"], bm_b[0], ALU.mult, [cbfb], [cb])
                yield
                P.tt("pool", msk[:, 256:384], cs["xa"][:, 0:128], m1x, ALU.mult, [cbfb], [cb])
                P.tt("pool", msk[:, 384:512], cs["y0"], m1y, ALU.mult, [cbfb], [cb])
                P.tt("pool", msk[:, 512:640], cs["y0"], m2y, ALU.mult, [cbfb], [cb])
                Xk, Yk = msk[:, 0:128], msk[:, 128:256]
                Rk = ident_b
                for k in range(5):
                    prs, prb = bank()
                    if k <= 3:
                        P.mm(prs[:, 0:128], Yk, Xk, True, True, [cb], [prb])
                        P.mm(prs[:, 128:256], Xk, Yk, True, True, [cb], [prb])
                    P.mm(prs[:, 256:384], ident_b, Rk, True, False, [cbfb, cb], [prb])
                    P.mm(prs[:, 256:384], Yk, nident_b if k == 0 else Rk, False, True, [cb, cbfb], [prb])
                    nx = cs["xyrtA"] if k % 2 == 0 else cs["xyrtB"]
                    lo = 0 if k <= 3 else 256
                    P.copy("act" if k % 2 == 0 else "dve", nx[:, lo:384], prs[:, lo:384], [prb], [cb])
                    Xk, Yk, Rk = nx[:, 0:128], nx[:, 128:256], nx[:, 256:384]
                    yield
                ptr_, ptrb = bank()
                ptr_b = ptr_[:].bitcast(BF16)
                P.tr(ptr_b[:, 0:128], Rk, ident_b, [cb, cbfb], [ptrb])
                P.copy("dve", cs["xyrtA"][:, 384:512], ptr_b[:, 0:128], [ptrb], [cb])
                Tk = cs["xyrtA"][:, 384:512]
                yield
                pl, plb = bank()
                P.mm(pl[:, 0:128], msk[:, 384:512], Rk, True, True, [cb], [plb])
                P.mm(pl[:, 128:256], msk[:, 256:384], Tk, True, True, [cb], [plb])
                P.copy("act", cs["ab"], pl[:, 0:256], [plb], [cb])
                yield
                pl, plb = bank()
                P.mm(pl[:, 0:128], Tk, cs["ab"][:, 0:128], True, True, [cb], [plb])
                P.mm(pl[:, 128:256], Rk, cs["ab"][:, 128:256], True, True, [cb], [plb])
                P.tt("dve", cs["r1t1"], cs["xyrtA"][:, 256:512], pl[:, 0:256], ALU.subtract, [plb], [cb])
                yield
                pl, plb = bank()
                P.mm(pl[:, 0:128], msk[:, 512:640], cs["r1t1"][:, 0:128], True, True, [cb], [plb])
                P.copy("act", cs["a2"], pl[:, 0:128], [plb], [cb])
                yield
                pl, plb = bank()
                P.mm(pl[:, 0:128], cs["r1t1"][:, 128:256], cs["a2"], True, True, [cb], [plb])
                P.tt("dve", cs["rf"], cs["r1t1"][:, 0:128], pl[:, 0:128], ALU.subtract, [plb], [cb])
                P.tt("pool", cs["kbg"], sl["ktok"][:, c, :], bexp3[:, c, gcol:gcol + 1].to_broadcast([128, 128]), ALU.mult,
                     [sl["ktok_b"], gatesb], [cb])
                yield
                pw, pwb = bank()
                P.mm(pw[:, 0:128], cs["kbg"], cs["rf"], True, True, [cb], [pwb])
                P.act(cs["nw"], pw[:, 0:128], AF.Identity, [pwb], [cb], scale=-1.0)

            def scan_gen(h, ph, c, cs, last):
                sl = slots[h % 2]
                gcol = ph * 16 + h
                cb = cs["buf"]
                S, Sb_ = sl["S"], sl["S_b"]
                Sb, Sbb = sl["Sb"], sl["Sb_b"]
                pv, pvb = bank()
                P.mm(pv[:, 0:128], cs["rf"], cs["vb"], True, False, [cb], [pvb])
                P.mm(pv[:, 0:128], cs["nw"], Sb, False, True, [cb, Sbb], [pvb])
                vn, vnb = vn_ring[ph].next()
                P.copy("act", vn, pv[:, 0:128], [pvb], [vnb])
                yield
                po, pob = bank()
                if c >= 2:
                    P.mm(po[:, 0:128], Sb, cs["qg"], True, False, [Sbb, cb], [pob])
                    P.mm(po[:, 0:128], vn, cs["xa"][:, 128:256], False, True, [vnb, cb], [pob])
                P.mm(po[:, 128:256], cs["kg"], vn, True, True, [cb, vnb], [pob])
                P.stt(S, S, gl3[:, c, gcol:gcol + 1], po[:, 128:256], ALU.mult, ALU.add, [Sb_, gatesb, pob], [Sb_])
                if not last:
                    P.copy("act", Sb, S, [Sb_], [Sbb])
                if c >= 2:
                    tk = (c - 2) * 128
                    if ph == 0:
                        P.copy("dve", sl["o1"][:, tk:tk + 128], po[:, 0:128], [pob], [sl["o1_b"]])
                    else:
                        P.tt("dve", oacc[:, tk:tk + 128], po[:, 0:128], sl["o1"][:, tk:tk + 128], ALU.add,
                             [pob, sl["o1_b"]], [oaccb])

            def run_block(phases):
                st = []
                for (h, ph) in phases:
                    order = list(range(10)) if ph == 0 else list(range(9, 1, -1))
                    st.append(dict(h=h, ph=ph, order=order, istart=0, idone=set(), sdone=0, scan=None, intras=[]))
                while True:
                    progressed = False
                    for p in st:
                        n = len(p["order"])
                        ns = len(chain_slots[p["ph"]])
                        while p["istart"] < n and p["istart"] - p["sdone"] < ns:
                            i = p["istart"]
                            g = intra_gen(p["h"], p["ph"], p["order"][i], chain_slots[p["ph"]][i % ns])
                            p["intras"].append((i, g))
                            p["istart"] += 1
                        if p["scan"] is None and p["sdone"] < n and p["sdone"] in p["idone"]:
                            i = p["sdone"]
                            if i == 0 and p["ph"] == 1:
                                exchange_finish(p["h"])
                            p["scan"] = scan_gen(p["h"], p["ph"], p["order"][i], chain_slots[p["ph"]][i % ns], i == n - 1)
                    for p in st:
                        keep = []
                        for (i, g) in p["intras"]:
                            try:
                                next(g)
                                keep.append((i, g))
                            except StopIteration:
                                p["idone"].add(i)
                            progressed = True
                        p["intras"] = keep
                        if p["scan"] is not None:
                            try:
                                next(p["scan"])
                            except StopIteration:
                                p["scan"] = None
                                p["sdone"] += 1
                            progressed = True
                    if not progressed:
                        break

            def exchange(h):
                sl = slots[h % 2]
                P.dma("sp", cin_d[h][:, :], sl["S"], [sl["S_b"]], [cinb[h]], f"xi{h}")
                P.op("pool", lambda e, h=h: e.collective_compute(
                    "AllGather", ALU.bypass, replica_groups=groups or [[0, 1], [2, 3], [4, 5], [6, 7]],
                    ins=[cin_d[h].ap().opt()], outs=[cout_d[h].ap().opt()]), [cinb[h]], [coutb[h]], key="cc", inc=1)
                P.dma("sp", Gt.rearrange("p (r n) -> p r n", r=2), cout_d[h][:, :].rearrange("(r p) n -> p r n", p=128),
                      [coutb[h]], [Gtb], f"xo{h}")

            def exchange_finish(h):
                sl = slots[h % 2]
                P.ts("dve", sl["S"], Gt[:, 0:128], vec[:, V_SEL:V_SEL + 1], None, ALU.mult, None, [Gtb, vecb], [sl["S_b"]])
                P.stt(sl["S"], Gt[:, 128:256], vec[:, V_SEL + 1:V_SEL + 2], sl["S"], ALU.mult, ALU.add,
                      [Gtb, vecb, sl["S_b"]], [sl["S_b"]])
                P.copy("act", sl["Sb"], sl["S"], [sl["S_b"]], [sl["Sb_b"]])

            def finish_head(h):
                sl = slots[h % 2]
                sq, sqb = fsq, fsqb
                P.tt("pool", sq[:, 0:1024], oacc, oacc, ALU.mult, [oaccb], [sqb])
                ss2, ssb2 = banks(2)
                for bi in range(2):
                    P.mm(ss2[:, bi * 512:(bi + 1) * 512], ones_b, sq[:, bi * 512:(bi + 1) * 512], True, True, [cbfb, sqb],
                         [ssb2[bi]])
                P.act(ss2[:, 0:1024], ss2[:, 0:1024], AF.Ln, ssb2 + [smallb], ssb2, bias=small[:, 64:65], scale=1.0 / 128)
                P.act(ss2[:, 0:1024], ss2[:, 0:1024], AF.Exp, ssb2, ssb2, scale=-0.5)
                P.stt(oacc, oacc, vec[:, V_HN:V_HN + 1], ss2[:, 0:1024], ALU.mult, ALU.mult, [oaccb, vecb] + ssb2, [oaccb])
                P.tt("pool", gat[:, h, :], oacc, sl["zAs"], ALU.mult, [oaccb, sl["zAs_b"]], [gb[h]])

            mod1_gen = emit_mod_gen(1) if with_mod1 else iter(())
            P.nw = 3 if False else 2
            load_head_w(0)
            for h in range(nheads + 1):
                if h < nheads:
                    prep_head(h)
                    P.barrier()
                if h >= 1:
                    exchange(h - 1)
                next(mod1_gen, None)
                if h + 1 < nheads:
                    load_head_w(h + 1)
                ph_list = ([(h, 0)] if h < nheads else []) + ([(h - 1, 1)] if h >= 1 else [])
                P.bank_lo = 3
                run_block(ph_list)
                P.bank_lo = 0
                if h >= 1:
                    finish_head(h - 1)
                next(mod1_gen, None)
                P.barrier()
            for _ in mod1_gen:
                pass
            P.barrier()
            if stop_after == "gdn":
                return
            emit_wout_pass(ewout_d, 0, gat, gb, 0, dd, first=True, ntok=1024, tok0=0)
            P.barrier()
            emit_mixer_B(dd, hT, hTb, gat, gb)
            emit_wout_pass(ewout_d, 2048, gat, gb, 0, dd, first=False, ntok=1024, tok0=0)
            P.barrier()
            stg_ring2 = Ring([G2_t[:, i * 2048:(i + 1) * 2048] for i in range(4)], "xstg2")
            for t in range(8):
                stg, stgb = stg_ring2.next()
                P.dma("sp", stg, xs_d[t * 128:(t + 1) * 128, :], (), [stgb], f"xl{t % 4}")
                for d4 in range(4):
                    ps, pb = bank()
                    for j in range(4):
                        dc = d4 * 4 + j
                        P.tr(ps[:, j * 128:(j + 1) * 128], stg[:, dc * 128:(dc + 1) * 128], ident_f, [stgb, cstb], [pb])
                    for j in range(4):
                        dc = d4 * 4 + j
                        dst = x1T[:, dc, t * 128:(t + 1) * 128]
                        P.tt("dve", dst, ps[:, j * 128:(j + 1) * 128], dst, ALU.add, [pb, x1b[dc][t // 4]], [x1b[dc][t // 4]])
            P.barrier()
            P.nw = 3
            P.nslot = 0

        def emit_mixer_B(dd, hT, hTb, gat, gb):
            BASE = 6144 + 2048 + 64
            for cg in range(16):
                wap, wb = load_w(ewmb_d, cg * D, 16, 0, 512)
                w0 = vec[:, V_CB + cg * 3:V_CB + cg * 3 + 1]
                w1 = vec[:, V_CB + cg * 3 + 1:V_CB + cg * 3 + 2]
                w2 = vec[:, V_CB + cg * 3 + 2:V_CB + cg * 3 + 3]
                for half in range(2):
                    n0 = 256 + half * 512
                    pp = []
                    for j in range(4):
                        ps, pb = bank()
                        for dc in range(16):
                            P.mm(ps[:, :], wap[:, dc, j * 128:(j + 1) * 128], hT[:, dc, n0:n0 + 512], dc == 0, dc == 15,
                                 [wb, hTb], [pb])
                        pp.append((ps, pb))
                    (pbg, pbgb), (pcg, pcgb), (phb, phbb), (pzb, pzbb) = pp
                    cgs, cgsb = mb_ring.next()
                    P.copy("act", cgs, pcg[:, :], [pcgb], [cgsb])
                    t1, t1b = mb_ring.next()
                    P.tt("dve", t1, phb[:, :], cgs, ALU.mult, [phbb, cgsb], [t1b])
                    cv, cvb = mb_ring.next()
                    P.act(cv, t1, AF.Identity, [t1b, vecb], [cvb], scale=w1)
                    cv3 = cv.rearrange("p (r t) -> p r t", t=64)
                    t13 = t1.rearrange("p (r t) -> p r t", t=64)
                    P.stt(cv3[:, :, 1:64], t13[:, :, 0:63], w0, cv3[:, :, 1:64], ALU.mult, ALU.add, [t1b, vecb, cvb], [cvb])
                    P.stt(cv3[:, :, 0:63], t13[:, :, 1:64], w2, cv3[:, :, 0:63], ALU.mult, ALU.add, [t1b, vecb, cvb], [cvb])
                    sz, szb = mb_ring.next()
                    P.act(sz, pzb[:, :], AF.Silu, [pzbb], [szb])
                    P.tt("dve", cv, pbg[:, :], cv, ALU.mult, [pbgb, cvb], [cvb])
                    P.tt("pool", gat[:, cg, half * 512:(half + 1) * 512], cv, sz, ALU.mult, [cvb, szb], [gb[cg]])

        if mode == "L0":
            emit_L0()
            for dc in range(16):
                P.dma("sp", out_d[:, dc * NT:(dc + 1) * NT], x1T[:, dc, :], [x1b[dc][0], x1b[dc][1]], [], f"out{dc % 4}")
        if mode == "full":
            emit_L0(True)
            pre = emit_L1_prelude(True)
            for half in range(2):
                P.barrier()
                emit_L1_half(half, *pre)
            for half in range(2):
                P.barrier()
                emit_final(half)
        if mode == "L1":
            x1all = Buf("x1all")
            for dc in range(16):
                P.dma("sp", x1T[:, dc, :], x1in_d[:, dc * NT:(dc + 1) * NT], (), [x1b[dc][0], x1b[dc][1]], "cld")
            pre = emit_L1_prelude()
            for half in range(2):
                P.barrier()
                emit_L1_half(half, *pre)
            for half in range(2):
                P.barrier()
                emit_final(half)

        with nc.Block() as block:
            P.flush(block)
    return nc


def make_consts():
    c = np.zeros((128, NCST), np.float32)
    idx = np.arange(128)
    c[:, C_ID:C_ID + 128] = np.eye(128)
    c[:, C_TA:C_TA + 128] = (idx[:, None] <= idx[None, :])
    c[:, C_TD:C_TD + 128] = (idx[:, None] >= idx[None, :])
    P_, F_ = idx[:, None], idx[None, :]
    for base, asc in ((C_MA, True), (C_MD, False)):
        if asc:
            m1 = F_ > P_; m2 = F_ >= P_; m3 = P_ > F_
        else:
            m1 = F_ < P_; m2 = F_ <= P_; m3 = P_ < F_
        c[:, base:base + 128] = np.where(m1, 0.0, -BIG)
        c[:, base + 128:base + 256] = np.where(m2, 0.0, -BIG)
        c[:, base + 256:base + 384] = np.where(m3, 0.0, BIG)
    for k, r in enumerate((1, 2, 4, 8)):
        Dm = np.zeros((128, 128), np.float64)
        for i in range(128):
            row = i // 64
            lo = max(i - r, row * 64)
            hi = min(i + r + 1, row * 64 + 64)
            Dm[i, lo:hi] = 1.0 / (hi - lo)
            Dm[i, i] -= 1.0
        c[:, C_BAND + 128 * k:C_BAND + 128 * (k + 1)] = Dm.T
    c[:, C_ONES:C_ONES + 128] = 1.0
    pb, fb = P_ // 32, F_ // 32
    m1_lo = ((pb == 1) & (fb == 0)) | ((pb == 3) & (fb == 2))
    m2_lo = (P_ >= 64) & (F_ < 64)
    for i, m in enumerate((pb == fb, m1_lo, m1_lo.T, m2_lo, m2_lo.T)):
        c[:, C_BM + 128 * i:C_BM + 128 * (i + 1)] = m
    return c


def fm(v, n):
    return np.ascontiguousarray(np.asarray(v, np.float32).reshape(n, 128).T)


def make_vec(inp, b, s):
    v = np.zeros((128, NV), np.float32)
    v[:, V_C:V_C + 16] = fm(inp["c"][b], 16)
    v[:, V_CC:V_CC + 16] = fm(inp["c_ctx"], 16)
    v[:, V_AB0:V_AB0 + 48] = fm(inp["ada_b"][0], 48)
    v[:, V_AB1:V_AB1 + 48] = fm(inp["ada_b"][1], 48)
    v[:, V_NW0:V_NW0 + 16] = fm(inp["norm_w"][0], 16)
    v[:, V_NW1:V_NW1 + 16] = fm(inp["norm_w"][1], 16)
    v[:, V_LNW:V_LNW + 16] = fm(inp["o_ln_w"][0], 16)
    v[:, V_LNB:V_LNB + 16] = fm(inp["o_ln_b"][0], 16)
    v[:, V_PS:V_PS + 16] = fm(inp["o_pool_scale"][0], 16)
    v[:, V_FNW:V_FNW + 16] = fm(inp["final_norm_w"], 16)
    cq = np.asarray(inp["e_conv_qkv"][0], np.float32)
    cb = np.asarray(inp["e_conv_b"][0], np.float32)
    if s == 1:
        cq = cq[::-1]
        cb = cb[::-1]
    v[:, V_CQ:V_CQ + 144] = np.stack([fm(cq[t], 48) for t in range(3)], axis=2).reshape(128, 144)
    v[:, V_CB:V_CB + 48] = np.stack([fm(cb[t], 16) for t in range(3)], axis=2).reshape(128, 48)
    v[:, V_HN] = np.asarray(inp["e_head_norm"][0], np.float32)
    v[:, V_SEL] = 1.0 if s == 1 else 0.0
    v[:, V_SEL + 1] = 1.0 if s == 0 else 0.0
    dirs = (0, 1) if s == 0 else (1, 0)
    dtb = np.asarray(inp["e_dt_bias"][0], np.float32)
    alog = np.asarray(inp["e_a_log"][0], np.float32)
    v[:, V_DTB:V_DTB + 32] = np.concatenate([dtb[dirs[0]], dtb[dirs[1]]])[None, :]
    v[:, V_ALOG:V_ALOG + 32] = np.concatenate([alog[dirs[0]], alog[dirs[1]]])[None, :]
    return v


def common_maps(inp):
    f = lambda a: np.ascontiguousarray(np.asarray(a, np.float32))
    return {
        "cst": make_consts(),
        "ada_w0": f(inp["ada_w"][0]), "ada_w1": f(inp["ada_w"][1]),
        "o_w_in": f(inp["o_w_in"][0]),
        "o_pool_w": f(np.asarray(inp["o_pool_w"][0]).reshape(2048, 512)),
        "o_w_out": f(inp["o_w_out"][0]),
    }


def core_maps_L1(inp, b, s):
    ws = np.asarray(inp["o_w_s"][0], np.float32)
    bs = np.asarray(inp["o_b_s"][0], np.float32)
    if s == 1:
        ws = ws[:, ::-1, ::-1]
        bs = bs[:, ::-1]
    return {
        "vec": make_vec(inp, b, s),
        "ws": np.ascontiguousarray(ws.transpose(1, 0, 2).reshape(128, 2048)),
        "bsb": np.ascontiguousarray(np.broadcast_to(bs.reshape(1, 2048), (128, 2048))),
    }


_NC_CACHE = {}


def kernel(**inputs):
    inp = {k: np.asarray(v) for k, v in inputs.items()}
    if "full" not in _NC_CACHE:
        _NC_CACHE["full"] = build("full")
    nc = _NC_CACHE["full"]
    com = common_maps(inp)
    com.update(common_maps_L0(inp))
    maps = []
    for core in range(8):
        b, s = core // 2, core % 2
        m = dict(com)
        m.update(core_maps_L1(inp, b, s))
        m.update(core_maps_L0(inp, b, s))
        maps.append(m)
    res = run_bass_kernel_spmd(nc, maps, core_ids=list(range(8)))
    out = np.empty((4, 2048, D), np.float32)
    for core in range(8):
        b, s = core // 2, core % 2
        o = np.asarray(res.results[core]["out"], np.float32)
        if s == 1:
            o = o[::-1]
        out[b, s * NT:(s + 1) * NT] = o
    return out


def common_maps_L0(inp):
    f = lambda a: np.ascontiguousarray(np.asarray(a, np.float32))
    w = np.asarray(inp["e_w_in"][0], np.float32)
    hd = np.empty((16, D, 512), np.float32)
    mb = np.empty((16, D, 512), np.float32)
    base = 6144 + 2048 + 64
    for h in range(16):
        for j, c0 in enumerate((h * 128, 2048 + h * 128, 4096 + h * 128, 6144 + h * 128)):
            hd[h, :, j * 128:(j + 1) * 128] = w[:, c0:c0 + 128]
        for j in range(4):
            c0 = base + j * 2048 + h * 128
            mb[h, :, j * 128:(j + 1) * 128] = w[:, c0:c0 + 128]
    return {"e_w_hd": hd.reshape(16 * D, 512), "e_w_mb": mb.reshape(16 * D, 512), "e_w_out": f(inp["e_w_out"][0])}


def core_maps_L0(inp, b, s):
    xs = np.asarray(inp["x"][b, s * NT:(s + 1) * NT], np.float32)
    cx = np.asarray(inp["ctx"][b], np.float32)
    if s == 1:
        xs = xs[::-1]
        cx = cx[::-1]
    d1, d2 = (0, 1) if s == 0 else (1, 0)
    base = 8192
    cols = np.concatenate([np.arange(base + d1 * 16, base + d1 * 16 + 16), np.arange(base + d2 * 16, base + d2 * 16 + 16),
                           np.arange(base + 32 + d1 * 16, base + 32 + d1 * 16 + 16),
                           np.arange(base + 32 + d2 * 16, base + 32 + d2 * 16 + 16)])
    wab = np.asarray(inp["e_w_in"][0], np.float32)[:, cols]
    return {"xs": np.ascontiguousarray(xs), "ctxs": np.ascontiguousarray(cx), "w_ab": np.ascontiguousarray(wab)}
```

```python
import numpy as np
from contextlib import ExitStack
import concourse.bass as bass
import concourse.mybir as mybir
from concourse.bass_utils import run_bass_kernel_spmd

F32 = mybir.dt.float32
BF16 = mybir.dt.bfloat16
AF = mybir.ActivationFunctionType
ALU = mybir.AluOpType

D = 2048
NT = 1024
NCTX = 256
EPS = 1e-6
BIG = 30000.0
EVEN_COLS = 16448
ODD_COLS = 10240
WSPLIT = 2

C_ID, C_TA, C_TD, C_MA, C_MD, C_BAND, C_ONES, C_BM, NCST = 0, 128, 256, 384, 768, 1152, 1664, 1792, 2432
V_C, V_CC, V_AB0, V_AB1, V_NW0, V_NW1, V_LNW, V_LNB, V_PS, V_FNW = 0, 16, 32, 80, 128, 144, 160, 176, 192, 208
V_CQ, V_CB, V_HN, V_SEL, V_DTB, V_ALOG, NV = 224, 368, 416, 417, 419, 451, 512


class Buf:
    __slots__ = ("name", "w", "r", "excl")

    def __init__(self, name, excl=False):
        self.name = name
        self.w = None
        self.r = []
        self.excl = excl


class Prog:
    ENGS = ("pe", "act", "dve", "pool", "sp")

    def __init__(self, nc, es):
        self.nc = nc
        self.es = es
        self.q = {k: [] for k in self.ENGS}
        self.sems = {}
        self.cnt = {}
        self.known = {k: {} for k in self.ENGS}
        self.nbank = 0
        self.nslot = 0
        self.nw = 3
        self.rings = {}

    def _sem(self, key):
        if key not in self.sems:
            self.sems[key] = self.es.enter_context(self.nc.semaphore("s_" + key))
            self.cnt[key] = 0
        return self.sems[key]

    def op(self, eng, fn, R=(), W=(), key=None, inc=1):
        deps = []
        for b in R:
            if b.w is not None:
                deps.append(b.w)
            if b.excl:
                deps.extend(b.r)
        for b in W:
            if b.w is not None:
                deps.append(b.w)
            deps.extend(b.r)
        waits = {}
        kn = self.known[eng]
        for (k, v) in deps:
            if k == "pe" and eng == "pe" and key is None:
                continue
            if kn.get(k, 0) >= v:
                continue
            if waits.get(k, 0) < v:
                waits[k] = v
        for k, v in waits.items():
            kn[k] = v
        if key is None:
            key = eng
        self._sem(key)
        self.cnt[key] += inc
        t = (key, self.cnt[key])
        self.q[eng].append((tuple(waits.items()), fn, key, inc))
        for b in R:
            b.r.append(t)
        for b in W:
            b.w = t
            b.r = []
        return t

    def barrier(self):
        snap = {k: v for k, v in self.cnt.items() if v > 0}
        for eng in self.ENGS:
            kn = self.known[eng]
            waits = tuple((k, v) for k, v in snap.items() if kn.get(k, 0) < v)
            for k, v in waits:
                kn[k] = v
            self.q[eng].append((waits, None, None, 0))

    def mm(self, out, lhsT, rhs, start, stop, R, W):
        return self.op("pe", lambda e: e.matmul(out, lhsT, rhs, start=start, stop=stop, skip_group_check=True), R, W)

    def tr(self, out, in_, ident, R, W):
        return self.op("pe", lambda e: e.transpose(out, in_, ident), R, W)

    def act(self, out, in_, func, R, W, bias=None, scale=None, accum_out=None):
        kw = {}
        if bias is not None:
            kw["bias"] = bias
        if scale is not None:
            kw["scale"] = scale
        if accum_out is not None:
            kw["accum_out"] = accum_out
        return self.op("act", lambda e: e.activation(out=out, in_=in_, func=func, **kw), R, W)

    def tt(self, eng, out, in0, in1, op, R, W):
        return self.op(eng, lambda e: e.tensor_tensor(out=out, in0=in0, in1=in1, op=op), R, W)

    def ts(self, eng, out, in0, s1, s2, op0, op1, R, W, accum_out=None):
        if accum_out is not None:
            return self.op(eng, lambda e: e.tensor_scalar(out=out, in0=in0, scalar1=s1, scalar2=s2, op0=op0, op1=op1,
                                                          accum_out=accum_out), R, W)
        if op1 is None:
            return self.op(eng, lambda e: e.tensor_scalar(out=out, in0=in0, scalar1=s1, scalar2=None, op0=op0), R, W)
        return self.op(eng, lambda e: e.tensor_scalar(out=out, in0=in0, scalar1=s1, scalar2=s2, op0=op0, op1=op1), R, W)

    def stt(self, out, in0, scalar, in1, op0, op1, R, W, accum_out=None):
        if accum_out is not None:
            return self.op("dve", lambda e: e.scalar_tensor_tensor(out=out, in0=in0, scalar=scalar, in1=in1, op0=op0,
                                                                   op1=op1, accum_out=accum_out), R, W)
        return self.op("dve", lambda e: e.scalar_tensor_tensor(out=out, in0=in0, scalar=scalar, in1=in1, op0=op0,
                                                               op1=op1), R, W)

    def copy(self, eng, out, in_, R, W):
        if eng == "act":
            return self.op("act", lambda e: e.copy(out=out, in_=in_), R, W)
        return self.op(eng, lambda e: e.tensor_copy(out=out, in_=in_), R, W)

    def recip(self, out, in_, R, W):
        return self.op("dve", lambda e: e.reciprocal(out=out, in_=in_), R, W)

    def memset(self, eng, ap, val, W):
        return self.op(eng, lambda e: e.memset(ap, val), (), W)

    def dma(self, eng, out, in_, R, W, key, slow=False):
        if key == "cld":
            self.ncld = getattr(self, "ncld", 0) + 1
            key = f"cld{self.ncld}"
        if slow:
            return self.op(eng, lambda e: e.dma_start(out=out, in_=in_, allow_slow_non_contiguous=True), R, W, key=key, inc=16)
        return self.op(eng, lambda e: e.dma_start(out=out, in_=in_), R, W, key=key, inc=16)

    def flush(self, block):
        engs = {"pe": block.tensor, "act": block.scalar, "dve": block.vector, "pool": block.gpsimd, "sp": block.sync}
        for name in self.ENGS:
            items = self.q[name]
            sems = self.sems
            final = []
            if name == "sp":
                final = [(k, self.cnt[k]) for k in self.cnt if k.startswith("out")]

            def body(e, items=items, final=final):
                for waits, fn, key, inc in items:
                    for k, v in waits:
                        e.wait_ge(sems[k], v)
                    if fn is not None:
                        fn(e).then_inc(sems[key], inc)
                for k, v in final:
                    e.wait_ge(sems[k], v)

            engs[name](body)


class Ring:
    def __init__(self, aps, name):
        self.aps = aps
        self.bufs = [Buf(f"{name}{i}") for i in range(len(aps))]
        self.i = 0

    def next(self):
        k = self.i % len(self.aps)
        self.i += 1
        return self.aps[k], self.bufs[k]


def build(mode="full", nheads=16, groups=None, stop_after=None):
    nc = bass.Bass("TRN2", target_bir_lowering=False)
    dr = {}

    def din(name, shape):
        dr[name] = nc.dram_tensor(name, list(shape), F32, kind="ExternalInput").ap()
        return dr[name]

    vec_d = din("vec", [128, NV])
    cst_d = din("cst", [128, NCST])
    adaw_d = [din("ada_w0", [D, 3 * D]), din("ada_w1", [D, 3 * D])]
    owin_d = din("o_w_in", [D, ODD_COLS])
    ws_d = din("ws", [128, 2048])
    bsb_d = din("bsb", [128, 2048])
    opw_d = din("o_pool_w", [2048, 512])
    owout_d = din("o_w_out", [2 * D, D])
    if mode in ("full", "L0"):
        xs_d = din("xs", [NT, D])
        ctx_d = din("ctxs", [NCTX, D])
        ewhd_d = din("e_w_hd", [16 * D, 512])
        ewmb_d = din("e_w_mb", [16 * D, 512])
        wab_d = din("w_ab", [D, 64])
        ewout_d = din("e_w_out", [2 * D, D])
    if mode == "L1":
        x1in_d = din("x1T_in", [128, 16 * NT])
    if mode == "L0":
        out_d = nc.dram_tensor("out", [128, 16 * NT], F32, kind="ExternalOutput").ap()
    else:
        out_d = nc.dram_tensor("out", [NT, D], F32, kind="ExternalOutput").ap()

    with ExitStack() as es:
        P = Prog(nc, es)

        def sb(name, shape, dt):
            return es.enter_context(nc.sbuf_tensor("sb_" + name, list(shape), dt))

        x1T_t = sb("x1T", [128, 16 * NT], F32)
        G1_t = sb("G1", [128, 8192], F32)
        G2_t = sb("G2", [128, 10240], F32)
        W_t = [sb(f"W{i}", [128, 16 * 512], BF16) for i in range(3)]
        cst = sb("cst", [128, C_BAND], F32)
        vec = sb("vec", [128, NV], F32)
        cbf = sb("cbf", [128, 12 * 128], BF16)
        mod_t = sb("mod", [128, 2 * 96], F32)
        der_t = sb("der", [128, 2 * 5 * 16], F32)
        small = sb("small", [128, 256], F32)
        scin = sb("scin", [128, 32], BF16)
        aux = sb("aux", [128, 2048], F32)
        mb_ring = Ring([aux[:, i * 512:(i + 1) * 512] for i in range(4)], "mb")
        ps_all = es.enter_context(nc.psum_tensor("ps_all", [128, 4096], F32))
        psum = [ps_all[:, i * 512:(i + 1) * 512] for i in range(8)]
        pbuf = [Buf(f"ps{i}", excl=True) for i in range(8)]

        x1T = x1T_t[:].rearrange("p (c t) -> p c t", c=16)
        x1b = [[Buf(f"x1_{dc}_{h}") for h in range(2)] for dc in range(16)]
        Wap = [w[:].rearrange("p (c n) -> p c n", c=16) for w in W_t]
        Wbuf = [Buf(f"W{i}") for i in range(3)]
        cstb = Buf("cst")
        vecb = Buf("vec")
        cbfb = Buf("cbf")
        modb = Buf("mod")
        derb = Buf("der")
        smallb = Buf("small")
        ident_f = cst[:, C_ID:C_ID + 128]
        ident_b = cbf[:, 0:128]
        nident_b = cbf[:, 128:256]
        ones_b = cbf[:, 256:384]
        band_b = [cbf[:, 384 + 128 * i: 512 + 128 * i] for i in range(4)]
        bm_b = [cbf[:, 896 + 128 * i: 1024 + 128 * i] for i in range(5)]

        P.bank_lo = 0

        def bank():
            n = 8 - P.bank_lo
            i = P.bank_lo + (P.nbank % n)
            P.nbank += 1
            return psum[i], pbuf[i]

        def banks(n):
            i = P.nbank % 8
            if i + n > 8:
                P.nbank += 8 - i
                i = 0
            P.nbank += n
            return ps_all[:, i * 512:(i + n) * 512], [pbuf[i + k] for k in range(n)]

        def wslot():
            i = P.nslot % P.nw
            P.nslot += 1
            return Wap[i], Wbuf[i], f"w{i}"

        def load_w(src2d, r0, nrow_chunks, c0, ncols):
            ap, b, key = wslot()
            src = src2d[r0:r0 + 128 * nrow_chunks, c0:c0 + ncols].rearrange("(c p) n -> p c n", p=128)
            nsp = WSPLIT if nrow_chunks % WSPLIT == 0 else 1
            step = nrow_chunks // nsp
            t = None
            for i in range(nsp):
                t = P.dma("pool", ap[:, i * step:(i + 1) * step, 0:ncols], src[:, i * step:(i + 1) * step, :], R=(),
                          W=[b] if i == 0 else [], key=key)
            b.w = t
            return ap, b

        P.dma("sp", cst[:], cst_d[:, 0:C_BAND], (), [cstb], "cld")
        ctmp = G1_t[:, 0:NCST - C_BAND]
        ctmpb = Buf("ctmp")
        P.dma("sp", ctmp, cst_d[:, C_BAND:NCST], (), [ctmpb], "cld")
        P.dma("sp", vec[:], vec_d[:, :], (), [vecb], "cld")
        P.copy("dve", ident_b, ident_f, [cstb], [cbfb])
        P.ts("dve", nident_b, ident_f, -1.0, None, ALU.mult, None, [cstb], [cbfb])
        P.copy("dve", ones_b, ctmp[:, C_ONES - C_BAND:C_ONES - C_BAND + 128], [ctmpb], [cbfb])
        for i in range(4):
            P.copy("dve", band_b[i], ctmp[:, 128 * i:128 * (i + 1)], [ctmpb], [cbfb])
        for i in range(5):
            P.copy("dve", bm_b[i], ctmp[:, C_BM - C_BAND + 128 * i:C_BM - C_BAND + 128 * (i + 1)], [ctmpb], [cbfb])
        P.barrier()

        def emit_mod_gen(l):
            sc3 = scin[:].rearrange("p (c k) -> p c k", k=2)
            if not getattr(P, "sc_done", False):
                P.sc_done = True
                P.act(sc3[:, :, 0], vec[:, V_C:V_C + 16], AF.Silu, [vecb], [smallb])
                P.act(sc3[:, :, 1], vec[:, V_CC:V_CC + 16], AF.Silu, [vecb], [smallb])
            m3 = mod_t[:, l * 96:(l + 1) * 96].rearrange("p (g k) -> p g k", k=2)
            vab = V_AB0 if l == 0 else V_AB1
            for cb in range(12):
                wap, wb = load_w(adaw_d[l], 0, 16, cb * 512, 512)
                if l == 1:
                    yield
                mps, mpb = bank()
                for j in range(4):
                    for dc in range(16):
                        P.mm(mps[:, 2 * j:2 * j + 2], wap[:, dc, j * 128:(j + 1) * 128], sc3[:, dc, :],
                             dc == 0, dc == 15, [wb, smallb], [mpb])
                mp3 = mps[:, 0:8].rearrange("p (g k) -> p g k", k=2)
                for k in range(2):
                    P.tt("dve", m3[:, cb * 4:cb * 4 + 4, k], mp3[:, :, k], vec[:, vab + cb * 4:vab + cb * 4 + 4], ALU.add,
                         [mpb, vecb], [modb])
                yield
            vnw = V_NW0 if l == 0 else V_NW1
            dd = der_t[:, l * 80:(l + 1) * 80]
            P.stt(dd[:, 0:16], m3[:, 16:32, 0], 1.0, vec[:, vnw:vnw + 16], ALU.add, ALU.mult, [modb, vecb], [derb])
            P.copy("dve", dd[:, 16:32], m3[:, 0:16, 0], [modb], [derb])
            P.copy("dve", dd[:, 32:48], m3[:, 32:48, 0], [modb], [derb])
            P.stt(dd[:, 48:64], m3[:, 16:32, 1], 1.0, vec[:, vnw:vnw + 16], ALU.add, ALU.mult, [modb, vecb], [derb])
            P.copy("dve", dd[:, 64:80], m3[:, 0:16, 1], [modb], [derb])

        def emit_mod(l):
            for _ in emit_mod_gen(l):
                pass
            return der_t[:, l * 80:(l + 1) * 80]

        def emit_L1_prelude(mod_done=False):
            dd = der_t[:, 80:160] if mod_done else emit_mod(1)
            wsT_t = aux[:, 0:1024].bitcast(BF16)
            bias2_t = aux[:, 1024:2048].bitcast(BF16)
            wsTb = Buf("wsT")
            bias2b = Buf("bias2")
            wap, wb, key = wslot()
            ws_sb = wap[:, 0:4, :].rearrange("p c n -> p (c n)")
            P.dma("pool", ws_sb, ws_d[:, :], (), [wb], key)
            for q4 in range(4):
                ps, pb = bank()
                psb = ps[:].bitcast(BF16)
                for j in range(4):
                    g = q4 * 4 + j
                    P.tr(psb[:, j * 128:(j + 1) * 128], ws_sb[:, g * 128:(g + 1) * 128], ident_b, [wb, cbfb], [pb])
                P.copy("dve", wsT_t[:, q4 * 512:(q4 + 1) * 512], psb[:, 0:512], [pb], [wsTb])
            bs_sb = G1_t[:, 0:2048]
            g1b = Buf("g1tmp")
            P.dma("sp", bs_sb, bsb_d[:, :], (), [g1b], "cld")
            for q4 in range(4):
                ps, pb = bank()
                P.mm(ps[:, :], ones_b, wsT_t[:, q4 * 512:(q4 + 1) * 512], True, True, [cbfb, wsTb], [pb])
                for j in range(4):
                    g = q4 * 4 + j
                    P.stt(bias2_t[:, g * 128:(g + 1) * 128], ps[:, j * 128:(j + 1) * 128],
                          vec[:, V_LNB + g:V_LNB + g + 1], bs_sb[:, g * 128:(g + 1) * 128], ALU.mult, ALU.add,
                          [pb, vecb, g1b], [bias2b])
            return dd, wsT_t, wsTb, bias2_t, bias2b

        def emit_rstd_bc(src_of_dc, srcbufs_of_dc, ntok, tmp_ring, out_ap, outb, inv_n):
            nb = (ntok + 511) // 512
            for bi in range(nb):
                n0 = bi * 512
                n1 = min(ntok, n0 + 512)
                ps, pb = bank()
                for dc in range(16):
                    sq, sqb = tmp_ring.next()
                    P.act(sq[:, 0:n1 - n0], src_of_dc(dc)[:, n0:n1], AF.Square, srcbufs_of_dc(dc), [sqb])
                    P.mm(ps[:, 0:n1 - n0], ones_b, sq[:, 0:n1 - n0], dc == 0, dc == 15, [cbfb, sqb], [pb])
                P.ts("dve", out_ap[:, n0:n1], ps[:, 0:n1 - n0], inv_n, EPS, ALU.mult, ALU.add, [pb], [outb])
                P.act(out_ap[:, n0:n1], out_ap[:, n0:n1], AF.Sqrt, [outb], [outb])
                P.recip(out_ap[:, n0:n1], out_ap[:, n0:n1], [outb], [outb])

        def emit_L1_half(half, dd, wsT_t, wsTb, bias2_t, bias2b):
            T0 = half * 512
            h1T = G2_t[:, 0:4096].bitcast(BF16).rearrange("p (c t) -> p c t", c=16)
            vtok = G2_t[:, 4096:8192].bitcast(BF16).rearrange("p (t n) -> p t n", t=4)
            gat = G1_t[:, 0:4096].bitcast(BF16).rearrange("p (c t) -> p c t", c=16)
            tmpA = G1_t[:, 4096:8192]
            rs = G2_t[:, 8192:8704]
            misc = G2_t[:, 8704:10240]
            hb = [Buf(f"h1_{dc}") for dc in range(16)]
            vb_ = [Buf(f"vt_{t}") for t in range(4)]
            gb = [Buf(f"gat_{c}") for c in range(16)]
            rsb = Buf("rs")
            miscb = Buf("misc")
            sq_ring = Ring([tmpA[:, i * 256:(i + 1) * 256].bitcast(BF16) for i in range(3)], "sq")
            f_ring = Ring([tmpA[:, 768 + i * 512:768 + (i + 1) * 512] for i in range(6)], "f")
            emit_rstd_bc(lambda dc: x1T[:, dc, T0:T0 + 512], lambda dc: [x1b[dc][half]], 512, sq_ring, rs, rsb, 1.0 / D)
            for dc in range(16):
                t, tb = f_ring.next()
                P.tt("dve", t, x1T[:, dc, T0:T0 + 512], rs, ALU.mult, [x1b[dc][half], rsb], [tb])
                P.act(h1T[:, dc, :], t, AF.Identity, [tb, derb], [hb[dc]], bias=dd[:, 16 + dc:17 + dc],
                      scale=dd[:, dc:dc + 1])
            st = misc[:, 0:64]
            stb = [Buf(f"st{i}") for i in range(32)]
            for vbk in range(4):
                wap, wb = load_w(owin_d, 0, 16, 2048 + vbk * 512, 512)
                for t4 in range(4):
                    ps, pb = bank()
                    for dc in range(16):
                        P.mm(ps[:, :], h1T[:, dc, t4 * 128:(t4 + 1) * 128], wap[:, dc, :], dc == 0, dc == 15,
                             [hb[dc], wb], [pb])
                    vblk = vtok[:, t4, vbk * 512:(vbk + 1) * 512]
                    P.act(vblk, ps[:, :], AF.Gelu_apprx_tanh, [pb], [vb_[t4]])
                    j1, j1b = f_ring.next()
                    P.act(j1, vblk, AF.Square, [vb_[t4]], [j1b, stb[16 + t4 * 4 + vbk]],
                          accum_out=st[:, 16 + t4 * 4 + vbk:17 + t4 * 4 + vbk])
                    j2, j2b = f_ring.next()
                    P.ts("dve", j2, vblk, 1.0, 0.0, ALU.mult, ALU.add, [vb_[t4]], [j2b, stb[t4 * 4 + vbk]],
                         accum_out=st[:, t4 * 4 + vbk:t4 * 4 + vbk + 1])
            st3 = st[:, 0:32].rearrange("p (a t v) -> p a t v", a=2, v=4)
            red = misc[:, 64:72].rearrange("p (a t) -> p a t", a=2)
            P.tt("dve", red, st3[:, :, :, 0], st3[:, :, :, 1], ALU.add, stb, [miscb])
            P.tt("dve", red, red, st3[:, :, :, 2], ALU.add, [miscb], [miscb])
            P.tt("dve", red, red, st3[:, :, :, 3], ALU.add, [miscb], [miscb])
            mu = misc[:, 72:76]
            var = misc[:, 76:80]
            rstd = misc[:, 80:84]
            nmr = misc[:, 84:88]
            P.ts("dve", mu, red[:, 0, :], 1.0 / 2048, None, ALU.mult, None, [miscb], [miscb])
            P.ts("dve", var, red[:, 1, :], 1.0 / 2048, EPS, ALU.mult, ALU.add, [miscb], [miscb])
            P.tt("dve", nmr, mu, mu, ALU.mult, [miscb], [miscb])
            P.tt("dve", var, var, nmr, ALU.subtract, [miscb], [miscb])
            P.act(var, var, AF.Sqrt, [miscb], [miscb])
            P.recip(rstd, var, [miscb], [miscb])
            P.stt(nmr, mu, -1.0, rstd, ALU.mult, ALU.mult, [miscb], [miscb])
            for t4 in range(4):
                P.ts("dve", vtok[:, t4, :], vtok[:, t4, :], rstd[:, t4:t4 + 1], nmr[:, t4:t4 + 1], ALU.mult, ALU.add,
                     [vb_[t4], miscb], [vb_[t4]])
            for g4 in range(4):
                wu, wub = load_w(owin_d, 0, 16, g4 * 512, 512)
                wz, wzb = load_w(owin_d, 0, 16, 4096 + g4 * 512, 512)
                for j in range(4):
                    g = g4 * 4 + j
                    pu, pub = bank()
                    for dc in range(16):
                        P.mm(pu[:, :], wu[:, dc, j * 128:(j + 1) * 128], h1T[:, dc, :], dc == 0, dc == 15,
                             [wub, hb[dc]], [pub])
                    pz, pzb = bank()
                    for dc in range(16):
                        P.mm(pz[:, :], wz[:, dc, j * 128:(j + 1) * 128], h1T[:, dc, :], dc == 0, dc == 15,
                             [wzb, hb[dc]], [pzb])
                    pss, pssb = bank()
                    for t4 in range(4):
                        P.mm(pss[:, t4 * 128:(t4 + 1) * 128], vtok[:, t4, g * 128:(g + 1) * 128],
                             wsT_t[:, g * 128:(g + 1) * 128], True, True, [vb_[t4], wsTb], [pssb])
                    gu, gub = f_ring.next()
                    P.act(gu, pu[:, :], AF.Gelu_apprx_tanh, [pub], [gub])
                    sz, szb = f_ring.next()
                    P.act(sz, pz[:, :], AF.Silu, [pzb], [szb])
                    s2, s2b = f_ring.next()
                    b2 = bias2_t[:, g * 128:(g + 1) * 128]
                    for t4 in range(4):
                        P.stt(s2[:, t4 * 128:(t4 + 1) * 128], pss[:, t4 * 128:(t4 + 1) * 128],
                              vec[:, V_LNW + g:V_LNW + g + 1], b2, ALU.mult, ALU.add, [pssb, vecb, bias2b], [s2b])
                    P.tt("pool", gu, gu, sz, ALU.mult, [gub, szb], [gub])
                    P.tt("dve", gat[:, g, :], s2, gu, ALU.mult, [s2b, gub], [gb[g]])
            emit_wout_pass(owout_d, 0, gat, gb, half, dd)
            pt = G2_t[:, 4096:5120].bitcast(BF16).rearrange("p (t n) -> p t n", t=4)
            dfT = G2_t[:, 5120:6144].bitcast(BF16).rearrange("p (c t) -> p c t", c=4)
            ptb = [Buf(f"pt_{t}") for t in range(4)]
            dfb = [Buf(f"df_{c}") for c in range(4)]
            for pg in range(4):
                wp, wpb = load_w(owin_d, 0, 16, 6144 + pg * 512, 512)
                for t4 in range(4):
                    ps, pb = bank()
                    for dc in range(16):
                        P.mm(ps[:, :], h1T[:, dc, t4 * 128:(t4 + 1) * 128], wp[:, dc, :], dc == 0, dc == 15,
                             [hb[dc], wpb], [pb])
                    P.copy("act", pt[:, t4, 0:512], ps[:, :], [pb], [ptb[t4]] + (vb_ if pg == 0 else []))
                for cc in range(4):
                    ps, pb = bank()
                    for t4 in range(4):
                        P.mm(ps[:, t4 * 128:(t4 + 1) * 128], pt[:, t4, cc * 128:(cc + 1) * 128], band_b[pg], True, True,
                             [ptb[t4], cbfb], [pb])
                    P.copy("dve", dfT[:, cc, :], ps[:, :], [pb], [dfb[cc]] + (vb_ if pg == 0 else []))
                wz, wzb = load_w(owin_d, 0, 16, 8192 + pg * 512, 512)
                wq, wqb = load_w(opw_d, pg * 512, 4, 0, 512)
                for j in range(4):
                    g = pg * 4 + j
                    py, pyb = bank()
                    for cc in range(4):
                        P.mm(py[:, :], wq[:, cc, j * 128:(j + 1) * 128], dfT[:, cc, :], cc == 0, cc == 3,
                             [wqb, dfb[cc]], [pyb])
                    pz, pzb = bank()
                    for dc in range(16):
                        P.mm(pz[:, :], wz[:, dc, j * 128:(j + 1) * 128], h1T[:, dc, :], dc == 0, dc == 15,
                             [wzb, hb[dc]], [pzb])
                    sz, szb = f_ring.next()
                    P.act(sz, pz[:, :], AF.Silu, [pzb], [szb])
                    P.stt(gat[:, g, :], py[:, :], vec[:, V_PS + g:V_PS + g + 1], sz, ALU.mult, ALU.mult,
                          [pyb, vecb, szb], [gb[g]])
            emit_wout_pass(owout_d, 2048, gat, gb, half, dd)


        def emit_L1_full(dd, wsT_t, wsTb, bias2_t, bias2b):
            P.nw = 2
            P.nslot = 0
            XB = W_t[2][:].bitcast(F32)
            h1T = G2_t[:, 0:8192].bitcast(BF16).rearrange("p (c t) -> p c t", c=16)
            rs = G2_t[:, 8192:9216]
            pt = G2_t[:, 8192:10240].bitcast(BF16).rearrange("p (t n) -> p t n", t=8)
            vt = G1_t[:].bitcast(BF16).rearrange("p (g t n) -> p g t n", g=16, t=8)
            gat = G1_t[:].bitcast(BF16).rearrange("p (c t) -> p c t", c=16)
            dfT = aux[:].bitcast(BF16).rearrange("p (c t) -> p c t", c=4)
            hb = [Buf(f"h1_{dc}") for dc in range(16)]
            gb = [Buf(f"gat_{c}") for c in range(16)]
            rsb = Buf("rs")
            miscb = Buf("misc")
            ptb = [Buf(f"pt_{t}") for t in range(8)]
            dfb = [Buf(f"df_{c}") for c in range(4)]
            sq_ring = Ring([XB[:, i * 256:(i + 1) * 256].bitcast(BF16) for i in range(2)], "sq")
            f_ring = Ring([XB[:, 512 + i * 512:512 + (i + 1) * 512] for i in range(2)], "f")
            s2_ring = Ring([XB[:, 1536 + i * 1024:1536 + (i + 1) * 1024] for i in range(1)], "s2")
            gu_ring = Ring([XB[:, 2560 + i * 512:2560 + (i + 1) * 512].bitcast(BF16) for i in range(2)], "gu")
            sz_ring = Ring([XB[:, 3584 + i * 256:3584 + (i + 1) * 256].bitcast(BF16) for i in range(1)], "szx")
            misc = XB[:, 3840:4096]
            emit_rstd_bc(lambda dc: x1T[:, dc, :], lambda dc: [x1b[dc][0], x1b[dc][1]], 1024, sq_ring, rs, rsb, 1.0 / D)
            for dc in range(16):
                for hf in range(2):
                    t, tb = f_ring.next()
                    P.tt("dve", t, x1T[:, dc, hf * 512:(hf + 1) * 512], rs[:, hf * 512:(hf + 1) * 512], ALU.mult,
                         [x1b[dc][hf], rsb], [tb])
                    P.act(h1T[:, dc, hf * 512:(hf + 1) * 512], t, AF.Identity, [tb, derb], [hb[dc]], bias=dd[:, 16 + dc:17 + dc],
                          scale=dd[:, dc:dc + 1])
            st = misc[:, 0:64]
            stb = [Buf(f"st{i}") for i in range(64)]
            for vbk in range(4):
                wap, wb = load_w(owin_d, 0, 16, 2048 + vbk * 512, 512)
                for t8 in range(8):
                    ps, pb = bank()
                    for dc in range(16):
                        P.mm(ps[:, :], h1T[:, dc, t8 * 128:(t8 + 1) * 128], wap[:, dc, :], dc == 0, dc == 15, [hb[dc], wb], [pb])
                    vblk = vt[:, 4 * vbk:4 * vbk + 4, t8, :]
                    gbs = gb[4 * vbk:4 * vbk + 4]
                    P.act(vblk, ps[:, :].rearrange("p (g n) -> p g n", g=4), AF.Gelu_apprx_tanh, [pb], gbs)
                    j1, j1b = f_ring.next()
                    P.act(j1.rearrange("p (g n) -> p g n", g=4), vblk, AF.Square, gbs, [j1b, stb[32 + t8 * 4 + vbk]],
                          accum_out=st[:, 32 + t8 * 4 + vbk:33 + t8 * 4 + vbk])
                    j2, j2b = f_ring.next()
                    P.ts("dve", j2.rearrange("p (g n) -> p g n", g=4), vblk, 1.0, 0.0, ALU.mult, ALU.add, gbs,
                         [j2b, stb[t8 * 4 + vbk]], accum_out=st[:, t8 * 4 + vbk:t8 * 4 + vbk + 1])
            st3 = st[:, 0:64].rearrange("p (a t v) -> p a t v", a=2, v=4)
            red = misc[:, 64:80].rearrange("p (a t) -> p a t", a=2)
            P.tt("dve", red, st3[:, :, :, 0], st3[:, :, :, 1], ALU.add, stb, [miscb])
            P.tt("dve", red, red, st3[:, :, :, 2], ALU.add, [miscb], [miscb])
            P.tt("dve", red, red, st3[:, :, :, 3], ALU.add, [miscb], [miscb])
            mu, var, rstd, nmr = misc[:, 80:88], misc[:, 88:96], misc[:, 96:104], misc[:, 104:112]
            P.ts("dve", mu, red[:, 0, :], 1.0 / 2048, None, ALU.mult, None, [miscb], [miscb])
            P.ts("dve", var, red[:, 1, :], 1.0 / 2048, EPS, ALU.mult, ALU.add, [miscb], [miscb])
            P.tt("dve", nmr, mu, mu, ALU.mult, [miscb], [miscb])
            P.tt("dve", var, var, nmr, ALU.subtract, [miscb], [miscb])
            P.act(var, var, AF.Sqrt, [miscb], [miscb])
            P.recip(rstd, var, [miscb], [miscb])
            P.stt(nmr, mu, -1.0, rstd, ALU.mult, ALU.mult, [miscb], [miscb])
            for t8 in range(8):
                P.ts("dve", vt[:, :, t8, :], vt[:, :, t8, :], rstd[:, t8:t8 + 1], nmr[:, t8:t8 + 1], ALU.mult, ALU.add,
                     gb + [miscb], gb)
            for g4 in range(4):
                wu, wub = load_w(owin_d, 0, 16, g4 * 512, 512)
                wz, wzb = load_w(owin_d, 0, 16, 4096 + g4 * 512, 512)
                for j in range(4):
                    g = g4 * 4 + j
                    pu, pub = banks(2)
                    pz, pzb = banks(2)
                    for hf in range(2):
                        for dc in range(16):
                            P.mm(pu[:, hf * 512:(hf + 1) * 512], wu[:, dc, j * 128:(j + 1) * 128], h1T[:, dc, hf * 512:(hf + 1) * 512],
                                 dc == 0, dc == 15, [wub, hb[dc]], [pub[hf]])
                    for hf in range(2):
                        for dc in range(16):
                            P.mm(pz[:, hf * 512:(hf + 1) * 512], wz[:, dc, j * 128:(j + 1) * 128], h1T[:, dc, hf * 512:(hf + 1) * 512],
                                 dc == 0, dc == 15, [wzb, hb[dc]], [pzb[hf]])
                    pss, pssb = banks(2)
                    for t8 in range(8):
                        P.mm(pss[:, t8 * 128:(t8 + 1) * 128], vt[:, g, t8, :], wsT_t[:, g * 128:(g + 1) * 128], True, True,
                             [gb[g], wsTb], [pssb[t8 // 4]])
                    gu, gub = gu_ring.next()
                    P.act(gu, pu[:, 0:1024], AF.Gelu_apprx_tanh, pub, [gub])
                    s2, s2b = s2_ring.next()
                    b2 = bias2_t[:, g * 128:(g + 1) * 128]
                    for t8 in range(8):
                        P.stt(s2[:, t8 * 128:(t8 + 1) * 128], pss[:, t8 * 128:(t8 + 1) * 128], vec[:, V_LNW + g:V_LNW + g + 1], b2,
                              ALU.mult, ALU.add, pssb + [vecb, bias2b], [s2b])
                    P.tt("pool", s2, s2, gu, ALU.mult, [s2b, gub], [s2b])
                    sz, szb = gu_ring.next()
                    P.act(sz, pz[:, 0:1024], AF.Silu, pzb, [szb])
                    P.tt("dve", gat[:, g, :], s2, sz, ALU.mult, [s2b, szb], [gb[g]])
            emit_wout_pass(owout_d, 0, gat, gb, 0, dd, first=False, ntok=1024, tok0=0)
            first_pt = True
            for pg in range(4):
                wp, wpb = load_w(owin_d, 0, 16, 6144 + pg * 512, 512)
                for t8 in range(8):
                    ps, pb = bank()
                    for dc in range(16):
                        P.mm(ps[:, :], h1T[:, dc, t8 * 128:(t8 + 1) * 128], wp[:, dc, :], dc == 0, dc == 15, [hb[dc], wpb], [pb])
                    P.copy("act", pt[:, t8, :], ps[:, :], [pb], [ptb[t8]] + ([rsb] if first_pt else []))
                for cc in range(4):
                    pb2, pb2b = banks(2)
                    for t8 in range(8):
                        P.mm(pb2[:, t8 * 128:(t8 + 1) * 128], pt[:, t8, cc * 128:(cc + 1) * 128], band_b[pg], True, True,
                             [ptb[t8], cbfb], [pb2b[t8 // 4]])
                    P.copy("dve", dfT[:, cc, :], pb2[:, 0:1024], pb2b, [dfb[cc]] + ([wsTb, bias2b] if first_pt else []))
                first_pt = False
                wz, wzb = load_w(owin_d, 0, 16, 8192 + pg * 512, 512)
                wq, wqb = load_w(opw_d, pg * 512, 4, 0, 512)
                for j in range(4):
                    g = pg * 4 + j
                    py, pyb = banks(2)
                    pz, pzb = banks(2)
                    for hf in range(2):
                        for cc in range(4):
                            P.mm(py[:, hf * 512:(hf + 1) * 512], wq[:, cc, j * 128:(j + 1) * 128], dfT[:, cc, hf * 512:(hf + 1) * 512],
                                 cc == 0, cc == 3, [wqb, dfb[cc]], [pyb[hf]])
                    for hf in range(2):
                        for dc in range(16):
                            P.mm(pz[:, hf * 512:(hf + 1) * 512], wz[:, dc, j * 128:(j + 1) * 128], h1T[:, dc, hf * 512:(hf + 1) * 512],
                                 dc == 0, dc == 15, [wzb, hb[dc]], [pzb[hf]])
                    sz, szb = gu_ring.next()
                    P.act(sz, pz[:, 0:1024], AF.Silu, pzb, [szb])
                    P.stt(gat[:, g, :], py[:, 0:1024], vec[:, V_PS + g:V_PS + g + 1], sz, ALU.mult, ALU.mult,
                          pyb + [vecb, szb], [gb[g]])
            emit_wout_pass(owout_d, 2048, gat, gb, 0, dd, first=False, ntok=1024, tok0=0)
            P.barrier()
            P.nw = 3
            P.nslot = 0

        def emit_wout_pass(wsrc, r0, gat, gb, half, dd, first=False, ntok=512, tok0=None):
            T0 = half * 512 if tok0 is None else tok0
            for db in range(4):
                ww, wwb = load_w(wsrc, r0, 16, db * 512, 512)
                for j in range(4):
                    dch = db * 4 + j
                    for n0 in range(0, ntok, 512):
                        ps, pb = bank()
                        for fc in range(16):
                            P.mm(ps[:, :], ww[:, fc, j * 128:(j + 1) * 128], gat[:, fc, n0:n0 + 512], fc == 0, fc == 15,
                                 [wwb, gb[fc]], [pb])
                        hh = (T0 + n0) // 512
                        dst = x1T[:, dch, T0 + n0:T0 + n0 + 512]
                        if first:
                            P.ts("dve", dst, ps[:, :], dd[:, 32 + dch:33 + dch], None, ALU.mult, None, [pb, derb],
                                 [x1b[dch][hh]])
                        else:
                            P.stt(dst, ps[:, :], dd[:, 32 + dch:33 + dch], dst, ALU.mult, ALU.add, [pb, derb, x1b[dch][hh]],
                                  [x1b[dch][hh]])

        def emit_final(half):
            T0 = half * 512
            tmpA = G1_t[:, 0:4096]
            rs = G2_t[:, 8192:8704]
            rsb = Buf("rs_f")
            sq_ring = Ring([tmpA[:, i * 256:(i + 1) * 256].bitcast(BF16) for i in range(3)], "sqf")
            f_ring = Ring([tmpA[:, 768 + i * 512:768 + (i + 1) * 512] for i in range(4)], "ff")
            stg = [G2_t[:, 0:2048], G2_t[:, 2048:4096], G2_t[:, 4096:6144], G2_t[:, 6144:8192]]
            stgb = [Buf(f"stg{i}") for i in range(4)]
            emit_rstd_bc(lambda dc: x1T[:, dc, T0:T0 + 512], lambda dc: [x1b[dc][half]], 512, sq_ring, rs, rsb, 1.0 / D)
            xn = [None] * 16
            for d4 in range(4):
                tl = []
                for j in range(4):
                    dc = d4 * 4 + j
                    t, tb = f_ring.next()
                    P.stt(t, x1T[:, dc, T0:T0 + 512], vec[:, V_FNW + dc:V_FNW + dc + 1], rs, ALU.mult, ALU.mult,
                          [x1b[dc][half], vecb, rsb], [tb])
                    tl.append((t, tb))
                for t4 in range(4):
                    ps, pb = bank()
                    for j in range(4):
                        P.tr(ps[:, j * 128:(j + 1) * 128], tl[j][0][:, t4 * 128:(t4 + 1) * 128], ident_f,
                             [tl[j][1], cstb], [pb])
                    eng = "act" if (t4 % 2 == 0) else "dve"
                    P.copy(eng, stg[t4][:, d4 * 512:(d4 + 1) * 512], ps[:, :], [pb], [stgb[t4]])
            for t4 in range(4):
                r = T0 + t4 * 128
                P.dma("sp", out_d[r:r + 128, :], stg[t4], [stgb[t4]], [], f"out{t4}")


        def emit_L0(with_mod1=False):
            P.nw = 2
            dd = emit_mod(0)
            P.barrier()
            NTA = NCTX + NT
            hT = G2_t[:].bitcast(BF16).rearrange("p (c t) -> p c t", c=16)
            hTb = Buf("hT")
            gat = G1_t[:].bitcast(BF16).rearrange("p (c t) -> p c t", c=16)
            gb = [Buf(f"g0_{c}") for c in range(16)]
            XR = x1T_t
            XB = W_t[2][:].bitcast(F32)
            stg_ring = Ring([G1_t[:, i * 2048:(i + 1) * 2048] for i in range(4)], "xstg")
            ssr = small[:, 0:32]
            ssb = [Buf(f"ss{i}") for i in range(10)]
            for t in range(10):
                stg, stgb = stg_ring.next()
                src = ctx_d[t * 128:(t + 1) * 128, :] if t < 2 else xs_d[(t - 2) * 128:(t - 1) * 128, :]
                P.dma("sp", stg, src, (), [stgb], f"xl{t % 4}")
                junk = XR[:, 0:2048]
                junkb = Buf("junk")
                P.act(junk, stg, AF.Square, [stgb], [junkb, ssb[t]], accum_out=ssr[:, t:t + 1])
                P.ts("dve", ssr[:, t:t + 1], ssr[:, t:t + 1], 1.0 / D, EPS, ALU.mult, ALU.add, [ssb[t]], [ssb[t]])
                P.act(ssr[:, t:t + 1], ssr[:, t:t + 1], AF.Sqrt, [ssb[t]], [ssb[t]])
                P.recip(ssr[:, t:t + 1], ssr[:, t:t + 1], [ssb[t]], [ssb[t]])
                P.ts("dve", stg, stg, ssr[:, t:t + 1], None, ALU.mult, None, [stgb, ssb[t]], [stgb])
                so, bo = (48, 64) if t < 2 else (0, 16)
                for d4 in range(4):
                    ps, pb = bank()
                    for j in range(4):
                        dc = d4 * 4 + j
                        P.tr(ps[:, j * 128:(j + 1) * 128], stg[:, dc * 128:(dc + 1) * 128], ident_f, [stgb, cstb], [pb])
                    for j in range(4):
                        dc = d4 * 4 + j
                        P.act(hT[:, dc, t * 128:(t + 1) * 128], ps[:, j * 128:(j + 1) * 128], AF.Identity, [pb, derb], [hTb],
                              bias=dd[:, bo + dc:bo + dc + 1], scale=dd[:, so + dc:so + dc + 1])
            P.barrier()
            o = 0
            ob = 0
            def carve(n):
                nonlocal o
                a = XR[:, o:o + n]
                o += n
                assert o <= 16384, o
                return a
            def carveB(n):
                nonlocal ob
                a = XB[:, ob:ob + n]
                ob += n
                assert ob <= 4096, ob
                return a
            gc3 = carve(320).rearrange("p (c k) -> p c k", c=10)
            glb3 = carve(320).rearrange("p (c k) -> p c k", c=10)
            ngc3 = carve(320).rearrange("p (c k) -> p c k", c=10)
            bexp3 = carve(320).rearrange("p (c k) -> p c k", c=10)
            beta3 = carve(320).rearrange("p (c k) -> p c k", c=10)
            egl3 = carve(320).rearrange("p (c k) -> p c k", c=10)
            gl3 = carve(320).rearrange("p (c k) -> p c k", c=10)
            o_save = o
            o = 2240 + 7552
            Graw = carve(640).rearrange("p (c k) -> p c k", c=10)
            ones_f = carve(128)
            gtmp = carve(320).rearrange("p (c k) -> p c k", c=10)
            o = o_save
            gatesb = Buf("gates")
            P.memset("dve", ones_f, 1.0, [gatesb])
            P.memset("dve", small[:, 64:65], EPS, [smallb])
            wap, wb, key = wslot()
            P.dma("pool", wap[:, :, 0:64], wab_d[:, :].rearrange("(c p) n -> p c n", p=128), (), [wb], key)
            pg2, pg2b = banks(2)
            for t in range(10):
                for dc in range(16):
                    P.mm(pg2[:, t * 64:(t + 1) * 64], hT[:, dc, t * 128:(t + 1) * 128], wap[:, dc, 0:64], dc == 0, dc == 15,
                         [hTb, wb], [pg2b[(t * 64) // 512]])
            pg3 = pg2[:, 0:640].rearrange("p (c k) -> p c k", c=10)
            dtb_bc = vec[:, V_DTB:V_DTB + 32]
            nA = small[:, 32:64]
            P.act(nA, vec[:, V_ALOG:V_ALOG + 32], AF.Exp, [vecb], [smallb])
            P.ts("dve", nA, nA, -1.0, None, ALU.mult, None, [smallb], [smallb])
            for t in range(10):
                P.tt("dve", Graw[:, t, 0:32], pg3[:, t, 0:32], dtb_bc, ALU.add, pg2b + [vecb], [gatesb])
            P.act(Graw[:, :, 0:32], Graw[:, :, 0:32], AF.Exp, [gatesb], [gatesb])
            P.act(Graw[:, :, 0:32], Graw[:, :, 0:32], AF.Ln, [gatesb], [gatesb], bias=1.0)
            for t in range(10):
                P.tt("dve", Graw[:, t, 0:32], Graw[:, t, 0:32], nA, ALU.mult, [gatesb, smallb], [gatesb])
            P.act(Graw[:, :, 32:64], pg3[:, :, 32:64], AF.Exp, pg2b, [gatesb], scale=-1.0)
            P.act(Graw[:, :, 32:64], Graw[:, :, 32:64], AF.Ln, [gatesb], [gatesb], bias=1.0)
            P.ts("dve", Graw[:, :, 32:64], Graw[:, :, 32:64], -1.0, None, ALU.mult, None, [gatesb], [gatesb])
            pcs_, pcsb = bank()
            ptt_, pttb = bank()
            pc3 = pcs_[:, 0:320].rearrange("p (c k) -> p c k", c=10)
            pt3 = ptt_[:, 0:320].rearrange("p (c k) -> p c k", c=10)
            tri_a = cst[:, C_TA:C_TA + 128]
            tri_d = cst[:, C_TD:C_TD + 128]
            for t in range(10):
                P.mm(pc3[:, t, 0:16], tri_a, Graw[:, t, 0:16], True, True, [cstb, gatesb], [pcsb])
                P.mm(pc3[:, t, 16:32], tri_d, Graw[:, t, 16:32], True, True, [cstb, gatesb], [pcsb])
                P.mm(pt3[:, t, :], ones_f, Graw[:, t, 0:32], True, True, [gatesb], [pttb])
            P.copy("act", gc3, pc3, [pcsb], [gatesb])
            P.ts("dve", ngc3, pc3, -1.0, None, ALU.mult, None, [pcsb], [gatesb])
            P.tt("dve", glb3, pc3, Graw[:, :, 32:64], ALU.add, [pcsb, gatesb], [gatesb])
            P.act(bexp3, glb3, AF.Exp, [gatesb], [gatesb])
            P.act(beta3, Graw[:, :, 32:64], AF.Exp, [gatesb], [gatesb])
            P.tt("dve", gtmp, pt3, gc3, ALU.subtract, [pttb, gatesb], [gatesb])
            P.act(egl3, gtmp, AF.Exp, [gatesb], [gatesb])
            P.act(gl3, pt3, AF.Exp, [pttb], [gatesb])
            P.barrier()
            slots = []
            for i in range(2):
                sl = {}
                sl["kqT"] = carve(1280).bitcast(BF16).rearrange("p (c w t) -> p c w t", c=10, w=2)
                sl["ktok"] = carve(640).bitcast(BF16).rearrange("p (c d) -> p c d", c=10)
                sl["vtok"] = carve(640).bitcast(BF16).rearrange("p (c d) -> p c d", c=10)
                sl["zAs"] = carve(512).bitcast(BF16)
                sl["o1"] = carve(512).bitcast(BF16)
                sl["S"] = carve(128)
                sl["Sb"] = carve(64).bitcast(BF16)
                for nm in ("kqT", "ktok", "vtok", "zAs", "o1", "S", "Sb"):
                    sl[nm + "_b"] = Buf(f"{nm}{i}")
                slots.append(sl)
            oc = 0
            def carveC(n):
                nonlocal oc
                a = aux[:, oc:oc + n]
                oc += n
                assert oc <= 2048, oc
                return a
            CH = 1664
            chain_bases = [carve(CH), carve(CH), carve(CH), carveB(CH), carveC(CH)]
            vn_ring = [Ring([carve(64).bitcast(BF16) for _ in range(2)], f"vn{p}") for p in range(2)]
            fsq = carve(512).bitcast(BF16)
            fsqb = Buf("fsq")
            oacc = carveB(1024)
            oaccb = Buf("oacc")
            Gt = carveB(256)
            Gtb = Buf("Gt")

            def mk_chain_slot(base, nm):
                bfv = lambda a: a.bitcast(BF16)
                cs = dict(
                    em=base[:, 0:384], xyrtA=bfv(base[:, 0:256]), ab=bfv(base[:, 256:384]),
                    F=bfv(base[:, 384:576]), r1t1=bfv(base[:, 384:512]), a2=bfv(base[:, 512:576]),
                    egc=bfv(base[:, 576:640]), kbg=bfv(base[:, 576:640]),
                    y0=bfv(base[:, 640:704]), nw=bfv(base[:, 640:704]),
                    xa=bfv(base[:, 704:832]), msk=bfv(base[:, 832:1152]), xyrtB=bfv(base[:, 1152:1408]),
                    rf=bfv(base[:, 1408:1472]), vb=bfv(base[:, 1472:1536]), kg=bfv(base[:, 1536:1600]),
                    qg=bfv(base[:, 1600:1664]), buf=Buf("chain_" + nm))
                return cs
            chain_slots = [[mk_chain_slot(chain_bases[i], f"p0_{i}") for i in range(3)],
                           [mk_chain_slot(chain_bases[3], "p1_0"), mk_chain_slot(chain_bases[4], "p1_1")]]
            xA = carve(832)
            xBt = carveB(832)
            bfv_ = lambda a: a.bitcast(BF16)
            cs6 = dict(
                em=xA[:, 0:384], xyrtA=bfv_(xA[:, 0:256]), ab=bfv_(xA[:, 256:384]),
                F=bfv_(xA[:, 384:576]), r1t1=bfv_(xA[:, 384:512]), a2=bfv_(xA[:, 512:576]),
                egc=bfv_(xA[:, 576:640]), kbg=bfv_(xA[:, 576:640]),
                y0=bfv_(xA[:, 640:704]), nw=bfv_(xA[:, 640:704]),
                xa=bfv_(xA[:, 704:832]), msk=bfv_(xBt[:, 0:320]), xyrtB=bfv_(xBt[:, 320:576]),
                rf=bfv_(xBt[:, 576:640]), vb=bfv_(xBt[:, 640:704]), kg=bfv_(xBt[:, 704:768]),
                qg=bfv_(xBt[:, 768:832]), buf=Buf("chain_p1_2"))
            chain_slots[1].append(cs6)
            for j, cs_ in enumerate(chain_slots[0] + chain_slots[1]):
                cs_["pcs"] = psum[j // 2][:, (j % 2) * 256:(j % 2) * 256 + 256]
                cs_["pcb"] = pbuf[j // 2]
            cin_d = [nc.dram_tensor(f"cin{h}", [128, 128], F32) for h in range(16)]
            cout_d = [nc.dram_tensor(f"cout{h}", [256, 128], F32) for h in range(16)]
            cinb = [Buf(f"cin{h}") for h in range(16)]
            coutb = [Buf(f"cout{h}") for h in range(16)]
            BLK = ((0, 512), (512, 1024), (1024, 1280))

            cvq, cvk, cvv = chain_bases[0][:, 0:1280], chain_bases[1][:, 0:1280], chain_bases[2][:, 0:1280]
            sqq = chain_bases[3][:, 0:640].bitcast(BF16)
            sqk = chain_bases[3][:, 640:1280].bitcast(BF16)
            vTt = chain_bases[4][:, 0:640].bitcast(BF16)

            head_w = {}

            def load_head_w(h):
                head_w[h] = load_w(ewhd_d, h * D, 16, 0, 512)

            def prep_head(h):
                sl = slots[h % 2]
                wap, wb = head_w.pop(h)

                def proj_conv(which, cv, cvb):
                    ps3, pb3 = banks(3)
                    for bi, (n0, n1) in enumerate(BLK):
                        for dc in range(16):
                            P.mm(ps3[:, n0:n1], wap[:, dc, which * 128:(which + 1) * 128], hT[:, dc, n0:n1], dc == 0, dc == 15,
                                 [wb, hTb], [pb3[bi]])
                    grp = which * 16 + h
                    w0 = vec[:, V_CQ + grp * 3:V_CQ + grp * 3 + 1]
                    w1 = vec[:, V_CQ + grp * 3 + 1:V_CQ + grp * 3 + 2]
                    w2 = vec[:, V_CQ + grp * 3 + 2:V_CQ + grp * 3 + 3]
                    P.act(cv, ps3[:, 0:NTA], AF.Identity, pb3 + [vecb], [cvb], scale=w1)
                    P.stt(cv[:, 1:256], ps3[:, 0:255], w0, cv[:, 1:256], ALU.mult, ALU.add, pb3 + [vecb, cvb], [cvb])
                    P.stt(cv[:, 0:255], ps3[:, 1:256], w2, cv[:, 0:255], ALU.mult, ALU.add, pb3 + [vecb, cvb], [cvb])
                    cvx = cv[:, 256:NTA].rearrange("p (r t) -> p r t", t=64)
                    psx = ps3[:, 256:NTA].rearrange("p (r t) -> p r t", t=64)
                    P.stt(cvx[:, :, 1:64], psx[:, :, 0:63], w0, cvx[:, :, 1:64], ALU.mult, ALU.add, pb3 + [vecb, cvb], [cvb])
                    P.stt(cvx[:, :, 0:63], psx[:, :, 1:64], w2, cvx[:, :, 0:63], ALU.mult, ALU.add, pb3 + [vecb, cvb], [cvb])

                def to_tokmajor(src_T, srcb, dst, dstb):
                    pa, pab = bank()
                    pa_b = pa[:].bitcast(BF16)
                    for c in range(8):
                        P.tr(pa_b[:, c * 128:(c + 1) * 128], src_T(c), ident_b, [srcb, cbfb], [pab])
                    P.copy("act", dst[:, 0:8, :], pa_b[:, 0:1024].rearrange("p (c d) -> p c d", c=8), [pab], [dstb])
                    pa2, pa2b = bank()
                    pa2_b = pa2[:].bitcast(BF16)
                    for c in range(8, 10):
                        P.tr(pa2_b[:, (c - 8) * 128:(c - 7) * 128], src_T(c), ident_b, [srcb, cbfb], [pa2b])
                    P.copy("dve", dst[:, 8:10, :], pa2_b[:, 0:256].rearrange("p (c d) -> p c d", c=2), [pa2b], [dstb])

                def chain_qk(which, cv, sq):
                    cvb, sqb = Buf("cv"), Buf("sq")
                    proj_conv(which, cv, cvb)
                    yield
                    P.act(cv, cv, AF.Silu, [cvb], [cvb])
                    P.tt("pool", sq, cv, cv, ALU.mult, [cvb], [sqb])
                    yield
                    ss3, ssb3 = banks(3)
                    for bi, (n0, n1) in enumerate(BLK):
                        P.mm(ss3[:, n0:n1], ones_b, sq[:, n0:n1], True, True, [cbfb, sqb], [ssb3[bi]])
                    P.act(ss3[:, 0:NTA], ss3[:, 0:NTA], AF.Ln, ssb3 + [smallb], ssb3, bias=small[:, 64:65])
                    P.act(ss3[:, 0:NTA], ss3[:, 0:NTA], AF.Exp, ssb3, ssb3, scale=-0.5)
                    cv3 = cv.rearrange("p (c t) -> p c t", c=10)
                    ri3 = ss3[:, 0:NTA].rearrange("p (c t) -> p c t", c=10)
                    if which == 0:
                        P.stt(sl["kqT"][:, :, 1, :], cv3, 128.0 ** -0.5, ri3, ALU.mult, ALU.mult, [cvb] + ssb3, [sl["kqT_b"]])
                    else:
                        P.tt("dve", sl["kqT"][:, :, 0, :], cv3, ri3, ALU.mult, [cvb] + ssb3, [sl["kqT_b"]])
                        yield
                        to_tokmajor(lambda c: sl["kqT"][:, c, 0, :], sl["kqT_b"], sl["ktok"], sl["ktok_b"])

                def chain_v():
                    cvb, vTb = Buf("cvv"), Buf("vT")
                    proj_conv(2, cvv, cvb)
                    yield
                    P.act(vTt, cvv, AF.Silu, [cvb], [vTb])
                    yield
                    to_tokmajor(lambda c: vTt[:, c * 128:(c + 1) * 128], vTb, sl["vtok"], sl["vtok_b"])

                def chain_z():
                    pz2, pz2b = banks(2)
                    for bi in range(2):
                        n0 = 256 + bi * 512
                        for dc in range(16):
                            P.mm(pz2[:, bi * 512:(bi + 1) * 512], wap[:, dc, 384:512], hT[:, dc, n0:n0 + 512], dc == 0, dc == 15,
                                 [wb, hTb], [pz2b[bi]])
                    P.act(sl["zAs"], pz2[:, 0:1024], AF.Silu, pz2b, [sl["zAs_b"]])
                    P.memset("pool", sl["S"], 0.0, [sl["S_b"]])
                    P.memset("pool", sl["Sb"], 0.0, [sl["Sb_b"]])
                    return
                    yield

                gens = [chain_qk(0, cvq, sqq), chain_qk(1, cvk, sqk), chain_v(), chain_z()]
                while gens:
                    keep = []
                    for g in gens:
                        try:
                            next(g)
                            keep.append(g)
                        except StopIteration:
                            pass
                    gens = keep

            def intra_gen(h, ph, c, cs):
                sl = slots[h % 2]
                gcol = ph * 16 + h
                kq = sl["kqT"]
                kqb = sl["kqT_b"]
                cb = cs["buf"]
                pcs, pcb = cs["pcs"], cs["pcb"]
                P.mm(pcs[:, 0:256], kq[:, c, 0, :], kq[:, c, :, :].rearrange("p w t -> p (w t)"), True, True, [kqb], [pcb])
                pes, peb = bank()
                P.tr(pes[:, 0:128], glb3[:, c, gcol:gcol + 1].to_broadcast([128, 128]), ident_f, [gatesb, cstb], [peb])
                P.tr(pes[:, 128:256], gc3[:, c, gcol:gcol + 1].to_broadcast([128, 128]), ident_f, [gatesb, cstb], [peb])
                P.tr(pes[:, 256:384], gc3[:, c, gcol:gcol + 1].to_broadcast([128, 128]), ident_f, [gatesb, cstb], [peb])
                mbase = C_MA if ph == 0 else C_MD
                P.act(cs["egc"], pes[:, 128:256], AF.Exp, [peb], [cb])
                P.tt("dve", cs["em"], pes[:, 0:384], cst[:, mbase:mbase + 384], ALU.add, [peb, cstb], [cb])
                P.tt("pool", cs["qg"], kq[:, c, 1, :], cs["egc"], ALU.mult, [kqb], [cb])
                yield
                P.act(cs["F"][:, 0:256], cs["em"][:, 0:256], AF.Exp, [gatesb], [cb], bias=ngc3[:, c, gcol:gcol + 1])
                P.act(cs["F"][:, 256:384], cs["em"][:, 256:384], AF.Exp, [gatesb], [cb], bias=glb3[:, c, gcol:gcol + 1],
                      scale=-1.0)
                yield
                P.tt("dve", cs["xa"], pcs[:, 0:256], cs["F"][:, 0:256], ALU.mult, [pcb], [cb])
                P.tt("dve", cs["y0"], pcs[:, 0:128], cs["F"][:, 256:384], ALU.mult, [pcb], [cb])
                P.tt("pool", cs["vb"], sl["vtok"][:, c, :], beta3[:, c, gcol:gcol + 1].to_broadcast([128, 128]), ALU.mult,
                     [sl["vtok_b"], gatesb], [cb])
                P.tt("pool", cs["kg"], sl["ktok"][:, c, :], egl3[:, c, gcol:gcol + 1].to_broadcast([128, 128]), ALU.mult,
                     [sl["ktok_b"], gatesb], [cb])
                yield
                msk = cs["msk"]
                m1x, m1y, m2y = (bm_b[2], bm_b[1], bm_b[3]) if ph == 0 else (bm_b[1], bm_b[2], bm_b[4])
                P.tt("pool", msk[:, 0:128], cs["xa"][:, 0:128], bm_b[0], ALU.mult, [cbfb], [cb])
                P.tt("pool", msk[:, 128:256], cs["y0"], bm_b[0], ALU.mult, [cbfb], [cb])
                yield
                P.tt("pool", msk[:, 256:384], cs["xa"][:, 0:128], m1x, ALU.mult, [cbfb], [cb])
                P.tt("pool", msk[:, 384:512], cs["y0"], m1y, ALU.mult, [cbfb], [cb])
                P.tt("pool", msk[:, 512:640], cs["y0"], m2y, ALU.mult, [cbfb], [cb])
                Xk, Yk = msk[:, 0:128], msk[:, 128:256]
                Rk = ident_b
                for k in range(5):
                    prs, prb = bank()
                    if k <= 3:
                        P.mm(prs[:, 0:128], Yk, Xk, True, True, [cb], [prb])
                        P.mm(prs[:, 128:256], Xk, Yk, True, True, [cb], [prb])
                    P.mm(prs[:, 256:384], ident_b, Rk, True, False, [cbfb, cb], [prb])
                    P.mm(prs[:, 256:384], Yk, nident_b if k == 0 else Rk, False, True, [cb, cbfb], [prb])
                    nx = cs["xyrtA"] if k % 2 == 0 else cs["xyrtB"]
                    lo = 0 if k <= 3 else 256
                    P.copy("act" if k % 2 == 0 else "dve", nx[:, lo:384], prs[:, lo:384], [prb], [cb])
                    Xk, Yk, Rk = nx[:, 0:128], nx[:, 128:256], nx[:, 256:384]
                    yield
                ptr_, ptrb = bank()
                ptr_b = ptr_[:].bitcast(BF16)
                P.tr(ptr_b[:, 0:128], Rk, ident_b, [cb, cbfb], [ptrb])
                P.copy("dve", cs["xyrtA"][:, 384:512], ptr_b[:, 0:128], [ptrb], [cb])
                Tk = cs["xyrtA"][:, 384:512]
                yield
                pl, plb = bank()
                P.mm(pl[:, 0:128], msk[:, 384:512], Rk, True, True, [cb], [plb])
                P.mm(pl[:, 128:256], msk[:, 256:384], Tk, True, True, [cb], [plb])
                P.copy("act", cs["ab"], pl[:, 0:256], [plb], [cb])
                yield
                pl, plb = bank()
                P.mm(pl[:, 0:128], Tk, cs["ab"][:, 0:128], True, True, [cb], [plb])
                P.mm(pl[:, 128:256], Rk, cs["ab"][:, 128:256], True, True, [cb], [plb])
                P.tt("dve", cs["r1t1"], cs["xyrtA"][:, 256:512], pl[:, 0:256], ALU.subtract, [plb], [cb])
                yield
                pl, plb = bank()
                P.mm(pl[:, 0:128], msk[:, 512:640], cs["r1t1"][:, 0:128], True, True, [cb], [plb])
                P.copy("act", cs["a2"], pl[:, 0:128], [plb], [cb])
                yield
                pl, plb = bank()
                P.mm(pl[:, 0:128], cs["r1t1"][:, 128:256], cs["a2"], True, True, [cb], [plb])
                P.tt("dve", cs["rf"], cs["r1t1"][:, 0:128], pl[:, 0:128], ALU.subtract, [plb], [cb])
                P.tt("pool", cs["kbg"], sl["ktok"][:, c, :], bexp3[:, c, gcol:gcol + 1].to_broadcast([128, 128]), ALU.mult,
                     [sl["ktok_b"], gatesb], [cb])
                yield
                pw, pwb = bank()
                P.mm(pw[:, 0:128], cs["kbg"], cs["rf"], True, True, [cb], [pwb])
                P.act(cs["nw"], pw[:, 0:128], AF.Identity, [pwb], [cb], scale=-1.0)

            def scan_gen(h, ph, c, cs, last):
                sl = slots[h % 2]
                gcol = ph * 16 + h
                cb = cs["buf"]
                S, Sb_ = sl["S"], sl["S_b"]
                Sb, Sbb = sl["Sb"], sl["Sb_b"]
                pv, pvb = bank()
                P.mm(pv[:, 0:128], cs["rf"], cs["vb"], True, False, [cb], [pvb])
                P.mm(pv[:, 0:128], cs["nw"], Sb, False, True, [cb, Sbb], [pvb])
                vn, vnb = vn_ring[ph].next()
                P.copy("act", vn, pv[:, 0:128], [pvb], [vnb])
                yield
                po, pob = bank()
                if c >= 2:
                    P.mm(po[:, 0:128], Sb, cs["qg"], True, False, [Sbb, cb], [pob])
                    P.mm(po[:, 0:128], vn, cs["xa"][:, 128:256], False, True, [vnb, cb], [pob])
                P.mm(po[:, 128:256], cs["kg"], vn, True, True, [cb, vnb], [pob])
                P.stt(S, S, gl3[:, c, gcol:gcol + 1], po[:, 128:256], ALU.mult, ALU.add, [Sb_, gatesb, pob], [Sb_])
                if not last:
                    P.copy("act", Sb, S, [Sb_], [Sbb])
                if c >= 2:
                    tk = (c - 2) * 128
                    if ph == 0:
                        P.copy("dve", sl["o1"][:, tk:tk + 128], po[:, 0:128], [pob], [sl["o1_b"]])
                    else:
                        P.tt("dve", oacc[:, tk:tk + 128], po[:, 0:128], sl["o1"][:, tk:tk + 128], ALU.add,
                             [pob, sl["o1_b"]], [oaccb])

            def run_block(phases):
                st = []
                for (h, ph) in phases:
                    order = list(range(10)) if ph == 0 else list(range(9, 1, -1))
                    st.append(dict(h=h, ph=ph, order=order, istart=0, idone=set(), sdone=0, scan=None, intras=[]))
                while True:
                    progressed = False
                    for p in st:
                        n = len(p["order"])
                        ns = len(chain_slots[p["ph"]])
                        while p["istart"] < n and p["istart"] - p["sdone"] < ns:
                            i = p["istart"]
                            g = intra_gen(p["h"], p["ph"], p["order"][i], chain_slots[p["ph"]][i % ns])
                            p["intras"].append((i, g))
                            p["istart"] += 1
                        if p["scan"] is None and p["sdone"] < n and p["sdone"] in p["idone"]:
                            i = p["sdone"]
                            if i == 0 and p["ph"] == 1:
                                exchange_finish(p["h"])
                            p["scan"] = scan_gen(p["h"], p["ph"], p["order"][i], chain_slots[p["ph"]][i % ns], i == n - 1)
                    for p in st:
                        keep = []
                        for (i, g) in p["intras"]:
                            try:
                                next(g)
                                keep.append((i, g))
                            except StopIteration:
                                p["idone"].add(i)
                            progressed = True
                        p["intras"] = keep
                        if p["scan"] is not None:
                            try:
                                next(p["scan"])
                            except StopIteration:
                                p["scan"] = None
                                p["sdone"] += 1
                            progressed = True
                    if not progressed:
                        break

            def exchange(h):
                sl = slots[h % 2]
                P.dma("sp", cin_d[h][:, :], sl["S"], [sl["S_b"]], [cinb[h]], f"xi{h}")
                P.op("pool", lambda e, h=h: e.collective_compute(
                    "AllGather", ALU.bypass, replica_groups=groups or [[0, 1], [2, 3], [4, 5], [6, 7]],
                    ins=[cin_d[h].ap().opt()], outs=[cout_d[h].ap().opt()]), [cinb[h]], [coutb[h]], key="cc", inc=1)
                P.dma("sp", Gt.rearrange("p (r n) -> p r n", r=2), cout_d[h][:, :].rearrange("(r p) n -> p r n", p=128),
                      [coutb[h]], [Gtb], f"xo{h}")

            def exchange_finish(h):
                sl = slots[h % 2]
                P.ts("dve", sl["S"], Gt[:, 0:128], vec[:, V_SEL:V_SEL + 1], None, ALU.mult, None, [Gtb, vecb], [sl["S_b"]])
                P.stt(sl["S"], Gt[:, 128:256], vec[:, V_SEL + 1:V_SEL + 2], sl["S"], ALU.mult, ALU.add,
                      [Gtb, vecb, sl["S_b"]], [sl["S_b"]])
                P.copy("act", sl["Sb"], sl["S"], [sl["S_b"]], [sl["Sb_b"]])

            def finish_head(h):
                sl = slots[h % 2]
                sq, sqb = fsq, fsqb
                P.tt("pool", sq[:, 0:1024], oacc, oacc, ALU.mult, [oaccb], [sqb])
                ss2, ssb2 = banks(2)
                for bi in range(2):
                    P.mm(ss2[:, bi * 512:(bi + 1) * 512], ones_b, sq[:, bi * 512:(bi + 1) * 512], True, True, [cbfb, sqb],
                         [ssb2[bi]])
                P.act(ss2[:, 0:1024], ss2[:, 0:1024], AF.Ln, ssb2 + [smallb], ssb2, bias=small[:, 64:65], scale=1.0 / 128)
                P.act(ss2[:, 0:1024], ss2[:, 0:1024], AF.Exp, ssb2, ssb2, scale=-0.5)
                P.stt(oacc, oacc, vec[:, V_HN:V_HN + 1], ss2[:, 0:1024], ALU.mult, ALU.mult, [oaccb, vecb] + ssb2, [oaccb])
                P.tt("pool", gat[:, h, :], oacc, sl["zAs"], ALU.mult, [oaccb, sl["zAs_b"]], [gb[h]])

            mod1_gen = emit_mod_gen(1) if with_mod1 else iter(())
            P.nw = 3 if False else 2
            load_head_w(0)
            for h in range(nheads + 1):
                if h < nheads:
                    prep_head(h)
                    P.barrier()
                if h >= 1:
                    exchange(h - 1)
                next(mod1_gen, None)
                if h + 1 < nheads:
                    load_head_w(h + 1)
                ph_list = ([(h, 0)] if h < nheads else []) + ([(h - 1, 1)] if h >= 1 else [])
                P.bank_lo = 3
                run_block(ph_list)
                P.bank_lo = 0
                if h >= 1:
                    finish_head(h - 1)
                next(mod1_gen, None)
                P.barrier()
            for _ in mod1_gen:
                pass
            P.barrier()
            if stop_after == "gdn":
                return
            emit_wout_pass(ewout_d, 0, gat, gb, 0, dd, first=True, ntok=1024, tok0=0)
            P.barrier()
            emit_mixer_B(dd, hT, hTb, gat, gb)
            emit_wout_pass(ewout_d, 2048, gat, gb, 0, dd, first=False, ntok=1024, tok0=0)
            P.barrier()
            stg_ring2 = Ring([G2_t[:, i * 2048:(i + 1) * 2048] for i in range(4)], "xstg2")
            for t in range(8):
                stg, stgb = stg_ring2.next()
                P.dma("sp", stg, xs_d[t * 128:(t + 1) * 128, :], (), [stgb], f"xl{t % 4}")
                for d4 in range(4):
                    ps, pb = bank()
                    for j in range(4):
                        dc = d4 * 4 + j
                        P.tr(ps[:, j * 128:(j + 1) * 128], stg[:, dc * 128:(dc + 1) * 128], ident_f, [stgb, cstb], [pb])
                    for j in range(4):
                        dc = d4 * 4 + j
                        dst = x1T[:, dc, t * 128:(t + 1) * 128]
                        P.tt("dve", dst, ps[:, j * 128:(j + 1) * 128], dst, ALU.add, [pb, x1b[dc][t // 4]], [x1b[dc][t // 4]])
            P.barrier()
            P.nw = 3
            P.nslot = 0

        def emit_mixer_B(dd, hT, hTb, gat, gb):
            BASE = 6144 + 2048 + 64
            for cg in range(16):
                wap, wb = load_w(ewmb_d, cg * D, 16, 0, 512)
                w0 = vec[:, V_CB + cg * 3:V_CB + cg * 3 + 1]
                w1 = vec[:, V_CB + cg * 3 + 1:V_CB + cg * 3 + 2]
                w2 = vec[:, V_CB + cg * 3 + 2:V_CB + cg * 3 + 3]
                for half in range(2):
                    n0 = 256 + half * 512
                    pp = []
                    for j in range(4):
                        ps, pb = bank()
                        for dc in range(16):
                            P.mm(ps[:, :], wap[:, dc, j * 128:(j + 1) * 128], hT[:, dc, n0:n0 + 512], dc == 0, dc == 15,
                                 [wb, hTb], [pb])
                        pp.append((ps, pb))
                    (pbg, pbgb), (pcg, pcgb), (phb, phbb), (pzb, pzbb) = pp
                    cgs, cgsb = mb_ring.next()
                    P.copy("act", cgs, pcg[:, :], [pcgb], [cgsb])
                    t1, t1b = mb_ring.next()
                    P.tt("dve", t1, phb[:, :], cgs, ALU.mult, [phbb, cgsb], [t1b])
                    cv, cvb = mb_ring.next()
                    P.act(cv, t1, AF.Identity, [t1b, vecb], [cvb], scale=w1)
                    cv3 = cv.rearrange("p (r t) -> p r t", t=64)
                    t13 = t1.rearrange("p (r t) -> p r t", t=64)
                    P.stt(cv3[:, :, 1:64], t13[:, :, 0:63], w0, cv3[:, :, 1:64], ALU.mult, ALU.add, [t1b, vecb, cvb], [cvb])
                    P.stt(cv3[:, :, 0:63], t13[:, :, 1:64], w2, cv3[:, :, 0:63], ALU.mult, ALU.add, [t1b, vecb, cvb], [cvb])
                    sz, szb = mb_ring.next()
                    P.act(sz, pzb[:, :], AF.Silu, [pzbb], [szb])
                    P.tt("dve", cv, pbg[:, :], cv, ALU.mult, [pbgb, cvb], [cvb])
                    P.tt("pool", gat[:, cg, half * 512:(half + 1) * 512], cv, sz, ALU.mult, [cvb, szb], [gb[cg]])

        if mode == "L0":
            emit_L0()
            for dc in range(16):
                P.dma("sp", out_d[:, dc * NT:(dc + 1) * NT], x1T[:, dc, :], [x1b[dc][0], x1b[dc][1]], [], f"out{dc % 4}")
        if mode == "full":
            emit_L0(True)
            pre = emit_L1_prelude(True)
            P.barrier()
            emit_L1_full(*pre)
            for half in range(2):
                P.barrier()
                emit_final(half)
        if mode == "L1":
            x1all = Buf("x1all")
            for dc in range(16):
                P.dma("sp", x1T[:, dc, :], x1in_d[:, dc * NT:(dc + 1) * NT], (), [x1b[dc][0], x1b[dc][1]], "cld")
            pre = emit_L1_prelude()
            P.barrier()
            emit_L1_full(*pre)
            for half in range(2):
                P.barrier()
                emit_final(half)

        with nc.Block() as block:
            P.flush(block)
    return nc


def make_consts():
    c = np.zeros((128, NCST), np.float32)
    idx = np.arange(128)
    c[:, C_ID:C_ID + 128] = np.eye(128)
    c[:, C_TA:C_TA + 128] = (idx[:, None] <= idx[None, :])
    c[:, C_TD:C_TD + 128] = (idx[:, None] >= idx[None, :])
    P_, F_ = idx[:, None], idx[None, :]
    for base, asc in ((C_MA, True), (C_MD, False)):
        if asc:
            m1 = F_ > P_; m2 = F_ >= P_; m3 = P_ > F_
        else:
            m1 = F_ < P_; m2 = F_ <= P_; m3 = P_ < F_
        c[:, base:base + 128] = np.where(m1, 0.0, -BIG)
        c[:, base + 128:base + 256] = np.where(m2, 0.0, -BIG)
        c[:, base + 256:base + 384] = np.where(m3, 0.0, BIG)
    for k, r in enumerate((1, 2, 4, 8)):
        Dm = np.zeros((128, 128), np.float64)
        for i in range(128):
            row = i // 64
            lo = max(i - r, row * 64)
            hi = min(i + r + 1, row * 64 + 64)
            Dm[i, lo:hi] = 1.0 / (hi - lo)
            Dm[i, i] -= 1.0
        c[:, C_BAND + 128 * k:C_BAND + 128 * (k + 1)] = Dm.T
    c[:, C_ONES:C_ONES + 128] = 1.0
    pb, fb = P_ // 32, F_ // 32
    m1_lo = ((pb == 1) & (fb == 0)) | ((pb == 3) & (fb == 2))
    m2_lo = (P_ >= 64) & (F_ < 64)
    for i, m in enumerate((pb == fb, m1_lo, m1_lo.T, m2_lo, m2_lo.T)):
        c[:, C_BM + 128 * i:C_BM + 128 * (i + 1)] = m
    return c


def fm(v, n):
    return np.ascontiguousarray(np.asarray(v, np.float32).reshape(n, 128).T)


def make_vec(inp, b, s):
    v = np.zeros((128, NV), np.float32)
    v[:, V_C:V_C + 16] = fm(inp["c"][b], 16)
    v[:, V_CC:V_CC + 16] = fm(inp["c_ctx"], 16)
    v[:, V_AB0:V_AB0 + 48] = fm(inp["ada_b"][0], 48)
    v[:, V_AB1:V_AB1 + 48] = fm(inp["ada_b"][1], 48)
    v[:, V_NW0:V_NW0 + 16] = fm(inp["norm_w"][0], 16)
    v[:, V_NW1:V_NW1 + 16] = fm(inp["norm_w"][1], 16)
    v[:, V_LNW:V_LNW + 16] = fm(inp["o_ln_w"][0], 16)
    v[:, V_LNB:V_LNB + 16] = fm(inp["o_ln_b"][0], 16)
    v[:, V_PS:V_PS + 16] = fm(inp["o_pool_scale"][0], 16)
    v[:, V_FNW:V_FNW + 16] = fm(inp["final_norm_w"], 16)
    cq = np.asarray(inp["e_conv_qkv"][0], np.float32)
    cb = np.asarray(inp["e_conv_b"][0], np.float32)
    if s == 1:
        cq = cq[::-1]
        cb = cb[::-1]
    v[:, V_CQ:V_CQ + 144] = np.stack([fm(cq[t], 48) for t in range(3)], axis=2).reshape(128, 144)
    v[:, V_CB:V_CB + 48] = np.stack([fm(cb[t], 16) for t in range(3)], axis=2).reshape(128, 48)
    v[:, V_HN] = np.asarray(inp["e_head_norm"][0], np.float32)
    v[:, V_SEL] = 1.0 if s == 1 else 0.0
    v[:, V_SEL + 1] = 1.0 if s == 0 else 0.0
    dirs = (0, 1) if s == 0 else (1, 0)
    dtb = np.asarray(inp["e_dt_bias"][0], np.float32)
    alog = np.asarray(inp["e_a_log"][0], np.float32)
    v[:, V_DTB:V_DTB + 32] = np.concatenate([dtb[dirs[0]], dtb[dirs[1]]])[None, :]
    v[:, V_ALOG:V_ALOG + 32] = np.concatenate([alog[dirs[0]], alog[dirs[1]]])[None, :]
    return v


def common_maps(inp):
    f = lambda a: np.ascontiguousarray(np.asarray(a, np.float32))
    return {
        "cst": make_consts(),
        "ada_w0": f(inp["ada_w"][0]), "ada_w1": f(inp["ada_w"][1]),
        "o_w_in": f(inp["o_w_in"][0]),
        "o_pool_w": f(np.asarray(inp["o_pool_w"][0]).reshape(2048, 512)),
        "o_w_out": f(inp["o_w_out"][0]),
    }


def core_maps_L1(inp, b, s):
    ws = np.asarray(inp["o_w_s"][0], np.float32)
    bs = np.asarray(inp["o_b_s"][0], np.float32)
    if s == 1:
        ws = ws[:, ::-1, ::-1]
        bs = bs[:, ::-1]
    return {
        "vec": make_vec(inp, b, s),
        "ws": np.ascontiguousarray(ws.transpose(1, 0, 2).reshape(128, 2048)),
        "bsb": np.ascontiguousarray(np.broadcast_to(bs.reshape(1, 2048), (128, 2048))),
    }


_NC_CACHE = {}


def kernel(**inputs):
    inp = {k: np.asarray(v) for k, v in inputs.items()}
    if "full" not in _NC_CACHE:
        _NC_CACHE["full"] = build("full")
    nc = _NC_CACHE["full"]
    com = common_maps(inp)
    com.update(common_maps_L0(inp))
    maps = []
    for core in range(8):
        b, s = core // 2, core % 2
        m = dict(com)
        m.update(core_maps_L1(inp, b, s))
        m.update(core_maps_L0(inp, b, s))
        maps.append(m)
    res = run_bass_kernel_spmd(nc, maps, core_ids=list(range(8)))
    out = np.empty((4, 2048, D), np.float32)
    for core in range(8):
        b, s = core // 2, core % 2
        o = np.asarray(res.results[core]["out"], np.float32)
        if s == 1:
            o = o[::-1]
        out[b, s * NT:(s + 1) * NT] = o
    return out


def common_maps_L0(inp):
    f = lambda a: np.ascontiguousarray(np.asarray(a, np.float32))
    w = np.asarray(inp["e_w_in"][0], np.float32)
    hd = np.empty((16, D, 512), np.float32)
    mb = np.empty((16, D, 512), np.float32)
    base = 6144 + 2048 + 64
    for h in range(16):
        for j, c0 in enumerate((h * 128, 2048 + h * 128, 4096 + h * 128, 6144 + h * 128)):
            hd[h, :, j * 128:(j + 1) * 128] = w[:, c0:c0 + 128]
        for j in range(4):
            c0 = base + j * 2048 + h * 128
            mb[h, :, j * 128:(j + 1) * 128] = w[:, c0:c0 + 128]
    return {"e_w_hd": hd.reshape(16 * D, 512), "e_w_mb": mb.reshape(16 * D, 512), "e_w_out": f(inp["e_w_out"][0])}


def core_maps_L0(inp, b, s):
    xs = np.asarray(inp["x"][b, s * NT:(s + 1) * NT], np.float32)
    cx = np.asarray(inp["ctx"][b], np.float32)
    if s == 1:
        xs = xs[::-1]
        cx = cx[::-1]
    d1, d2 = (0, 1) if s == 0 else (1, 0)
    base = 8192
    cols = np.concatenate([np.arange(base + d1 * 16, base + d1 * 16 + 16), np.arange(base + d2 * 16, base + d2 * 16 + 16),
                           np.arange(base + 32 + d1 * 16, base + 32 + d1 * 16 + 16),
                           np.arange(base + 32 + d2 * 16, base + 32 + d2 * 16 + 16)])
    wab = np.asarray(inp["e_w_in"][0], np.float32)[:, cols]
    return {"xs": np.ascontiguousarray(xs), "ctxs": np.ascontiguousarray(cx), "w_ab": np.ascontiguousarray(wab)}
```

```python
import numpy as np
from contextlib import ExitStack
import concourse.bass as bass
import concourse.mybir as mybir
from concourse.bass_utils import run_bass_kernel_spmd

F32 = mybir.dt.float32
BF16 = mybir.dt.bfloat16
AF = mybir.ActivationFunctionType
ALU = mybir.AluOpType

D = 2048
NT = 1024
NCTX = 256
EPS = 1e-6
BIG = 30000.0
EVEN_COLS = 16448
ODD_COLS = 10240

C_ID, C_TA, C_TD, C_MA, C_MD, C_BAND, C_ONES, C_BM, NCST = 0, 128, 256, 384, 768, 1152, 1664, 1792, 2432
V_C, V_CC, V_AB0, V_AB1, V_NW0, V_NW1, V_LNW, V_LNB, V_PS, V_FNW = 0, 16, 32, 80, 128, 144, 160, 176, 192, 208
V_CQ, V_CB, V_HN, V_SEL, V_DTB, V_ALOG, NV = 224, 368, 416, 417, 419, 451, 512


class Buf:
    __slots__ = ("name", "w", "r", "excl")

    def __init__(self, name, excl=False):
        self.name = name
        self.w = None
        self.r = []
        self.excl = excl


class Prog:
    ENGS = ("pe", "act", "dve", "pool", "sp")

    def __init__(self, nc, es):
        self.nc = nc
        self.es = es
        self.q = {k: [] for k in self.ENGS}
        self.sems = {}
        self.cnt = {}
        self.known = {k: {} for k in self.ENGS}
        self.nbank = 0
        self.nslot = 0
        self.nw = 3
        self.rings = {}

    def _sem(self, key):
        if key not in self.sems:
            self.sems[key] = self.es.enter_context(self.nc.semaphore("s_" + key))
            self.cnt[key] = 0
        return self.sems[key]

    def op(self, eng, fn, R=(), W=(), key=None, inc=1):
        deps = []
        for b in R:
            if b.w is not None:
                deps.append(b.w)
            if b.excl:
                deps.extend(b.r)
        for b in W:
            if b.w is not None:
                deps.append(b.w)
            deps.extend(b.r)
        waits = {}
        kn = self.known[eng]
        for (k, v) in deps:
            if k == "pe" and eng == "pe" and key is None:
                continue
            if kn.get(k, 0) >= v:
                continue
            if waits.get(k, 0) < v:
                waits[k] = v
        for k, v in waits.items():
            kn[k] = v
        if key is None:
            key = eng
        self._sem(key)
        self.cnt[key] += inc
        t = (key, self.cnt[key])
        self.q[eng].append((tuple(waits.items()), fn, key, inc))
        for b in R:
            b.r.append(t)
        for b in W:
            b.w = t
            b.r = []
        return t

    def barrier(self):
        snap = {k: v for k, v in self.cnt.items() if v > 0}
        for eng in self.ENGS:
            kn = self.known[eng]
            waits = tuple((k, v) for k, v in snap.items() if kn.get(k, 0) < v)
            for k, v in waits:
                kn[k] = v
            self.q[eng].append((waits, None, None, 0))

    def mm(self, out, lhsT, rhs, start, stop, R, W):
        return self.op("pe", lambda e: e.matmul(out, lhsT, rhs, start=start, stop=stop, skip_group_check=True), R, W)

    def tr(self, out, in_, ident, R, W):
        return self.op("pe", lambda e: e.transpose(out, in_, ident), R, W)

    def act(self, out, in_, func, R, W, bias=None, scale=None, accum_out=None):
        kw = {}
        if bias is not None:
            kw["bias"] = bias
        if scale is not None:
            kw["scale"] = scale
        if accum_out is not None:
            kw["accum_out"] = accum_out
        return self.op("act", lambda e: e.activation(out=out, in_=in_, func=func, **kw), R, W)

    def tt(self, eng, out, in0, in1, op, R, W):
        return self.op(eng, lambda e: e.tensor_tensor(out=out, in0=in0, in1=in1, op=op), R, W)

    def ts(self, eng, out, in0, s1, s2, op0, op1, R, W, accum_out=None):
        if accum_out is not None:
            return self.op(eng, lambda e: e.tensor_scalar(out=out, in0=in0, scalar1=s1, scalar2=s2, op0=op0, op1=op1,
                                                          accum_out=accum_out), R, W)
        if op1 is None:
            return self.op(eng, lambda e: e.tensor_scalar(out=out, in0=in0, scalar1=s1, scalar2=None, op0=op0), R, W)
        return self.op(eng, lambda e: e.tensor_scalar(out=out, in0=in0, scalar1=s1, scalar2=s2, op0=op0, op1=op1), R, W)

    def stt(self, out, in0, scalar, in1, op0, op1, R, W, accum_out=None):
        if accum_out is not None:
            return self.op("dve", lambda e: e.scalar_tensor_tensor(out=out, in0=in0, scalar=scalar, in1=in1, op0=op0,
                                                                   op1=op1, accum_out=accum_out), R, W)
        return self.op("dve", lambda e: e.scalar_tensor_tensor(out=out, in0=in0, scalar=scalar, in1=in1, op0=op0,
                                                               op1=op1), R, W)

    def copy(self, eng, out, in_, R, W):
        if eng == "act":
            return self.op("act", lambda e: e.copy(out=out, in_=in_), R, W)
        return self.op(eng, lambda e: e.tensor_copy(out=out, in_=in_), R, W)

    def recip(self, out, in_, R, W):
        return self.op("dve", lambda e: e.reciprocal(out=out, in_=in_), R, W)

    def memset(self, eng, ap, val, W):
        return self.op(eng, lambda e: e.memset(ap, val), (), W)

    def dma(self, eng, out, in_, R, W, key, slow=False):
        if key == "cld":
            self.ncld = getattr(self, "ncld", 0) + 1
            key = f"cld{self.ncld}"
        if slow:
            return self.op(eng, lambda e: e.dma_start(out=out, in_=in_, allow_slow_non_contiguous=True), R, W, key=key, inc=16)
        return self.op(eng, lambda e: e.dma_start(out=out, in_=in_), R, W, key=key, inc=16)

    def flush(self, block):
        engs = {"pe": block.tensor, "act": block.scalar, "dve": block.vector, "pool": block.gpsimd, "sp": block.sync}
        for name in self.ENGS:
            items = self.q[name]
            sems = self.sems
            final = []
            if name == "sp":
                final = [(k, self.cnt[k]) for k in self.cnt if k.startswith("out")]

            def body(e, items=items, final=final):
                for waits, fn, key, inc in items:
                    for k, v in waits:
                        e.wait_ge(sems[k], v)
                    if fn is not None:
                        fn(e).then_inc(sems[key], inc)
                for k, v in final:
                    e.wait_ge(sems[k], v)

            engs[name](body)


class Ring:
    def __init__(self, aps, name):
        self.aps = aps
        self.bufs = [Buf(f"{name}{i}") for i in range(len(aps))]
        self.i = 0

    def next(self):
        k = self.i % len(self.aps)
        self.i += 1
        return self.aps[k], self.bufs[k]


def build(mode="full", nheads=16, groups=None, stop_after=None):
    nc = bass.Bass("TRN2", target_bir_lowering=False)
    dr = {}

    def din(name, shape):
        dr[name] = nc.dram_tensor(name, list(shape), F32, kind="ExternalInput").ap()
        return dr[name]

    vec_d = din("vec", [128, NV])
    cst_d = din("cst", [128, NCST])
    adaw_d = [din("ada_w0", [D, 3 * D]), din("ada_w1", [D, 3 * D])]
    owin_d = din("o_w_in", [D, ODD_COLS])
    ws_d = din("ws", [128, 2048])
    bsb_d = din("bsb", [128, 2048])
    opw_d = din("o_pool_w", [2048, 512])
    owout_d = din("o_w_out", [2 * D, D])
    if mode in ("full", "L0"):
        xs_d = din("xs", [NT, D])
        ctx_d = din("ctxs", [NCTX, D])
        ewhd_d = din("e_w_hd", [16 * D, 512])
        ewmb_d = din("e_w_mb", [16 * D, 512])
        wab_d = din("w_ab", [D, 64])
        ewout_d = din("e_w_out", [2 * D, D])
    if mode == "L1":
        x1in_d = din("x1T_in", [128, 16 * NT])
    if mode == "L0":
        out_d = nc.dram_tensor("out", [128, 16 * NT], F32, kind="ExternalOutput").ap()
    else:
        out_d = nc.dram_tensor("out", [NT, D], F32, kind="ExternalOutput").ap()

    with ExitStack() as es:
        P = Prog(nc, es)

        def sb(name, shape, dt):
            return es.enter_context(nc.sbuf_tensor("sb_" + name, list(shape), dt))

        x1T_t = sb("x1T", [128, 16 * NT], F32)
        G1_t = sb("G1", [128, 8192], F32)
        G2_t = sb("G2", [128, 10240], F32)
        W_t = [sb(f"W{i}", [128, 16 * 512], BF16) for i in range(3)]
        cst = sb("cst", [128, C_BAND], F32)
        vec = sb("vec", [128, NV], F32)
        cbf = sb("cbf", [128, 12 * 128], BF16)
        mod_t = sb("mod", [128, 2 * 96], F32)
        der_t = sb("der", [128, 2 * 5 * 16], F32)
        small = sb("small", [128, 256], F32)
        scin = sb("scin", [128, 32], BF16)
        aux = sb("aux", [128, 2048], F32)
        mb_ring = Ring([aux[:, i * 512:(i + 1) * 512] for i in range(4)], "mb")
        ps_all = es.enter_context(nc.psum_tensor("ps_all", [128, 4096], F32))
        psum = [ps_all[:, i * 512:(i + 1) * 512] for i in range(8)]
        pbuf = [Buf(f"ps{i}", excl=True) for i in range(8)]

        x1T = x1T_t[:].rearrange("p (c t) -> p c t", c=16)
        x1b = [[Buf(f"x1_{dc}_{h}") for h in range(2)] for dc in range(16)]
        Wap = [w[:].rearrange("p (c n) -> p c n", c=16) for w in W_t]
        Wbuf = [Buf(f"W{i}") for i in range(3)]
        cstb = Buf("cst")
        vecb = Buf("vec")
        cbfb = Buf("cbf")
        modb = Buf("mod")
        derb = Buf("der")
        smallb = Buf("small")
        ident_f = cst[:, C_ID:C_ID + 128]
        ident_b = cbf[:, 0:128]
        nident_b = cbf[:, 128:256]
        ones_b = cbf[:, 256:384]
        band_b = [cbf[:, 384 + 128 * i: 512 + 128 * i] for i in range(4)]
        bm_b = [cbf[:, 896 + 128 * i: 1024 + 128 * i] for i in range(5)]

        P.bank_lo = 0

        def bank():
            n = 8 - P.bank_lo
            i = P.bank_lo + (P.nbank % n)
            P.nbank += 1
            return psum[i], pbuf[i]

        def banks(n):
            i = P.nbank % 8
            if i + n > 8:
                P.nbank += 8 - i
                i = 0
            P.nbank += n
            return ps_all[:, i * 512:(i + n) * 512], [pbuf[i + k] for k in range(n)]

        def wslot():
            i = P.nslot % P.nw
            P.nslot += 1
            return Wap[i], Wbuf[i], f"w{i}"

        def load_w(src2d, r0, nrow_chunks, c0, ncols):
            ap, b, key = wslot()
            src = src2d[r0:r0 + 128 * nrow_chunks, c0:c0 + ncols].rearrange("(c p) n -> p c n", p=128)
            P.dma("pool", ap[:, 0:nrow_chunks, 0:ncols], src, R=(), W=[b], key=key)
            return ap, b

        P.dma("sp", cst[:], cst_d[:, 0:C_BAND], (), [cstb], "cld")
        ctmp = G1_t[:, 0:NCST - C_BAND]
        ctmpb = Buf("ctmp")
        P.dma("sp", ctmp, cst_d[:, C_BAND:NCST], (), [ctmpb], "cld")
        P.dma("sp", vec[:], vec_d[:, :], (), [vecb], "cld")
        P.copy("dve", ident_b, ident_f, [cstb], [cbfb])
        P.ts("dve", nident_b, ident_f, -1.0, None, ALU.mult, None, [cstb], [cbfb])
        P.copy("dve", ones_b, ctmp[:, C_ONES - C_BAND:C_ONES - C_BAND + 128], [ctmpb], [cbfb])
        for i in range(4):
            P.copy("dve", band_b[i], ctmp[:, 128 * i:128 * (i + 1)], [ctmpb], [cbfb])
        for i in range(5):
            P.copy("dve", bm_b[i], ctmp[:, C_BM - C_BAND + 128 * i:C_BM - C_BAND + 128 * (i + 1)], [ctmpb], [cbfb])
        P.barrier()

        def emit_mod_gen(l):
            sc3 = scin[:].rearrange("p (c k) -> p c k", k=2)
            if l == 0:
                P.act(sc3[:, :, 0], vec[:, V_C:V_C + 16], AF.Silu, [vecb], [smallb])
                P.act(sc3[:, :, 1], vec[:, V_CC:V_CC + 16], AF.Silu, [vecb], [smallb])
            m3 = mod_t[:, l * 96:(l + 1) * 96].rearrange("p (g k) -> p g k", k=2)
            vab = V_AB0 if l == 0 else V_AB1
            for cb in range(12):
                wap, wb = load_w(adaw_d[l], 0, 16, cb * 512, 512)
                if l == 1:
                    yield
                mps, mpb = bank()
                for j in range(4):
                    for dc in range(16):
                        P.mm(mps[:, 2 * j:2 * j + 2], wap[:, dc, j * 128:(j + 1) * 128], sc3[:, dc, :],
                             dc == 0, dc == 15, [wb, smallb], [mpb])
                mp3 = mps[:, 0:8].rearrange("p (g k) -> p g k", k=2)
                for k in range(2):
                    P.tt("dve", m3[:, cb * 4:cb * 4 + 4, k], mp3[:, :, k], vec[:, vab + cb * 4:vab + cb * 4 + 4], ALU.add,
                         [mpb, vecb], [modb])
                yield
            vnw = V_NW0 if l == 0 else V_NW1
            dd = der_t[:, l * 80:(l + 1) * 80]
            P.stt(dd[:, 0:16], m3[:, 16:32, 0], 1.0, vec[:, vnw:vnw + 16], ALU.add, ALU.mult, [modb, vecb], [derb])
            P.copy("dve", dd[:, 16:32], m3[:, 0:16, 0], [modb], [derb])
            P.copy("dve", dd[:, 32:48], m3[:, 32:48, 0], [modb], [derb])
            P.stt(dd[:, 48:64], m3[:, 16:32, 1], 1.0, vec[:, vnw:vnw + 16], ALU.add, ALU.mult, [modb, vecb], [derb])
            P.copy("dve", dd[:, 64:80], m3[:, 0:16, 1], [modb], [derb])

        def emit_mod(l):
            for _ in emit_mod_gen(l):
                pass
            return der_t[:, l * 80:(l + 1) * 80]

        def emit_L1_prelude(mod_done=False):
            dd = der_t[:, 80:160] if mod_done else emit_mod(1)
            wsT_t = aux[:, 0:1024].bitcast(BF16)
            bias2_t = aux[:, 1024:2048].bitcast(BF16)
            wsTb = Buf("wsT")
            bias2b = Buf("bias2")
            wap, wb, key = wslot()
            ws_sb = wap[:, 0:4, :].rearrange("p c n -> p (c n)")
            P.dma("pool", ws_sb, ws_d[:, :], (), [wb], key)
            for q4 in range(4):
                ps, pb = bank()
                psb = ps[:].bitcast(BF16)
                for j in range(4):
                    g = q4 * 4 + j
                    P.tr(psb[:, j * 128:(j + 1) * 128], ws_sb[:, g * 128:(g + 1) * 128], ident_b, [wb, cbfb], [pb])
                P.copy("dve", wsT_t[:, q4 * 512:(q4 + 1) * 512], psb[:, 0:512], [pb], [wsTb])
            bs_sb = G1_t[:, 0:2048]
            g1b = Buf("g1tmp")
            P.dma("sp", bs_sb, bsb_d[:, :], (), [g1b], "cld")
            for q4 in range(4):
                ps, pb = bank()
                P.mm(ps[:, :], ones_b, wsT_t[:, q4 * 512:(q4 + 1) * 512], True, True, [cbfb, wsTb], [pb])
                for j in range(4):
                    g = q4 * 4 + j
                    P.stt(bias2_t[:, g * 128:(g + 1) * 128], ps[:, j * 128:(j + 1) * 128],
                          vec[:, V_LNB + g:V_LNB + g + 1], bs_sb[:, g * 128:(g + 1) * 128], ALU.mult, ALU.add,
                          [pb, vecb, g1b], [bias2b])
            return dd, wsT_t, wsTb, bias2_t, bias2b

        def emit_rstd_bc(src_of_dc, srcbufs_of_dc, ntok, tmp_ring, out_ap, outb, inv_n):
            nb = (ntok + 511) // 512
            for bi in range(nb):
                n0 = bi * 512
                n1 = min(ntok, n0 + 512)
                ps, pb = bank()
                for dc in range(16):
                    sq, sqb = tmp_ring.next()
                    P.act(sq[:, 0:n1 - n0], src_of_dc(dc)[:, n0:n1], AF.Square, srcbufs_of_dc(dc), [sqb])
                    P.mm(ps[:, 0:n1 - n0], ones_b, sq[:, 0:n1 - n0], dc == 0, dc == 15, [cbfb, sqb], [pb])
                P.ts("dve", out_ap[:, n0:n1], ps[:, 0:n1 - n0], inv_n, EPS, ALU.mult, ALU.add, [pb], [outb])
                P.act(out_ap[:, n0:n1], out_ap[:, n0:n1], AF.Sqrt, [outb], [outb])
                P.recip(out_ap[:, n0:n1], out_ap[:, n0:n1], [outb], [outb])

        def emit_L1_half(half, dd, wsT_t, wsTb, bias2_t, bias2b):
            T0 = half * 512
            h1T = G2_t[:, 0:4096].bitcast(BF16).rearrange("p (c t) -> p c t", c=16)
            vtok = G2_t[:, 4096:8192].bitcast(BF16).rearrange("p (t n) -> p t n", t=4)
            gat = G1_t[:, 0:4096].bitcast(BF16).rearrange("p (c t) -> p c t", c=16)
            tmpA = G1_t[:, 4096:8192]
            rs = G2_t[:, 8192:8704]
            misc = G2_t[:, 8704:10240]
            hb = [Buf(f"h1_{dc}") for dc in range(16)]
            vb_ = [Buf(f"vt_{t}") for t in range(4)]
            gb = [Buf(f"gat_{c}") for c in range(16)]
            rsb = Buf("rs")
            miscb = Buf("misc")
            sq_ring = Ring([tmpA[:, i * 256:(i + 1) * 256].bitcast(BF16) for i in range(3)], "sq")
            f_ring = Ring([tmpA[:, 768 + i * 512:768 + (i + 1) * 512] for i in range(6)], "f")
            emit_rstd_bc(lambda dc: x1T[:, dc, T0:T0 + 512], lambda dc: [x1b[dc][half]], 512, sq_ring, rs, rsb, 1.0 / D)
            for dc in range(16):
                t, tb = f_ring.next()
                P.tt("dve", t, x1T[:, dc, T0:T0 + 512], rs, ALU.mult, [x1b[dc][half], rsb], [tb])
                P.act(h1T[:, dc, :], t, AF.Identity, [tb, derb], [hb[dc]], bias=dd[:, 16 + dc:17 + dc],
                      scale=dd[:, dc:dc + 1])
            st = misc[:, 0:64]
            stb = [Buf(f"st{i}") for i in range(32)]
            for vbk in range(4):
                wap, wb = load_w(owin_d, 0, 16, 2048 + vbk * 512, 512)
                for t4 in range(4):
                    ps, pb = bank()
                    for dc in range(16):
                        P.mm(ps[:, :], h1T[:, dc, t4 * 128:(t4 + 1) * 128], wap[:, dc, :], dc == 0, dc == 15,
                             [hb[dc], wb], [pb])
                    vblk = vtok[:, t4, vbk * 512:(vbk + 1) * 512]
                    P.act(vblk, ps[:, :], AF.Gelu_apprx_tanh, [pb], [vb_[t4]])
                    j1, j1b = f_ring.next()
                    P.act(j1, vblk, AF.Square, [vb_[t4]], [j1b, stb[16 + t4 * 4 + vbk]],
                          accum_out=st[:, 16 + t4 * 4 + vbk:17 + t4 * 4 + vbk])
                    j2, j2b = f_ring.next()
                    P.ts("dve", j2, vblk, 1.0, 0.0, ALU.mult, ALU.add, [vb_[t4]], [j2b, stb[t4 * 4 + vbk]],
                         accum_out=st[:, t4 * 4 + vbk:t4 * 4 + vbk + 1])
            st3 = st[:, 0:32].rearrange("p (a t v) -> p a t v", a=2, v=4)
            red = misc[:, 64:72].rearrange("p (a t) -> p a t", a=2)
            P.tt("dve", red, st3[:, :, :, 0], st3[:, :, :, 1], ALU.add, stb, [miscb])
            P.tt("dve", red, red, st3[:, :, :, 2], ALU.add, [miscb], [miscb])
            P.tt("dve", red, red, st3[:, :, :, 3], ALU.add, [miscb], [miscb])
            mu = misc[:, 72:76]
            var = misc[:, 76:80]
            rstd = misc[:, 80:84]
            nmr = misc[:, 84:88]
            P.ts("dve", mu, red[:, 0, :], 1.0 / 2048, None, ALU.mult, None, [miscb], [miscb])
            P.ts("dve", var, red[:, 1, :], 1.0 / 2048, EPS, ALU.mult, ALU.add, [miscb], [miscb])
            P.tt("dve", nmr, mu, mu, ALU.mult, [miscb], [miscb])
            P.tt("dve", var, var, nmr, ALU.subtract, [miscb], [miscb])
            P.act(var, var, AF.Sqrt, [miscb], [miscb])
            P.recip(rstd, var, [miscb], [miscb])
            P.stt(nmr, mu, -1.0, rstd, ALU.mult, ALU.mult, [miscb], [miscb])
            for t4 in range(4):
                P.ts("dve", vtok[:, t4, :], vtok[:, t4, :], rstd[:, t4:t4 + 1], nmr[:, t4:t4 + 1], ALU.mult, ALU.add,
                     [vb_[t4], miscb], [vb_[t4]])
            for g4 in range(4):
                wu, wub = load_w(owin_d, 0, 16, g4 * 512, 512)
                wz, wzb = load_w(owin_d, 0, 16, 4096 + g4 * 512, 512)
                for j in range(4):
                    g = g4 * 4 + j
                    pu, pub = bank()
                    for dc in range(16):
                        P.mm(pu[:, :], wu[:, dc, j * 128:(j + 1) * 128], h1T[:, dc, :], dc == 0, dc == 15,
                             [wub, hb[dc]], [pub])
                    pz, pzb = bank()
                    for dc in range(16):
                        P.mm(pz[:, :], wz[:, dc, j * 128:(j + 1) * 128], h1T[:, dc, :], dc == 0, dc == 15,
                             [wzb, hb[dc]], [pzb])
                    pss, pssb = bank()
                    for t4 in range(4):
                        P.mm(pss[:, t4 * 128:(t4 + 1) * 128], vtok[:, t4, g * 128:(g + 1) * 128],
                             wsT_t[:, g * 128:(g + 1) * 128], True, True, [vb_[t4], wsTb], [pssb])
                    gu, gub = f_ring.next()
                    P.act(gu, pu[:, :], AF.Gelu_apprx_tanh, [pub], [gub])
                    sz, szb = f_ring.next()
                    P.act(sz, pz[:, :], AF.Silu, [pzb], [szb])
                    s2, s2b = f_ring.next()
                    b2 = bias2_t[:, g * 128:(g + 1) * 128]
                    for t4 in range(4):
                        P.stt(s2[:, t4 * 128:(t4 + 1) * 128], pss[:, t4 * 128:(t4 + 1) * 128],
                              vec[:, V_LNW + g:V_LNW + g + 1], b2, ALU.mult, ALU.add, [pssb, vecb, bias2b], [s2b])
                    P.tt("pool", gu, gu, sz, ALU.mult, [gub, szb], [gub])
                    P.tt("dve", gat[:, g, :], s2, gu, ALU.mult, [s2b, gub], [gb[g]])
            emit_wout_pass(owout_d, 0, gat, gb, half, dd)
            pt = G2_t[:, 4096:5120].bitcast(BF16).rearrange("p (t n) -> p t n", t=4)
            dfT = G2_t[:, 5120:6144].bitcast(BF16).rearrange("p (c t) -> p c t", c=4)
            ptb = [Buf(f"pt_{t}") for t in range(4)]
            dfb = [Buf(f"df_{c}") for c in range(4)]
            for pg in range(4):
                wp, wpb = load_w(owin_d, 0, 16, 6144 + pg * 512, 512)
                for t4 in range(4):
                    ps, pb = bank()
                    for dc in range(16):
                        P.mm(ps[:, :], h1T[:, dc, t4 * 128:(t4 + 1) * 128], wp[:, dc, :], dc == 0, dc == 15,
                             [hb[dc], wpb], [pb])
                    P.copy("act", pt[:, t4, 0:512], ps[:, :], [pb], [ptb[t4]] + (vb_ if pg == 0 else []))
                for cc in range(4):
                    ps, pb = bank()
                    for t4 in range(4):
                        P.mm(ps[:, t4 * 128:(t4 + 1) * 128], pt[:, t4, cc * 128:(cc + 1) * 128], band_b[pg], True, True,
                             [ptb[t4], cbfb], [pb])
                    P.copy("dve", dfT[:, cc, :], ps[:, :], [pb], [dfb[cc]] + (vb_ if pg == 0 else []))
                wz, wzb = load_w(owin_d, 0, 16, 8192 + pg * 512, 512)
                wq, wqb = load_w(opw_d, pg * 512, 4, 0, 512)
                for j in range(4):
                    g = pg * 4 + j
                    py, pyb = bank()
                    for cc in range(4):
                        P.mm(py[:, :], wq[:, cc, j * 128:(j + 1) * 128], dfT[:, cc, :], cc == 0, cc == 3,
                             [wqb, dfb[cc]], [pyb])
                    pz, pzb = bank()
                    for dc in range(16):
                        P.mm(pz[:, :], wz[:, dc, j * 128:(j + 1) * 128], h1T[:, dc, :], dc == 0, dc == 15,
                             [wzb, hb[dc]], [pzb])
                    sz, szb = f_ring.next()
                    P.act(sz, pz[:, :], AF.Silu, [pzb], [szb])
                    P.stt(gat[:, g, :], py[:, :], vec[:, V_PS + g:V_PS + g + 1], sz, ALU.mult, ALU.mult,
                          [pyb, vecb, szb], [gb[g]])
            emit_wout_pass(owout_d, 2048, gat, gb, half, dd)

        def emit_wout_pass(wsrc, r0, gat, gb, half, dd, first=False, ntok=512, tok0=None):
            T0 = half * 512 if tok0 is None else tok0
            for db in range(4):
                ww, wwb = load_w(wsrc, r0, 16, db * 512, 512)
                for j in range(4):
                    dch = db * 4 + j
                    for n0 in range(0, ntok, 512):
                        ps, pb = bank()
                        for fc in range(16):
                            P.mm(ps[:, :], ww[:, fc, j * 128:(j + 1) * 128], gat[:, fc, n0:n0 + 512], fc == 0, fc == 15,
                                 [wwb, gb[fc]], [pb])
                        hh = (T0 + n0) // 512
                        dst = x1T[:, dch, T0 + n0:T0 + n0 + 512]
                        if first:
                            P.ts("dve", dst, ps[:, :], dd[:, 32 + dch:33 + dch], None, ALU.mult, None, [pb, derb],
                                 [x1b[dch][hh]])
                        else:
                            P.stt(dst, ps[:, :], dd[:, 32 + dch:33 + dch], dst, ALU.mult, ALU.add, [pb, derb, x1b[dch][hh]],
                                  [x1b[dch][hh]])

        def emit_final(half):
            T0 = half * 512
            tmpA = G1_t[:, 0:4096]
            rs = G2_t[:, 8192:8704]
            rsb = Buf("rs_f")
            sq_ring = Ring([tmpA[:, i * 256:(i + 1) * 256].bitcast(BF16) for i in range(3)], "sqf")
            f_ring = Ring([tmpA[:, 768 + i * 512:768 + (i + 1) * 512] for i in range(4)], "ff")
            stg = [G2_t[:, 0:2048], G2_t[:, 2048:4096], G2_t[:, 4096:6144], G2_t[:, 6144:8192]]
            stgb = [Buf(f"stg{i}") for i in range(4)]
            emit_rstd_bc(lambda dc: x1T[:, dc, T0:T0 + 512], lambda dc: [x1b[dc][half]], 512, sq_ring, rs, rsb, 1.0 / D)
            xn = [None] * 16
            for d4 in range(4):
                tl = []
                for j in range(4):
                    dc = d4 * 4 + j
                    t, tb = f_ring.next()
                    P.stt(t, x1T[:, dc, T0:T0 + 512], vec[:, V_FNW + dc:V_FNW + dc + 1], rs, ALU.mult, ALU.mult,
                          [x1b[dc][half], vecb, rsb], [tb])
                    tl.append((t, tb))
                for t4 in range(4):
                    ps, pb = bank()
                    for j in range(4):
                        P.tr(ps[:, j * 128:(j + 1) * 128], tl[j][0][:, t4 * 128:(t4 + 1) * 128], ident_f,
                             [tl[j][1], cstb], [pb])
                    eng = "act" if (t4 % 2 == 0) else "dve"
                    P.copy(eng, stg[t4][:, d4 * 512:(d4 + 1) * 512], ps[:, :], [pb], [stgb[t4]])
            for t4 in range(4):
                r = T0 + t4 * 128
                P.dma("sp", out_d[r:r + 128, :], stg[t4], [stgb[t4]], [], f"out{t4}")


        def emit_L0(with_mod1=False):
            P.nw = 3
            dd = emit_mod(0)
            P.barrier()
            NTA = NCTX + NT
            hT = G2_t[:].bitcast(BF16).rearrange("p (c t) -> p c t", c=16)
            hTb = Buf("hT")
            gat = G1_t[:].bitcast(BF16).rearrange("p (c t) -> p c t", c=16)
            gb = [Buf(f"g0_{c}") for c in range(16)]
            XR = x1T_t
            XB = W_t[2][:].bitcast(F32)
            stg_ring = Ring([G1_t[:, i * 2048:(i + 1) * 2048] for i in range(4)], "xstg")
            ssr = small[:, 0:32]
            ssb = [Buf(f"ss{i}") for i in range(10)]
            for t in range(10):
                stg, stgb = stg_ring.next()
                src = ctx_d[t * 128:(t + 1) * 128, :] if t < 2 else xs_d[(t - 2) * 128:(t - 1) * 128, :]
                P.dma("sp", stg, src, (), [stgb], f"xl{t % 4}")
                junk = XR[:, 0:2048]
                junkb = Buf("junk")
                P.act(junk, stg, AF.Square, [stgb], [junkb, ssb[t]], accum_out=ssr[:, t:t + 1])
                P.ts("dve", ssr[:, t:t + 1], ssr[:, t:t + 1], 1.0 / D, EPS, ALU.mult, ALU.add, [ssb[t]], [ssb[t]])
                P.act(ssr[:, t:t + 1], ssr[:, t:t + 1], AF.Sqrt, [ssb[t]], [ssb[t]])
                P.recip(ssr[:, t:t + 1], ssr[:, t:t + 1], [ssb[t]], [ssb[t]])
                P.ts("dve", stg, stg, ssr[:, t:t + 1], None, ALU.mult, None, [stgb, ssb[t]], [stgb])
                so, bo = (48, 64) if t < 2 else (0, 16)
                for d4 in range(4):
                    ps, pb = bank()
                    for j in range(4):
                        dc = d4 * 4 + j
                        P.tr(ps[:, j * 128:(j + 1) * 128], stg[:, dc * 128:(dc + 1) * 128], ident_f, [stgb, cstb], [pb])
                    for j in range(4):
                        dc = d4 * 4 + j
                        P.act(hT[:, dc, t * 128:(t + 1) * 128], ps[:, j * 128:(j + 1) * 128], AF.Identity, [pb, derb], [hTb],
                              bias=dd[:, bo + dc:bo + dc + 1], scale=dd[:, so + dc:so + dc + 1])
            P.barrier()
            o = 0
            ob = 0
            def carve(n):
                nonlocal o
                a = XR[:, o:o + n]
                o += n
                assert o <= 16384, o
                return a
            def carveB(n):
                nonlocal ob
                a = XB[:, ob:ob + n]
                ob += n
                assert ob <= 4096, ob
                return a
            gc3 = carve(320).rearrange("p (c k) -> p c k", c=10)
            glb3 = carve(320).rearrange("p (c k) -> p c k", c=10)
            ngc3 = carve(320).rearrange("p (c k) -> p c k", c=10)
            bexp3 = carve(320).rearrange("p (c k) -> p c k", c=10)
            beta3 = carve(320).rearrange("p (c k) -> p c k", c=10)
            egl3 = carve(320).rearrange("p (c k) -> p c k", c=10)
            gl3 = carve(320).rearrange("p (c k) -> p c k", c=10)
            o_save = o
            o = 2240 + 7552
            Graw = carve(640).rearrange("p (c k) -> p c k", c=10)
            ones_f = carve(128)
            gtmp = carve(320).rearrange("p (c k) -> p c k", c=10)
            o = o_save
            gatesb = Buf("gates")
            P.memset("dve", ones_f, 1.0, [gatesb])
            P.memset("dve", small[:, 64:65], EPS, [smallb])
            wap, wb, key = wslot()
            P.dma("pool", wap[:, :, 0:64], wab_d[:, :].rearrange("(c p) n -> p c n", p=128), (), [wb], key)
            pg2, pg2b = banks(2)
            for t in range(10):
                for dc in range(16):
                    P.mm(pg2[:, t * 64:(t + 1) * 64], hT[:, dc, t * 128:(t + 1) * 128], wap[:, dc, 0:64], dc == 0, dc == 15,
                         [hTb, wb], [pg2b[(t * 64) // 512]])
            pg3 = pg2[:, 0:640].rearrange("p (c k) -> p c k", c=10)
            dtb_bc = vec[:, V_DTB:V_DTB + 32]
            nA = small[:, 32:64]
            P.act(nA, vec[:, V_ALOG:V_ALOG + 32], AF.Exp, [vecb], [smallb])
            P.ts("dve", nA, nA, -1.0, None, ALU.mult, None, [smallb], [smallb])
            for t in range(10):
                P.tt("dve", Graw[:, t, 0:32], pg3[:, t, 0:32], dtb_bc, ALU.add, pg2b + [vecb], [gatesb])
            P.act(Graw[:, :, 0:32], Graw[:, :, 0:32], AF.Exp, [gatesb], [gatesb])
            P.act(Graw[:, :, 0:32], Graw[:, :, 0:32], AF.Ln, [gatesb], [gatesb], bias=1.0)
            for t in range(10):
                P.tt("dve", Graw[:, t, 0:32], Graw[:, t, 0:32], nA, ALU.mult, [gatesb, smallb], [gatesb])
            P.act(Graw[:, :, 32:64], pg3[:, :, 32:64], AF.Exp, pg2b, [gatesb], scale=-1.0)
            P.act(Graw[:, :, 32:64], Graw[:, :, 32:64], AF.Ln, [gatesb], [gatesb], bias=1.0)
            P.ts("dve", Graw[:, :, 32:64], Graw[:, :, 32:64], -1.0, None, ALU.mult, None, [gatesb], [gatesb])
            pcs_, pcsb = bank()
            ptt_, pttb = bank()
            pc3 = pcs_[:, 0:320].rearrange("p (c k) -> p c k", c=10)
            pt3 = ptt_[:, 0:320].rearrange("p (c k) -> p c k", c=10)
            tri_a = cst[:, C_TA:C_TA + 128]
            tri_d = cst[:, C_TD:C_TD + 128]
            for t in range(10):
                P.mm(pc3[:, t, 0:16], tri_a, Graw[:, t, 0:16], True, True, [cstb, gatesb], [pcsb])
                P.mm(pc3[:, t, 16:32], tri_d, Graw[:, t, 16:32], True, True, [cstb, gatesb], [pcsb])
                P.mm(pt3[:, t, :], ones_f, Graw[:, t, 0:32], True, True, [gatesb], [pttb])
            P.copy("act", gc3, pc3, [pcsb], [gatesb])
            P.ts("dve", ngc3, pc3, -1.0, None, ALU.mult, None, [pcsb], [gatesb])
            P.tt("dve", glb3, pc3, Graw[:, :, 32:64], ALU.add, [pcsb, gatesb], [gatesb])
            P.act(bexp3, glb3, AF.Exp, [gatesb], [gatesb])
            P.act(beta3, Graw[:, :, 32:64], AF.Exp, [gatesb], [gatesb])
            P.tt("dve", gtmp, pt3, gc3, ALU.subtract, [pttb, gatesb], [gatesb])
            P.act(egl3, gtmp, AF.Exp, [gatesb], [gatesb])
            P.act(gl3, pt3, AF.Exp, [pttb], [gatesb])
            P.barrier()
            slots = []
            for i in range(2):
                sl = {}
                sl["kqT"] = carve(1280).bitcast(BF16).rearrange("p (c w t) -> p c w t", c=10, w=2)
                sl["ktok"] = carve(640).bitcast(BF16).rearrange("p (c d) -> p c d", c=10)
                sl["vtok"] = carve(640).bitcast(BF16).rearrange("p (c d) -> p c d", c=10)
                sl["zAs"] = carve(512).bitcast(BF16)
                sl["o1"] = carve(512).bitcast(BF16)
                sl["S"] = carve(128)
                sl["Sb"] = carve(64).bitcast(BF16)
                for nm in ("kqT", "ktok", "vtok", "zAs", "o1", "S", "Sb"):
                    sl[nm + "_b"] = Buf(f"{nm}{i}")
                slots.append(sl)
            oc = 0
            def carveC(n):
                nonlocal oc
                a = aux[:, oc:oc + n]
                oc += n
                assert oc <= 2048, oc
                return a
            CH = 1664
            chain_bases = [carve(CH), carve(CH), carve(CH), carveB(CH), carveC(CH)]
            vn_ring = [Ring([carve(64).bitcast(BF16) for _ in range(2)], f"vn{p}") for p in range(2)]
            fsq = carve(512).bitcast(BF16)
            fsqb = Buf("fsq")
            oacc = carveB(1024)
            oaccb = Buf("oacc")
            Gt = carveB(256)
            Gtb = Buf("Gt")

            def mk_chain_slot(base, nm):
                bfv = lambda a: a.bitcast(BF16)
                cs = dict(
                    em=base[:, 0:384], xyrtA=bfv(base[:, 0:256]), ab=bfv(base[:, 256:384]),
                    F=bfv(base[:, 384:576]), r1t1=bfv(base[:, 384:512]), a2=bfv(base[:, 512:576]),
                    egc=bfv(base[:, 576:640]), kbg=bfv(base[:, 576:640]),
                    y0=bfv(base[:, 640:704]), nw=bfv(base[:, 640:704]),
                    xa=bfv(base[:, 704:832]), msk=bfv(base[:, 832:1152]), xyrtB=bfv(base[:, 1152:1408]),
                    rf=bfv(base[:, 1408:1472]), vb=bfv(base[:, 1472:1536]), kg=bfv(base[:, 1536:1600]),
                    qg=bfv(base[:, 1600:1664]), buf=Buf("chain_" + nm))
                return cs
            chain_slots = [[mk_chain_slot(chain_bases[i], f"p0_{i}") for i in range(3)],
                           [mk_chain_slot(chain_bases[3], "p1_0"), mk_chain_slot(chain_bases[4], "p1_1")]]
            for j, cs_ in enumerate(chain_slots[0] + chain_slots[1]):
                cs_["pcs"] = psum[j // 2][:, (j % 2) * 256:(j % 2) * 256 + 256]
                cs_["pcb"] = pbuf[j // 2]
            cin_d = [nc.dram_tensor(f"cin{h}", [128, 128], F32) for h in range(16)]
            cout_d = [nc.dram_tensor(f"cout{h}", [256, 128], F32) for h in range(16)]
            cinb = [Buf(f"cin{h}") for h in range(16)]
            coutb = [Buf(f"cout{h}") for h in range(16)]
            BLK = ((0, 512), (512, 1024), (1024, 1280))

            cvq, cvk, cvv = chain_bases[0][:, 0:1280], chain_bases[1][:, 0:1280], chain_bases[2][:, 0:1280]
            sqq = chain_bases[3][:, 0:640].bitcast(BF16)
            sqk = chain_bases[3][:, 640:1280].bitcast(BF16)
            vTt = chain_bases[4][:, 0:640].bitcast(BF16)

            head_w = {}

            def load_head_w(h):
                head_w[h] = load_w(ewhd_d, h * D, 16, 0, 512)

            def prep_head(h):
                sl = slots[h % 2]
                wap, wb = head_w.pop(h)

                def proj_conv(which, cv, cvb):
                    ps3, pb3 = banks(3)
                    for bi, (n0, n1) in enumerate(BLK):
                        for dc in range(16):
                            P.mm(ps3[:, n0:n1], wap[:, dc, which * 128:(which + 1) * 128], hT[:, dc, n0:n1], dc == 0, dc == 15,
                                 [wb, hTb], [pb3[bi]])
                    grp = which * 16 + h
                    w0 = vec[:, V_CQ + grp * 3:V_CQ + grp * 3 + 1]
                    w1 = vec[:, V_CQ + grp * 3 + 1:V_CQ + grp * 3 + 2]
                    w2 = vec[:, V_CQ + grp * 3 + 2:V_CQ + grp * 3 + 3]
                    P.act(cv, ps3[:, 0:NTA], AF.Identity, pb3 + [vecb], [cvb], scale=w1)
                    P.stt(cv[:, 1:256], ps3[:, 0:255], w0, cv[:, 1:256], ALU.mult, ALU.add, pb3 + [vecb, cvb], [cvb])
                    P.stt(cv[:, 0:255], ps3[:, 1:256], w2, cv[:, 0:255], ALU.mult, ALU.add, pb3 + [vecb, cvb], [cvb])
                    cvx = cv[:, 256:NTA].rearrange("p (r t) -> p r t", t=64)
                    psx = ps3[:, 256:NTA].rearrange("p (r t) -> p r t", t=64)
                    P.stt(cvx[:, :, 1:64], psx[:, :, 0:63], w0, cvx[:, :, 1:64], ALU.mult, ALU.add, pb3 + [vecb, cvb], [cvb])
                    P.stt(cvx[:, :, 0:63], psx[:, :, 1:64], w2, cvx[:, :, 0:63], ALU.mult, ALU.add, pb3 + [vecb, cvb], [cvb])

                def to_tokmajor(src_T, srcb, dst, dstb):
                    pa, pab = bank()
                    pa_b = pa[:].bitcast(BF16)
                    for c in range(8):
                        P.tr(pa_b[:, c * 128:(c + 1) * 128], src_T(c), ident_b, [srcb, cbfb], [pab])
                    P.copy("act", dst[:, 0:8, :], pa_b[:, 0:1024].rearrange("p (c d) -> p c d", c=8), [pab], [dstb])
                    pa2, pa2b = bank()
                    pa2_b = pa2[:].bitcast(BF16)
                    for c in range(8, 10):
                        P.tr(pa2_b[:, (c - 8) * 128:(c - 7) * 128], src_T(c), ident_b, [srcb, cbfb], [pa2b])
                    P.copy("dve", dst[:, 8:10, :], pa2_b[:, 0:256].rearrange("p (c d) -> p c d", c=2), [pa2b], [dstb])

                def chain_qk(which, cv, sq):
                    cvb, sqb = Buf("cv"), Buf("sq")
                    proj_conv(which, cv, cvb)
                    yield
                    P.act(cv, cv, AF.Silu, [cvb], [cvb])
                    P.tt("pool", sq, cv, cv, ALU.mult, [cvb], [sqb])
                    yield
                    ss3, ssb3 = banks(3)
                    for bi, (n0, n1) in enumerate(BLK):
                        P.mm(ss3[:, n0:n1], ones_b, sq[:, n0:n1], True, True, [cbfb, sqb], [ssb3[bi]])
                    P.act(ss3[:, 0:NTA], ss3[:, 0:NTA], AF.Ln, ssb3 + [smallb], ssb3, bias=small[:, 64:65])
                    P.act(ss3[:, 0:NTA], ss3[:, 0:NTA], AF.Exp, ssb3, ssb3, scale=-0.5)
                    cv3 = cv.rearrange("p (c t) -> p c t", c=10)
                    ri3 = ss3[:, 0:NTA].rearrange("p (c t) -> p c t", c=10)
                    if which == 0:
                        P.stt(sl["kqT"][:, :, 1, :], cv3, 128.0 ** -0.5, ri3, ALU.mult, ALU.mult, [cvb] + ssb3, [sl["kqT_b"]])
                    else:
                        P.tt("dve", sl["kqT"][:, :, 0, :], cv3, ri3, ALU.mult, [cvb] + ssb3, [sl["kqT_b"]])
                        yield
                        to_tokmajor(lambda c: sl["kqT"][:, c, 0, :], sl["kqT_b"], sl["ktok"], sl["ktok_b"])

                def chain_v():
                    cvb, vTb = Buf("cvv"), Buf("vT")
                    proj_conv(2, cvv, cvb)
                    yield
                    P.act(vTt, cvv, AF.Silu, [cvb], [vTb])
                    yield
                    to_tokmajor(lambda c: vTt[:, c * 128:(c + 1) * 128], vTb, sl["vtok"], sl["vtok_b"])

                def chain_z():
                    pz2, pz2b = banks(2)
                    for bi in range(2):
                        n0 = 256 + bi * 512
                        for dc in range(16):
                            P.mm(pz2[:, bi * 512:(bi + 1) * 512], wap[:, dc, 384:512], hT[:, dc, n0:n0 + 512], dc == 0, dc == 15,
                                 [wb, hTb], [pz2b[bi]])
                    P.act(sl["zAs"], pz2[:, 0:1024], AF.Silu, pz2b, [sl["zAs_b"]])
                    P.memset("pool", sl["S"], 0.0, [sl["S_b"]])
                    P.memset("pool", sl["Sb"], 0.0, [sl["Sb_b"]])
                    return
                    yield

                gens = [chain_qk(0, cvq, sqq), chain_qk(1, cvk, sqk), chain_v(), chain_z()]
                while gens:
                    keep = []
                    for g in gens:
                        try:
                            next(g)
                            keep.append(g)
                        except StopIteration:
                            pass
                    gens = keep

            def intra_gen(h, ph, c, cs):
                sl = slots[h % 2]
                gcol = ph * 16 + h
                kq = sl["kqT"]
                kqb = sl["kqT_b"]
                cb = cs["buf"]
                pcs, pcb = cs["pcs"], cs["pcb"]
                P.mm(pcs[:, 0:256], kq[:, c, 0, :], kq[:, c, :, :].rearrange("p w t -> p (w t)"), True, True, [kqb], [pcb])
                pes, peb = bank()
                P.tr(pes[:, 0:128], glb3[:, c, gcol:gcol + 1].to_broadcast([128, 128]), ident_f, [gatesb, cstb], [peb])
                P.tr(pes[:, 128:256], gc3[:, c, gcol:gcol + 1].to_broadcast([128, 128]), ident_f, [gatesb, cstb], [peb])
                P.tr(pes[:, 256:384], gc3[:, c, gcol:gcol + 1].to_broadcast([128, 128]), ident_f, [gatesb, cstb], [peb])
                mbase = C_MA if ph == 0 else C_MD
                P.act(cs["egc"], pes[:, 128:256], AF.Exp, [peb], [cb])
                P.tt("dve", cs["em"], pes[:, 0:384], cst[:, mbase:mbase + 384], ALU.add, [peb, cstb], [cb])
                P.tt("pool", cs["qg"], kq[:, c, 1, :], cs["egc"], ALU.mult, [kqb], [cb])
                yield
                P.act(cs["F"][:, 0:256], cs["em"][:, 0:256], AF.Exp, [gatesb], [cb], bias=ngc3[:, c, gcol:gcol + 1])
                P.act(cs["F"][:, 256:384], cs["em"][:, 256:384], AF.Exp, [gatesb], [cb], bias=glb3[:, c, gcol:gcol + 1],
                      scale=-1.0)
                yield
                P.tt("dve", cs["xa"], pcs[:, 0:256], cs["F"][:, 0:256], ALU.mult, [pcb], [cb])
                P.tt("dve", cs["y0"], pcs[:, 0:128], cs["F"][:, 256:384], ALU.mult, [pcb], [cb])
                P.tt("pool", cs["vb"], sl["vtok"][:, c, :], beta3[:, c, gcol:gcol + 1].to_broadcast([128, 128]), ALU.mult,
                     [sl["vtok_b"], gatesb], [cb])
                P.tt("pool", cs["kg"], sl["ktok"][:, c, :], egl3[:, c, gcol:gcol + 1].to_broadcast([128, 128]), ALU.mult,
                     [sl["ktok_b"], gatesb], [cb])
                yield
                msk = cs["msk"]
                m1x, m1y, m2y = (bm_b[2], bm_b[1], bm_b[3]) if ph == 0 else (bm_b[1], bm_b[2], bm_b[4])
                P.tt("pool", msk[:, 0:128], cs["xa"][:, 0:128], bm_b[0], ALU.mult, [cbfb], [cb])
                P.tt("pool", msk[:, 128:256], cs["y0"], bm_b[0], ALU.mult, [cbfb], [cb])
                yield
                P.tt("pool", msk[:, 256:384], cs["xa"][:, 0:128], m1x, ALU.mult, [cbfb], [cb])
                P.tt("pool", msk[:, 384:512], cs["y0"], m1y, ALU.mult, [cbfb], [cb])
                P.tt("pool", msk[:, 512:640], cs["y0"], m2y, ALU.mult, [cbfb], [cb])
                Xk, Yk = msk[:, 0:128], msk[:, 128:256]
                Rk = ident_b
                for k in range(5):
                    prs, prb = bank()
                    if k <= 3:
                        P.mm(prs[:, 0:128], Yk, Xk, True, True, [cb], [prb])
                        P.mm(prs[:, 128:256], Xk, Yk, True, True, [cb], [prb])
                    P.mm(prs[:, 256:384], ident_b, Rk, True, False, [cbfb, cb], [prb])
                    P.mm(prs[:, 256:384], Yk, nident_b if k == 0 else Rk, False, True, [cb, cbfb], [prb])
                    nx = cs["xyrtA"] if k % 2 == 0 else cs["xyrtB"]
                    lo = 0 if k <= 3 else 256
                    P.copy("act" if k % 2 == 0 else "dve", nx[:, lo:384], prs[:, lo:384], [prb], [cb])
                    Xk, Yk, Rk = nx[:, 0:128], nx[:, 128:256], nx[:, 256:384]
                    yield
                ptr_, ptrb = bank()
                ptr_b = ptr_[:].bitcast(BF16)
                P.tr(ptr_b[:, 0:128], Rk, ident_b, [cb, cbfb], [ptrb])
                P.copy("dve", cs["xyrtA"][:, 384:512], ptr_b[:, 0:128], [ptrb], [cb])
                Tk = cs["xyrtA"][:, 384:512]
                yield
                pl, plb = bank()
                P.mm(pl[:, 0:128], msk[:, 384:512], Rk, True, True, [cb], [plb])
                P.mm(pl[:, 128:256], msk[:, 256:384], Tk, True, True, [cb], [plb])
                P.copy("act", cs["ab"], pl[:, 0:256], [plb], [cb])
                yield
                pl, plb = bank()
                P.mm(pl[:, 0:128], Tk, cs["ab"][:, 0:128], True, True, [cb], [plb])
                P.mm(pl[:, 128:256], Rk, cs["ab"][:, 128:256], True, True, [cb], [plb])
                P.tt("dve", cs["r1t1"], cs["xyrtA"][:, 256:512], pl[:, 0:256], ALU.subtract, [plb], [cb])
                yield
                pl, plb = bank()
                P.mm(pl[:, 0:128], msk[:, 512:640], cs["r1t1"][:, 0:128], True, True, [cb], [plb])
                P.copy("act", cs["a2"], pl[:, 0:128], [plb], [cb])
                yield
                pl, plb = bank()
                P.mm(pl[:, 0:128], cs["r1t1"][:, 128:256], cs["a2"], True, True, [cb], [plb])
                P.tt("dve", cs["rf"], cs["r1t1"][:, 0:128], pl[:, 0:128], ALU.subtract, [plb], [cb])
                P.tt("pool", cs["kbg"], sl["ktok"][:, c, :], bexp3[:, c, gcol:gcol + 1].to_broadcast([128, 128]), ALU.mult,
                     [sl["ktok_b"], gatesb], [cb])
                yield
                pw, pwb = bank()
                P.mm(pw[:, 0:128], cs["kbg"], cs["rf"], True, True, [cb], [pwb])
                P.act(cs["nw"], pw[:, 0:128], AF.Identity, [pwb], [cb], scale=-1.0)

            def scan_gen(h, ph, c, cs, last):
                sl = slots[h % 2]
                gcol = ph * 16 + h
                cb = cs["buf"]
                S, Sb_ = sl["S"], sl["S_b"]
                Sb, Sbb = sl["Sb"], sl["Sb_b"]
                pv, pvb = bank()
                P.mm(pv[:, 0:128], cs["rf"], cs["vb"], True, False, [cb], [pvb])
                P.mm(pv[:, 0:128], cs["nw"], Sb, False, True, [cb, Sbb], [pvb])
                vn, vnb = vn_ring[ph].next()
                P.copy("act", vn, pv[:, 0:128], [pvb], [vnb])
                yield
                po, pob = bank()
                if c >= 2:
                    P.mm(po[:, 0:128], Sb, cs["qg"], True, False, [Sbb, cb], [pob])
                    P.mm(po[:, 0:128], vn, cs["xa"][:, 128:256], False, True, [vnb, cb], [pob])
                P.mm(po[:, 128:256], cs["kg"], vn, True, True, [cb, vnb], [pob])
                P.stt(S, S, gl3[:, c, gcol:gcol + 1], po[:, 128:256], ALU.mult, ALU.add, [Sb_, gatesb, pob], [Sb_])
                if not last:
                    P.copy("act", Sb, S, [Sb_], [Sbb])
                if c >= 2:
                    tk = (c - 2) * 128
                    if ph == 0:
                        P.copy("dve", sl["o1"][:, tk:tk + 128], po[:, 0:128], [pob], [sl["o1_b"]])
                    else:
                        P.tt("dve", oacc[:, tk:tk + 128], po[:, 0:128], sl["o1"][:, tk:tk + 128], ALU.add,
                             [pob, sl["o1_b"]], [oaccb])

            def run_block(phases):
                st = []
                for (h, ph) in phases:
                    order = list(range(10)) if ph == 0 else list(range(9, 1, -1))
                    st.append(dict(h=h, ph=ph, order=order, istart=0, idone=set(), sdone=0, scan=None, intras=[]))
                while True:
                    progressed = False
                    for p in st:
                        n = len(p["order"])
                        ns = len(chain_slots[p["ph"]])
                        while p["istart"] < n and p["istart"] - p["sdone"] < ns:
                            i = p["istart"]
                            g = intra_gen(p["h"], p["ph"], p["order"][i], chain_slots[p["ph"]][i % ns])
                            p["intras"].append((i, g))
                            p["istart"] += 1
                        if p["scan"] is None and p["sdone"] < n and p["sdone"] in p["idone"]:
                            i = p["sdone"]
                            if i == 0 and p["ph"] == 1:
                                exchange_finish(p["h"])
                            p["scan"] = scan_gen(p["h"], p["ph"], p["order"][i], chain_slots[p["ph"]][i % ns], i == n - 1)
                    for p in st:
                        keep = []
                        for (i, g) in p["intras"]:
                            try:
                                next(g)
                                keep.append((i, g))
                            except StopIteration:
                                p["idone"].add(i)
                            progressed = True
                        p["intras"] = keep
                        if p["scan"] is not None:
                            try:
                                next(p["scan"])
                            except StopIteration:
                                p["scan"] = None
                                p["sdone"] += 1
                            progressed = True
                    if not progressed:
                        break

            def exchange(h):
                sl = slots[h % 2]
                P.dma("sp", cin_d[h][:, :], sl["S"], [sl["S_b"]], [cinb[h]], f"xi{h}")
                P.op("pool", lambda e, h=h: e.collective_compute(
                    "AllGather", ALU.bypass, replica_groups=groups or [[0, 1], [2, 3], [4, 5], [6, 7]],
                    ins=[cin_d[h].ap().opt()], outs=[cout_d[h].ap().opt()]), [cinb[h]], [coutb[h]], key="cc", inc=1)
                P.dma("sp", Gt.rearrange("p (r n) -> p r n", r=2), cout_d[h][:, :].rearrange("(r p) n -> p r n", p=128),
                      [coutb[h]], [Gtb], f"xo{h}")

            def exchange_finish(h):
                sl = slots[h % 2]
                P.ts("dve", sl["S"], Gt[:, 0:128], vec[:, V_SEL:V_SEL + 1], None, ALU.mult, None, [Gtb, vecb], [sl["S_b"]])
                P.stt(sl["S"], Gt[:, 128:256], vec[:, V_SEL + 1:V_SEL + 2], sl["S"], ALU.mult, ALU.add,
                      [Gtb, vecb, sl["S_b"]], [sl["S_b"]])
                P.copy("act", sl["Sb"], sl["S"], [sl["S_b"]], [sl["Sb_b"]])

            def finish_head(h):
                sl = slots[h % 2]
                sq, sqb = fsq, fsqb
                P.tt("pool", sq[:, 0:1024], oacc, oacc, ALU.mult, [oaccb], [sqb])
                ss2, ssb2 = banks(2)
                for bi in range(2):
                    P.mm(ss2[:, bi * 512:(bi + 1) * 512], ones_b, sq[:, bi * 512:(bi + 1) * 512], True, True, [cbfb, sqb],
                         [ssb2[bi]])
                P.act(ss2[:, 0:1024], ss2[:, 0:1024], AF.Ln, ssb2 + [smallb], ssb2, bias=small[:, 64:65], scale=1.0 / 128)
                P.act(ss2[:, 0:1024], ss2[:, 0:1024], AF.Exp, ssb2, ssb2, scale=-0.5)
                P.stt(oacc, oacc, vec[:, V_HN:V_HN + 1], ss2[:, 0:1024], ALU.mult, ALU.mult, [oaccb, vecb] + ssb2, [oaccb])
                P.tt("pool", gat[:, h, :], oacc, sl["zAs"], ALU.mult, [oaccb, sl["zAs_b"]], [gb[h]])

            mod1_gen = emit_mod_gen(1) if with_mod1 else iter(())
            P.nw = 2
            P.nslot = 0
            load_head_w(0)
            for h in range(nheads + 1):
                if h < nheads:
                    prep_head(h)
                    P.barrier()
                if h >= 1:
                    exchange(h - 1)
                next(mod1_gen, None)
                if h + 1 < nheads:
                    load_head_w(h + 1)
                ph_list = ([(h, 0)] if h < nheads else []) + ([(h - 1, 1)] if h >= 1 else [])
                P.bank_lo = 3
                run_block(ph_list)
                P.bank_lo = 0
                if h >= 1:
                    finish_head(h - 1)
                next(mod1_gen, None)
                P.barrier()
            for _ in mod1_gen:
                pass
            P.barrier()
            P.nw = 3
            P.nslot = 0
            if stop_after == "gdn":
                return
            emit_wout_pass(ewout_d, 0, gat, gb, 0, dd, first=True, ntok=1024, tok0=0)
            P.barrier()
            emit_mixer_B(dd, hT, hTb, gat, gb)
            emit_wout_pass(ewout_d, 2048, gat, gb, 0, dd, first=False, ntok=1024, tok0=0)
            P.barrier()
            stg_ring2 = Ring([G2_t[:, i * 2048:(i + 1) * 2048] for i in range(4)], "xstg2")
            for t in range(8):
                stg, stgb = stg_ring2.next()
                P.dma("sp", stg, xs_d[t * 128:(t + 1) * 128, :], (), [stgb], f"xl{t % 4}")
                for d4 in range(4):
                    ps, pb = bank()
                    for j in range(4):
                        dc = d4 * 4 + j
                        P.tr(ps[:, j * 128:(j + 1) * 128], stg[:, dc * 128:(dc + 1) * 128], ident_f, [stgb, cstb], [pb])
                    for j in range(4):
                        dc = d4 * 4 + j
                        dst = x1T[:, dc, t * 128:(t + 1) * 128]
                        P.tt("dve", dst, ps[:, j * 128:(j + 1) * 128], dst, ALU.add, [pb, x1b[dc][t // 4]], [x1b[dc][t // 4]])
            P.barrier()
            P.nw = 3
            P.nslot = 0

        def emit_mixer_B(dd, hT, hTb, gat, gb):
            BASE = 6144 + 2048 + 64
            for cg in range(16):
                wap, wb = load_w(ewmb_d, cg * D, 16, 0, 512)
                w0 = vec[:, V_CB + cg * 3:V_CB + cg * 3 + 1]
                w1 = vec[:, V_CB + cg * 3 + 1:V_CB + cg * 3 + 2]
                w2 = vec[:, V_CB + cg * 3 + 2:V_CB + cg * 3 + 3]
                for half in range(2):
                    n0 = 256 + half * 512
                    pp = []
                    for j in range(4):
                        ps, pb = bank()
                        for dc in range(16):
                            P.mm(ps[:, :], wap[:, dc, j * 128:(j + 1) * 128], hT[:, dc, n0:n0 + 512], dc == 0, dc == 15,
                                 [wb, hTb], [pb])
                        pp.append((ps, pb))
                    (pbg, pbgb), (pcg, pcgb), (phb, phbb), (pzb, pzbb) = pp
                    cgs, cgsb = mb_ring.next()
                    P.copy("act", cgs, pcg[:, :], [pcgb], [cgsb])
                    t1, t1b = mb_ring.next()
                    P.tt("dve", t1, phb[:, :], cgs, ALU.mult, [phbb, cgsb], [t1b])
                    cv, cvb = mb_ring.next()
                    P.act(cv, t1, AF.Identity, [t1b, vecb], [cvb], scale=w1)
                    cv3 = cv.rearrange("p (r t) -> p r t", t=64)
                    t13 = t1.rearrange("p (r t) -> p r t", t=64)
                    P.stt(cv3[:, :, 1:64], t13[:, :, 0:63], w0, cv3[:, :, 1:64], ALU.mult, ALU.add, [t1b, vecb, cvb], [cvb])
                    P.stt(cv3[:, :, 0:63], t13[:, :, 1:64], w2, cv3[:, :, 0:63], ALU.mult, ALU.add, [t1b, vecb, cvb], [cvb])
                    sz, szb = mb_ring.next()
                    P.act(sz, pzb[:, :], AF.Silu, [pzbb], [szb])
                    P.tt("dve", cv, pbg[:, :], cv, ALU.mult, [pbgb, cvb], [cvb])
                    P.tt("pool", gat[:, cg, half * 512:(half + 1) * 512], cv, sz, ALU.mult, [cvb, szb], [gb[cg]])

        if mode == "L0":
            emit_L0()
            for dc in range(16):
                P.dma("sp", out_d[:, dc * NT:(dc + 1) * NT], x1T[:, dc, :], [x1b[dc][0], x1b[dc][1]], [], f"out{dc % 4}")
        if mode == "full":
            emit_L0(True)
            pre = emit_L1_prelude(True)
            for half in range(2):
                P.barrier()
                emit_L1_half(half, *pre)
            for half in range(2):
                P.barrier()
                emit_final(half)
        if mode == "L1":
            x1all = Buf("x1all")
            for dc in range(16):
                P.dma("sp", x1T[:, dc, :], x1in_d[:, dc * NT:(dc + 1) * NT], (), [x1b[dc][0], x1b[dc][1]], "cld")
            pre = emit_L1_prelude()
            for half in range(2):
                P.barrier()
                emit_L1_half(half, *pre)
            for half in range(2):
                P.barrier()
                emit_final(half)

        with nc.Block() as block:
            P.flush(block)
    return nc


def make_consts():
    c = np.zeros((128, NCST), np.float32)
    idx = np.arange(128)
    c[:, C_ID:C_ID + 128] = np.eye(128)
    c[:, C_TA:C_TA + 128] = (idx[:, None] <= idx[None, :])
    c[:, C_TD:C_TD + 128] = (idx[:, None] >= idx[None, :])
    P_, F_ = idx[:, None], idx[None, :]
    for base, asc in ((C_MA, True), (C_MD, False)):
        if asc:
            m1 = F_ > P_; m2 = F_ >= P_; m3 = P_ > F_
        else:
            m1 = F_ < P_; m2 = F_ <= P_; m3 = P_ < F_
        c[:, base:base + 128] = np.where(m1, 0.0, -BIG)
        c[:, base + 128:base + 256] = np.where(m2, 0.0, -BIG)
        c[:, base + 256:base + 384] = np.where(m3, 0.0, BIG)
    for k, r in enumerate((1, 2, 4, 8)):
        Dm = np.zeros((128, 128), np.float64)
        for i in range(128):
            row = i // 64
            lo = max(i - r, row * 64)
            hi = min(i + r + 1, row * 64 + 64)
            Dm[i, lo:hi] = 1.0 / (hi - lo)
            Dm[i, i] -= 1.0
        c[:, C_BAND + 128 * k:C_BAND + 128 * (k + 1)] = Dm.T
    c[:, C_ONES:C_ONES + 128] = 1.0
    pb, fb = P_ // 32, F_ // 32
    m1_lo = ((pb == 1) & (fb == 0)) | ((pb == 3) & (fb == 2))
    m2_lo = (P_ >= 64) & (F_ < 64)
    for i, m in enumerate((pb == fb, m1_lo, m1_lo.T, m2_lo, m2_lo.T)):
        c[:, C_BM + 128 * i:C_BM + 128 * (i + 1)] = m
    return c


def fm(v, n):
    return np.ascontiguousarray(np.asarray(v, np.float32).reshape(n, 128).T)


def make_vec(inp, b, s):
    v = np.zeros((128, NV), np.float32)
    v[:, V_C:V_C + 16] = fm(inp["c"][b], 16)
    v[:, V_CC:V_CC + 16] = fm(inp["c_ctx"], 16)
    v[:, V_AB0:V_AB0 + 48] = fm(inp["ada_b"][0], 48)
    v[:, V_AB1:V_AB1 + 48] = fm(inp["ada_b"][1], 48)
    v[:, V_NW0:V_NW0 + 16] = fm(inp["norm_w"][0], 16)
    v[:, V_NW1:V_NW1 + 16] = fm(inp["norm_w"][1], 16)
    v[:, V_LNW:V_LNW + 16] = fm(inp["o_ln_w"][0], 16)
    v[:, V_LNB:V_LNB + 16] = fm(inp["o_ln_b"][0], 16)
    v[:, V_PS:V_PS + 16] = fm(inp["o_pool_scale"][0], 16)
    v[:, V_FNW:V_FNW + 16] = fm(inp["final_norm_w"], 16)
    cq = np.asarray(inp["e_conv_qkv"][0], np.float32)
    cb = np.asarray(inp["e_conv_b"][0], np.float32)
    if s == 1:
        cq = cq[::-1]
        cb = cb[::-1]
    v[:, V_CQ:V_CQ + 144] = np.stack([fm(cq[t], 48) for t in range(3)], axis=2).reshape(128, 144)
    v[:, V_CB:V_CB + 48] = np.stack([fm(cb[t], 16) for t in range(3)], axis=2).reshape(128, 48)
    v[:, V_HN] = np.asarray(inp["e_head_norm"][0], np.float32)
    v[:, V_SEL] = 1.0 if s == 1 else 0.0
    v[:, V_SEL + 1] = 1.0 if s == 0 else 0.0
    dirs = (0, 1) if s == 0 else (1, 0)
    dtb = np.asarray(inp["e_dt_bias"][0], np.float32)
    alog = np.asarray(inp["e_a_log"][0], np.float32)
    v[:, V_DTB:V_DTB + 32] = np.concatenate([dtb[dirs[0]], dtb[dirs[1]]])[None, :]
    v[:, V_ALOG:V_ALOG + 32] = np.concatenate([alog[dirs[0]], alog[dirs[1]]])[None, :]
    return v


def common_maps(inp):
    f = lambda a: np.ascontiguousarray(np.asarray(a, np.float32))
    return {
        "cst": make_consts(),
        "ada_w0": f(inp["ada_w"][0]), "ada_w1": f(inp["ada_w"][1]),
        "o_w_in": f(inp["o_w_in"][0]),
        "o_pool_w": f(np.asarray(inp["o_pool_w"][0]).reshape(2048, 512)),
        "o_w_out": f(inp["o_w_out"][0]),
    }


def core_maps_L1(inp, b, s):
    ws = np.asarray(inp["o_w_s"][0], np.float32)
    bs = np.asarray(inp["o_b_s"][0], np.float32)
    if s == 1:
        ws = ws[:, ::-1, ::-1]
        bs = bs[:, ::-1]
    return {
        "vec": make_vec(inp, b, s),
        "ws": np.ascontiguousarray(ws.transpose(1, 0, 2).reshape(128, 2048)),
        "bsb": np.ascontiguousarray(np.broadcast_to(bs.reshape(1, 2048), (128, 2048))),
    }


_NC_CACHE = {}


def kernel(**inputs):
    inp = {k: np.asarray(v) for k, v in inputs.items()}
    if "full" not in _NC_CACHE:
        _NC_CACHE["full"] = build("full")
    nc = _NC_CACHE["full"]
    com = common_maps(inp)
    com.update(common_maps_L0(inp))
    maps = []
    for core in range(8):
        b, s = core // 2, core % 2
        m = dict(com)
        m.update(core_maps_L1(inp, b, s))
        m.update(core_maps_L0(inp, b, s))
        maps.append(m)
    res = run_bass_kernel_spmd(nc, maps, core_ids=list(range(8)))
    out = np.empty((4, 2048, D), np.float32)
    for core in range(8):
        b, s = core // 2, core % 2
        o = np.asarray(res.results[core]["out"], np.float32)
        if s == 1:
            o = o[::-1]
        out[b, s * NT:(s + 1) * NT] = o
    return out


def common_maps_L0(inp):
    f = lambda a: np.ascontiguousarray(np.asarray(a, np.float32))
    w = np.asarray(inp["e_w_in"][0], np.float32)
    hd = np.empty((16, D, 512), np.float32)
    mb = np.empty((16, D, 512), np.float32)
    base = 6144 + 2048 + 64
    for h in range(16):
        for j, c0 in enumerate((h * 128, 2048 + h * 128, 4096 + h * 128, 6144 + h * 128)):
            hd[h, :, j * 128:(j + 1) * 128] = w[:, c0:c0 + 128]
        for j in range(4):
            c0 = base + j * 2048 + h * 128
            mb[h, :, j * 128:(j + 1) * 128] = w[:, c0:c0 + 128]
    return {"e_w_hd": hd.reshape(16 * D, 512), "e_w_mb": mb.reshape(16 * D, 512), "e_w_out": f(inp["e_w_out"][0])}


def core_maps_L0(inp, b, s):
    xs = np.asarray(inp["x"][b, s * NT:(s + 1) * NT], np.float32)
    cx = np.asarray(inp["ctx"][b], np.float32)
    if s == 1:
        xs = xs[::-1]
        cx = cx[::-1]
    d1, d2 = (0, 1) if s == 0 else (1, 0)
    base = 8192
    cols = np.concatenate([np.arange(base + d1 * 16, base + d1 * 16 + 16), np.arange(base + d2 * 16, base + d2 * 16 + 16),
                           np.arange(base + 32 + d1 * 16, base + 32 + d1 * 16 + 16),
                           np.arange(base + 32 + d2 * 16, base + 32 + d2 * 16 + 16)])
    wab = np.asarray(inp["e_w_in"][0], np.float32)[:, cols]
    return {"xs": np.ascontiguousarray(xs), "ctxs": np.ascontiguousarray(cx), "w_ab": np.ascontiguousarray(wab)}
```

```python
import numpy as np
from contextlib import ExitStack
import concourse.bass as bass
import concourse.mybir as mybir
from concourse.bass_utils import run_bass_kernel_spmd

F32 = mybir.dt.float32
BF16 = mybir.dt.bfloat16
AF = mybir.ActivationFunctionType
ALU = mybir.AluOpType

D = 2048
NT = 1024
NCTX = 256
EPS = 1e-6
BIG = 30000.0
EVEN_COLS = 16448
ODD_COLS = 10240

C_ID, C_TA, C_TD, C_MA, C_MD, C_BAND, C_ONES, C_BM, NCST = 0, 128, 256, 384, 768, 1152, 1664, 1792, 2432
V_C, V_CC, V_AB0, V_AB1, V_NW0, V_NW1, V_LNW, V_LNB, V_PS, V_FNW = 0, 16, 32, 80, 128, 144, 160, 176, 192, 208
V_CQ, V_CB, V_HN, V_SEL, V_DTB, V_ALOG, NV = 224, 368, 416, 417, 419, 451, 512


class Buf:
    __slots__ = ("name", "w", "r", "excl")

    def __init__(self, name, excl=False):
        self.name = name
        self.w = None
        self.r = []
        self.excl = excl


class Prog:
    ENGS = ("pe", "act", "dve", "pool", "sp")

    def __init__(self, nc, es):
        self.nc = nc
        self.es = es
        self.q = {k: [] for k in self.ENGS}
        self.sems = {}
        self.cnt = {}
        self.known = {k: {} for k in self.ENGS}
        self.nbank = 0
        self.nslot = 0
        self.nw = 3
        self.rings = {}

    def _sem(self, key):
        if key not in self.sems:
            self.sems[key] = self.es.enter_context(self.nc.semaphore("s_" + key))
            self.cnt[key] = 0
        return self.sems[key]

    def op(self, eng, fn, R=(), W=(), key=None, inc=1):
        deps = []
        for b in R:
            if b.w is not None:
                deps.append(b.w)
            if b.excl:
                deps.extend(b.r)
        for b in W:
            if b.w is not None:
                deps.append(b.w)
            deps.extend(b.r)
        waits = {}
        kn = self.known[eng]
        for (k, v) in deps:
            if k == "pe" and eng == "pe" and key is None:
                continue
            if kn.get(k, 0) >= v:
                continue
            if waits.get(k, 0) < v:
                waits[k] = v
        for k, v in waits.items():
            kn[k] = v
        if key is None:
            key = eng
        self._sem(key)
        self.cnt[key] += inc
        t = (key, self.cnt[key])
        self.q[eng].append((tuple(waits.items()), fn, key, inc))
        for b in R:
            b.r.append(t)
        for b in W:
            b.w = t
            b.r = []
        return t

    def barrier(self):
        snap = {k: v for k, v in self.cnt.items() if v > 0}
        for eng in self.ENGS:
            kn = self.known[eng]
            waits = tuple((k, v) for k, v in snap.items() if kn.get(k, 0) < v)
            for k, v in waits:
                kn[k] = v
            self.q[eng].append((waits, None, None, 0))

    def mm(self, out, lhsT, rhs, start, stop, R, W):
        return self.op("pe", lambda e: e.matmul(out, lhsT, rhs, start=start, stop=stop, skip_group_check=True), R, W)

    def tr(self, out, in_, ident, R, W):
        return self.op("pe", lambda e: e.transpose(out, in_, ident), R, W)

    def act(self, out, in_, func, R, W, bias=None, scale=None, accum_out=None):
        kw = {}
        if bias is not None:
            kw["bias"] = bias
        if scale is not None:
            kw["scale"] = scale
        if accum_out is not None:
            kw["accum_out"] = accum_out
        return self.op("act", lambda e: e.activation(out=out, in_=in_, func=func, **kw), R, W)

    def tt(self, eng, out, in0, in1, op, R, W):
        return self.op(eng, lambda e: e.tensor_tensor(out=out, in0=in0, in1=in1, op=op), R, W)

    def ts(self, eng, out, in0, s1, s2, op0, op1, R, W, accum_out=None):
        if accum_out is not None:
            return self.op(eng, lambda e: e.tensor_scalar(out=out, in0=in0, scalar1=s1, scalar2=s2, op0=op0, op1=op1,
                                                          accum_out=accum_out), R, W)
        if op1 is None:
            return self.op(eng, lambda e: e.tensor_scalar(out=out, in0=in0, scalar1=s1, scalar2=None, op0=op0), R, W)
        return self.op(eng, lambda e: e.tensor_scalar(out=out, in0=in0, scalar1=s1, scalar2=s2, op0=op0, op1=op1), R, W)

    def stt(self, out, in0, scalar, in1, op0, op1, R, W, accum_out=None):
        if accum_out is not None:
            return self.op("dve", lambda e: e.scalar_tensor_tensor(out=out, in0=in0, scalar=scalar, in1=in1, op0=op0,
                                                                   op1=op1, accum_out=accum_out), R, W)
        return self.op("dve", lambda e: e.scalar_tensor_tensor(out=out, in0=in0, scalar=scalar, in1=in1, op0=op0,
                                                               op1=op1), R, W)

    def copy(self, eng, out, in_, R, W):
        if eng == "act":
            return self.op("act", lambda e: e.copy(out=out, in_=in_), R, W)
        return self.op(eng, lambda e: e.tensor_copy(out=out, in_=in_), R, W)

    def recip(self, out, in_, R, W):
        return self.op("dve", lambda e: e.reciprocal(out=out, in_=in_), R, W)

    def memset(self, eng, ap, val, W):
        return self.op(eng, lambda e: e.memset(ap, val), (), W)

    def dma(self, eng, out, in_, R, W, key, slow=False):
        if key == "cld":
            self.ncld = getattr(self, "ncld", 0) + 1
            key = f"cld{self.ncld}"
        if slow:
            return self.op(eng, lambda e: e.dma_start(out=out, in_=in_, allow_slow_non_contiguous=True), R, W, key=key, inc=16)
        return self.op(eng, lambda e: e.dma_start(out=out, in_=in_), R, W, key=key, inc=16)

    def flush(self, block):
        engs = {"pe": block.tensor, "act": block.scalar, "dve": block.vector, "pool": block.gpsimd, "sp": block.sync}
        for name in self.ENGS:
            items = self.q[name]
            sems = self.sems
            final = []
            if name == "sp":
                final = [(k, self.cnt[k]) for k in self.cnt if k.startswith("out")]

            def body(e, items=items, final=final):
                for waits, fn, key, inc in items:
                    for k, v in waits:
                        e.wait_ge(sems[k], v)
                    if fn is not None:
                        fn(e).then_inc(sems[key], inc)
                for k, v in final:
                    e.wait_ge(sems[k], v)

            engs[name](body)


class Ring:
    def __init__(self, aps, name):
        self.aps = aps
        self.bufs = [Buf(f"{name}{i}") for i in range(len(aps))]
        self.i = 0

    def next(self):
        k = self.i % len(self.aps)
        self.i += 1
        return self.aps[k], self.bufs[k]


def build(mode="full", nheads=16, groups=None, stop_after=None):
    nc = bass.Bass("TRN2", target_bir_lowering=False)
    dr = {}

    def din(name, shape):
        dr[name] = nc.dram_tensor(name, list(shape), F32, kind="ExternalInput").ap()
        return dr[name]

    vec_d = din("vec", [128, NV])
    cst_d = din("cst", [128, NCST])
    adaw_d = [din("ada_w0", [D, 3 * D]), din("ada_w1", [D, 3 * D])]
    owin_d = din("o_w_in", [D, ODD_COLS])
    ws_d = din("ws", [128, 2048])
    bsb_d = din("bsb", [128, 2048])
    opw_d = din("o_pool_w", [2048, 512])
    owout_d = din("o_w_out", [2 * D, D])
    if mode in ("full", "L0"):
        xs_d = din("xs", [NT, D])
        ctx_d = din("ctxs", [NCTX, D])
        ewhd_d = din("e_w_hd", [16 * D, 512])
        ewmb_d = din("e_w_mb", [16 * D, 512])
        wab_d = din("w_ab", [D, 64])
        ewout_d = din("e_w_out", [2 * D, D])
    if mode == "L1":
        x1in_d = din("x1T_in", [128, 16 * NT])
    if mode == "L0":
        out_d = nc.dram_tensor("out", [128, 16 * NT], F32, kind="ExternalOutput").ap()
    else:
        out_d = nc.dram_tensor("out", [NT, D], F32, kind="ExternalOutput").ap()

    with ExitStack() as es:
        P = Prog(nc, es)

        def sb(name, shape, dt):
            return es.enter_context(nc.sbuf_tensor("sb_" + name, list(shape), dt))

        x1T_t = sb("x1T", [128, 16 * NT], F32)
        G1_t = sb("G1", [128, 8192], F32)
        G2_t = sb("G2", [128, 10240], F32)
        W_t = [sb(f"W{i}", [128, 16 * 512], BF16) for i in range(3)]
        cst = sb("cst", [128, C_BAND], F32)
        vec = sb("vec", [128, NV], F32)
        cbf = sb("cbf", [128, 12 * 128], BF16)
        mod_t = sb("mod", [128, 2 * 96], F32)
        der_t = sb("der", [128, 2 * 5 * 16], F32)
        small = sb("small", [128, 256], F32)
        scin = sb("scin", [128, 32], BF16)
        aux = sb("aux", [128, 2048], F32)
        mb_ring = Ring([aux[:, i * 512:(i + 1) * 512] for i in range(4)], "mb")
        ps_all = es.enter_context(nc.psum_tensor("ps_all", [128, 4096], F32))
        psum = [ps_all[:, i * 512:(i + 1) * 512] for i in range(8)]
        pbuf = [Buf(f"ps{i}", excl=True) for i in range(8)]

        x1T = x1T_t[:].rearrange("p (c t) -> p c t", c=16)
        x1b = [[Buf(f"x1_{dc}_{h}") for h in range(2)] for dc in range(16)]
        Wap = [w[:].rearrange("p (c n) -> p c n", c=16) for w in W_t]
        Wbuf = [Buf(f"W{i}") for i in range(3)]
        cstb = Buf("cst")
        vecb = Buf("vec")
        cbfb = Buf("cbf")
        modb = Buf("mod")
        derb = Buf("der")
        smallb = Buf("small")
        ident_f = cst[:, C_ID:C_ID + 128]
        ident_b = cbf[:, 0:128]
        nident_b = cbf[:, 128:256]
        ones_b = cbf[:, 256:384]
        band_b = [cbf[:, 384 + 128 * i: 512 + 128 * i] for i in range(4)]
        bm_b = [cbf[:, 896 + 128 * i: 1024 + 128 * i] for i in range(5)]

        P.bank_lo = 0

        def bank():
            n = 8 - P.bank_lo
            i = P.bank_lo + (P.nbank % n)
            P.nbank += 1
            return psum[i], pbuf[i]

        def banks(n):
            i = P.nbank % 8
            if i + n > 8:
                P.nbank += 8 - i
                i = 0
            P.nbank += n
            return ps_all[:, i * 512:(i + n) * 512], [pbuf[i + k] for k in range(n)]

        def wslot():
            i = P.nslot % P.nw
            P.nslot += 1
            return Wap[i], Wbuf[i], f"w{i}"

        def load_w(src2d, r0, nrow_chunks, c0, ncols):
            ap, b, key = wslot()
            src = src2d[r0:r0 + 128 * nrow_chunks, c0:c0 + ncols].rearrange("(c p) n -> p c n", p=128)
            P.dma("pool", ap[:, 0:nrow_chunks, 0:ncols], src, R=(), W=[b], key=key)
            return ap, b

        P.dma("sp", cst[:], cst_d[:, 0:C_BAND], (), [cstb], "cld")
        ctmp = G1_t[:, 0:NCST - C_BAND]
        ctmpb = Buf("ctmp")
        P.dma("sp", ctmp, cst_d[:, C_BAND:NCST], (), [ctmpb], "cld")
        P.dma("sp", vec[:], vec_d[:, :], (), [vecb], "cld")
        P.copy("dve", ident_b, ident_f, [cstb], [cbfb])
        P.ts("dve", nident_b, ident_f, -1.0, None, ALU.mult, None, [cstb], [cbfb])
        P.copy("dve", ones_b, ctmp[:, C_ONES - C_BAND:C_ONES - C_BAND + 128], [ctmpb], [cbfb])
        for i in range(4):
            P.copy("dve", band_b[i], ctmp[:, 128 * i:128 * (i + 1)], [ctmpb], [cbfb])
        for i in range(5):
            P.copy("dve", bm_b[i], ctmp[:, C_BM - C_BAND + 128 * i:C_BM - C_BAND + 128 * (i + 1)], [ctmpb], [cbfb])
        P.barrier()

        def emit_mod_gen(l):
            sc3 = scin[:].rearrange("p (c k) -> p c k", k=2)
            if l == 0:
                P.act(sc3[:, :, 0], vec[:, V_C:V_C + 16], AF.Silu, [vecb], [smallb])
                P.act(sc3[:, :, 1], vec[:, V_CC:V_CC + 16], AF.Silu, [vecb], [smallb])
            m3 = mod_t[:, l * 96:(l + 1) * 96].rearrange("p (g k) -> p g k", k=2)
            vab = V_AB0 if l == 0 else V_AB1
            for cb in range(12):
                wap, wb = load_w(adaw_d[l], 0, 16, cb * 512, 512)
                if l == 1:
                    yield
                mps, mpb = bank()
                for j in range(4):
                    for dc in range(16):
                        P.mm(mps[:, 2 * j:2 * j + 2], wap[:, dc, j * 128:(j + 1) * 128], sc3[:, dc, :],
                             dc == 0, dc == 15, [wb, smallb], [mpb])
                mp3 = mps[:, 0:8].rearrange("p (g k) -> p g k", k=2)
                for k in range(2):
                    P.tt("dve", m3[:, cb * 4:cb * 4 + 4, k], mp3[:, :, k], vec[:, vab + cb * 4:vab + cb * 4 + 4], ALU.add,
                         [mpb, vecb], [modb])
                yield
            vnw = V_NW0 if l == 0 else V_NW1
            dd = der_t[:, l * 80:(l + 1) * 80]
            P.stt(dd[:, 0:16], m3[:, 16:32, 0], 1.0, vec[:, vnw:vnw + 16], ALU.add, ALU.mult, [modb, vecb], [derb])
            P.copy("dve", dd[:, 16:32], m3[:, 0:16, 0], [modb], [derb])
            P.copy("dve", dd[:, 32:48], m3[:, 32:48, 0], [modb], [derb])
            P.stt(dd[:, 48:64], m3[:, 16:32, 1], 1.0, vec[:, vnw:vnw + 16], ALU.add, ALU.mult, [modb, vecb], [derb])
            P.copy("dve", dd[:, 64:80], m3[:, 0:16, 1], [modb], [derb])

        def emit_mod(l):
            for _ in emit_mod_gen(l):
                pass
            return der_t[:, l * 80:(l + 1) * 80]

        def emit_L1_prelude(mod_done=False):
            dd = der_t[:, 80:160] if mod_done else emit_mod(1)
            wsT_t = aux[:, 0:1024].bitcast(BF16)
            bias2_t = aux[:, 1024:2048].bitcast(BF16)
            wsTb = Buf("wsT")
            bias2b = Buf("bias2")
            wap, wb, key = wslot()
            ws_sb = wap[:, 0:4, :].rearrange("p c n -> p (c n)")
            P.dma("pool", ws_sb, ws_d[:, :], (), [wb], key)
            for q4 in range(4):
                ps, pb = bank()
                psb = ps[:].bitcast(BF16)
                for j in range(4):
                    g = q4 * 4 + j
                    P.tr(psb[:, j * 128:(j + 1) * 128], ws_sb[:, g * 128:(g + 1) * 128], ident_b, [wb, cbfb], [pb])
                P.copy("dve", wsT_t[:, q4 * 512:(q4 + 1) * 512], psb[:, 0:512], [pb], [wsTb])
            bs_sb = G1_t[:, 0:2048]
            g1b = Buf("g1tmp")
            P.dma("sp", bs_sb, bsb_d[:, :], (), [g1b], "cld")
            for q4 in range(4):
                ps, pb = bank()
                P.mm(ps[:, :], ones_b, wsT_t[:, q4 * 512:(q4 + 1) * 512], True, True, [cbfb, wsTb], [pb])
                for j in range(4):
                    g = q4 * 4 + j
                    P.stt(bias2_t[:, g * 128:(g + 1) * 128], ps[:, j * 128:(j + 1) * 128],
                          vec[:, V_LNB + g:V_LNB + g + 1], bs_sb[:, g * 128:(g + 1) * 128], ALU.mult, ALU.add,
                          [pb, vecb, g1b], [bias2b])
            return dd, wsT_t, wsTb, bias2_t, bias2b

        def emit_rstd_bc(src_of_dc, srcbufs_of_dc, ntok, tmp_ring, out_ap, outb, inv_n):
            nb = (ntok + 511) // 512
            for bi in range(nb):
                n0 = bi * 512
                n1 = min(ntok, n0 + 512)
                ps, pb = bank()
                for dc in range(16):
                    sq, sqb = tmp_ring.next()
                    P.act(sq[:, 0:n1 - n0], src_of_dc(dc)[:, n0:n1], AF.Square, srcbufs_of_dc(dc), [sqb])
                    P.mm(ps[:, 0:n1 - n0], ones_b, sq[:, 0:n1 - n0], dc == 0, dc == 15, [cbfb, sqb], [pb])
                P.ts("dve", out_ap[:, n0:n1], ps[:, 0:n1 - n0], inv_n, EPS, ALU.mult, ALU.add, [pb], [outb])
                P.act(out_ap[:, n0:n1], out_ap[:, n0:n1], AF.Sqrt, [outb], [outb])
                P.recip(out_ap[:, n0:n1], out_ap[:, n0:n1], [outb], [outb])

        def emit_L1_half(half, dd, wsT_t, wsTb, bias2_t, bias2b):
            T0 = half * 512
            h1T = G2_t[:, 0:4096].bitcast(BF16).rearrange("p (c t) -> p c t", c=16)
            vtok = G2_t[:, 4096:8192].bitcast(BF16).rearrange("p (t n) -> p t n", t=4)
            gat = G1_t[:, 0:4096].bitcast(BF16).rearrange("p (c t) -> p c t", c=16)
            tmpA = G1_t[:, 4096:8192]
            rs = G2_t[:, 8192:8704]
            misc = G2_t[:, 8704:10240]
            hb = [Buf(f"h1_{dc}") for dc in range(16)]
            vb_ = [Buf(f"vt_{t}") for t in range(4)]
            gb = [Buf(f"gat_{c}") for c in range(16)]
            rsb = Buf("rs")
            miscb = Buf("misc")
            sq_ring = Ring([tmpA[:, i * 256:(i + 1) * 256].bitcast(BF16) for i in range(3)], "sq")
            f_ring = Ring([tmpA[:, 768 + i * 512:768 + (i + 1) * 512] for i in range(6)], "f")
            emit_rstd_bc(lambda dc: x1T[:, dc, T0:T0 + 512], lambda dc: [x1b[dc][half]], 512, sq_ring, rs, rsb, 1.0 / D)
            for dc in range(16):
                t, tb = f_ring.next()
                P.tt("dve", t, x1T[:, dc, T0:T0 + 512], rs, ALU.mult, [x1b[dc][half], rsb], [tb])
                P.act(h1T[:, dc, :], t, AF.Identity, [tb, derb], [hb[dc]], bias=dd[:, 16 + dc:17 + dc],
                      scale=dd[:, dc:dc + 1])
            st = misc[:, 0:64]
            stb = [Buf(f"st{i}") for i in range(32)]
            for vbk in range(4):
                wap, wb = load_w(owin_d, 0, 16, 2048 + vbk * 512, 512)
                for t4 in range(4):
                    ps, pb = bank()
                    for dc in range(16):
                        P.mm(ps[:, :], h1T[:, dc, t4 * 128:(t4 + 1) * 128], wap[:, dc, :], dc == 0, dc == 15,
                             [hb[dc], wb], [pb])
                    vblk = vtok[:, t4, vbk * 512:(vbk + 1) * 512]
                    P.act(vblk, ps[:, :], AF.Gelu_apprx_tanh, [pb], [vb_[t4]])
                    j1, j1b = f_ring.next()
                    P.act(j1, vblk, AF.Square, [vb_[t4]], [j1b, stb[16 + t4 * 4 + vbk]],
                          accum_out=st[:, 16 + t4 * 4 + vbk:17 + t4 * 4 + vbk])
                    j2, j2b = f_ring.next()
                    P.ts("dve", j2, vblk, 1.0, 0.0, ALU.mult, ALU.add, [vb_[t4]], [j2b, stb[t4 * 4 + vbk]],
                         accum_out=st[:, t4 * 4 + vbk:t4 * 4 + vbk + 1])
            st3 = st[:, 0:32].rearrange("p (a t v) -> p a t v", a=2, v=4)
            red = misc[:, 64:72].rearrange("p (a t) -> p a t", a=2)
            P.tt("dve", red, st3[:, :, :, 0], st3[:, :, :, 1], ALU.add, stb, [miscb])
            P.tt("dve", red, red, st3[:, :, :, 2], ALU.add, [miscb], [miscb])
            P.tt("dve", red, red, st3[:, :, :, 3], ALU.add, [miscb], [miscb])
            mu = misc[:, 72:76]
            var = misc[:, 76:80]
            rstd = misc[:, 80:84]
            nmr = misc[:, 84:88]
            P.ts("dve", mu, red[:, 0, :], 1.0 / 2048, None, ALU.mult, None, [miscb], [miscb])
            P.ts("dve", var, red[:, 1, :], 1.0 / 2048, EPS, ALU.mult, ALU.add, [miscb], [miscb])
            P.tt("dve", nmr, mu, mu, ALU.mult, [miscb], [miscb])
            P.tt("dve", var, var, nmr, ALU.subtract, [miscb], [miscb])
            P.act(var, var, AF.Sqrt, [miscb], [miscb])
            P.recip(rstd, var, [miscb], [miscb])
            P.stt(nmr, mu, -1.0, rstd, ALU.mult, ALU.mult, [miscb], [miscb])
            for t4 in range(4):
                P.ts("dve", vtok[:, t4, :], vtok[:, t4, :], rstd[:, t4:t4 + 1], nmr[:, t4:t4 + 1], ALU.mult, ALU.add,
                     [vb_[t4], miscb], [vb_[t4]])
            for g4 in range(4):
                wu, wub = load_w(owin_d, 0, 16, g4 * 512, 512)
                wz, wzb = load_w(owin_d, 0, 16, 4096 + g4 * 512, 512)
                for j in range(4):
                    g = g4 * 4 + j
                    pu, pub = bank()
                    for dc in range(16):
                        P.mm(pu[:, :], wu[:, dc, j * 128:(j + 1) * 128], h1T[:, dc, :], dc == 0, dc == 15,
                             [wub, hb[dc]], [pub])
                    pz, pzb = bank()
                    for dc in range(16):
                        P.mm(pz[:, :], wz[:, dc, j * 128:(j + 1) * 128], h1T[:, dc, :], dc == 0, dc == 15,
                             [wzb, hb[dc]], [pzb])
                    pss, pssb = bank()
                    for t4 in range(4):
                        P.mm(pss[:, t4 * 128:(t4 + 1) * 128], vtok[:, t4, g * 128:(g + 1) * 128],
                             wsT_t[:, g * 128:(g + 1) * 128], True, True, [vb_[t4], wsTb], [pssb])
                    gu, gub = f_ring.next()
                    P.act(gu, pu[:, :], AF.Gelu_apprx_tanh, [pub], [gub])
                    sz, szb = f_ring.next()
                    P.act(sz, pz[:, :], AF.Silu, [pzb], [szb])
                    s2, s2b = f_ring.next()
                    b2 = bias2_t[:, g * 128:(g + 1) * 128]
                    for t4 in range(4):
                        P.stt(s2[:, t4 * 128:(t4 + 1) * 128], pss[:, t4 * 128:(t4 + 1) * 128],
                              vec[:, V_LNW + g:V_LNW + g + 1], b2, ALU.mult, ALU.add, [pssb, vecb, bias2b], [s2b])
                    P.tt("pool", gu, gu, sz, ALU.mult, [gub, szb], [gub])
                    P.tt("dve", gat[:, g, :], s2, gu, ALU.mult, [s2b, gub], [gb[g]])
            emit_wout_pass(owout_d, 0, gat, gb, half, dd)
            pt = G2_t[:, 4096:5120].bitcast(BF16).rearrange("p (t n) -> p t n", t=4)
            dfT = G2_t[:, 5120:6144].bitcast(BF16).rearrange("p (c t) -> p c t", c=4)
            ptb = [Buf(f"pt_{t}") for t in range(4)]
            dfb = [Buf(f"df_{c}") for c in range(4)]
            for pg in range(4):
                wp, wpb = load_w(owin_d, 0, 16, 6144 + pg * 512, 512)
                for t4 in range(4):
                    ps, pb = bank()
                    for dc in range(16):
                        P.mm(ps[:, :], h1T[:, dc, t4 * 128:(t4 + 1) * 128], wp[:, dc, :], dc == 0, dc == 15,
                             [hb[dc], wpb], [pb])
                    P.copy("act", pt[:, t4, 0:512], ps[:, :], [pb], [ptb[t4]] + (vb_ if pg == 0 else []))
                for cc in range(4):
                    ps, pb = bank()
                    for t4 in range(4):
                        P.mm(ps[:, t4 * 128:(t4 + 1) * 128], pt[:, t4, cc * 128:(cc + 1) * 128], band_b[pg], True, True,
                             [ptb[t4], cbfb], [pb])
                    P.copy("dve", dfT[:, cc, :], ps[:, :], [pb], [dfb[cc]] + (vb_ if pg == 0 else []))
                wz, wzb = load_w(owin_d, 0, 16, 8192 + pg * 512, 512)
                wq, wqb = load_w(opw_d, pg * 512, 4, 0, 512)
                for j in range(4):
                    g = pg * 4 + j
                    py, pyb = bank()
                    for cc in range(4):
                        P.mm(py[:, :], wq[:, cc, j * 128:(j + 1) * 128], dfT[:, cc, :], cc == 0, cc == 3,
                             [wqb, dfb[cc]], [pyb])
                    pz, pzb = bank()
                    for dc in range(16):
                        P.mm(pz[:, :], wz[:, dc, j * 128:(j + 1) * 128], h1T[:, dc, :], dc == 0, dc == 15,
                             [wzb, hb[dc]], [pzb])
                    sz, szb = f_ring.next()
                    P.act(sz, pz[:, :], AF.Silu, [pzb], [szb])
                    P.stt(gat[:, g, :], py[:, :], vec[:, V_PS + g:V_PS + g + 1], sz, ALU.mult, ALU.mult,
                          [pyb, vecb, szb], [gb[g]])
            emit_wout_pass(owout_d, 2048, gat, gb, half, dd)

        def emit_wout_pass(wsrc, r0, gat, gb, half, dd, first=False, ntok=512, tok0=None):
            T0 = half * 512 if tok0 is None else tok0
            for db in range(4):
                ww, wwb = load_w(wsrc, r0, 16, db * 512, 512)
                for j in range(4):
                    dch = db * 4 + j
                    for n0 in range(0, ntok, 512):
                        ps, pb = bank()
                        for fc in range(16):
                            P.mm(ps[:, :], ww[:, fc, j * 128:(j + 1) * 128], gat[:, fc, n0:n0 + 512], fc == 0, fc == 15,
                                 [wwb, gb[fc]], [pb])
                        hh = (T0 + n0) // 512
                        dst = x1T[:, dch, T0 + n0:T0 + n0 + 512]
                        if first:
                            P.ts("dve", dst, ps[:, :], dd[:, 32 + dch:33 + dch], None, ALU.mult, None, [pb, derb],
                                 [x1b[dch][hh]])
                        else:
                            P.stt(dst, ps[:, :], dd[:, 32 + dch:33 + dch], dst, ALU.mult, ALU.add, [pb, derb, x1b[dch][hh]],
                                  [x1b[dch][hh]])

        def emit_final(half):
            T0 = half * 512
            tmpA = G1_t[:, 0:4096]
            rs = G2_t[:, 8192:8704]
            rsb = Buf("rs_f")
            sq_ring = Ring([tmpA[:, i * 256:(i + 1) * 256].bitcast(BF16) for i in range(3)], "sqf")
            f_ring = Ring([tmpA[:, 768 + i * 512:768 + (i + 1) * 512] for i in range(4)], "ff")
            stg = [G2_t[:, 0:2048], G2_t[:, 2048:4096], G2_t[:, 4096:6144], G2_t[:, 6144:8192]]
            stgb = [Buf(f"stg{i}") for i in range(4)]
            emit_rstd_bc(lambda dc: x1T[:, dc, T0:T0 + 512], lambda dc: [x1b[dc][half]], 512, sq_ring, rs, rsb, 1.0 / D)
            xn = [None] * 16
            for d4 in range(4):
                tl = []
                for j in range(4):
                    dc = d4 * 4 + j
                    t, tb = f_ring.next()
                    P.stt(t, x1T[:, dc, T0:T0 + 512], vec[:, V_FNW + dc:V_FNW + dc + 1], rs, ALU.mult, ALU.mult,
                          [x1b[dc][half], vecb, rsb], [tb])
                    tl.append((t, tb))
                for t4 in range(4):
                    ps, pb = bank()
                    for j in range(4):
                        P.tr(ps[:, j * 128:(j + 1) * 128], tl[j][0][:, t4 * 128:(t4 + 1) * 128], ident_f,
                             [tl[j][1], cstb], [pb])
                    eng = "act" if (t4 % 2 == 0) else "dve"
                    P.copy(eng, stg[t4][:, d4 * 512:(d4 + 1) * 512], ps[:, :], [pb], [stgb[t4]])
            for t4 in range(4):
                r = T0 + t4 * 128
                P.dma("sp", out_d[r:r + 128, :], stg[t4], [stgb[t4]], [], f"out{t4}")


        def emit_L0(with_mod1=False):
            P.nw = 3
            dd = emit_mod(0)
            P.barrier()
            NTA = NCTX + NT
            hT = G2_t[:].bitcast(BF16).rearrange("p (c t) -> p c t", c=16)
            hTb = Buf("hT")
            gat = G1_t[:].bitcast(BF16).rearrange("p (c t) -> p c t", c=16)
            gb = [Buf(f"g0_{c}") for c in range(16)]
            XR = x1T_t
            XB = W_t[2][:].bitcast(F32)
            stg_ring = Ring([G1_t[:, i * 2048:(i + 1) * 2048] for i in range(4)], "xstg")
            ssr = small[:, 0:32]
            ssb = [Buf(f"ss{i}") for i in range(10)]
            for t in range(10):
                stg, stgb = stg_ring.next()
                src = ctx_d[t * 128:(t + 1) * 128, :] if t < 2 else xs_d[(t - 2) * 128:(t - 1) * 128, :]
                P.dma("sp", stg, src, (), [stgb], f"xl{t % 4}")
                junk = XR[:, 0:2048]
                junkb = Buf("junk")
                P.act(junk, stg, AF.Square, [stgb], [junkb, ssb[t]], accum_out=ssr[:, t:t + 1])
                P.ts("dve", ssr[:, t:t + 1], ssr[:, t:t + 1], 1.0 / D, EPS, ALU.mult, ALU.add, [ssb[t]], [ssb[t]])
                P.act(ssr[:, t:t + 1], ssr[:, t:t + 1], AF.Sqrt, [ssb[t]], [ssb[t]])
                P.recip(ssr[:, t:t + 1], ssr[:, t:t + 1], [ssb[t]], [ssb[t]])
                P.ts("dve", stg, stg, ssr[:, t:t + 1], None, ALU.mult, None, [stgb, ssb[t]], [stgb])
                so, bo = (48, 64) if t < 2 else (0, 16)
                for d4 in range(4):
                    ps, pb = bank()
                    for j in range(4):
                        dc = d4 * 4 + j
                        P.tr(ps[:, j * 128:(j + 1) * 128], stg[:, dc * 128:(dc + 1) * 128], ident_f, [stgb, cstb], [pb])
                    for j in range(4):
                        dc = d4 * 4 + j
                        P.act(hT[:, dc, t * 128:(t + 1) * 128], ps[:, j * 128:(j + 1) * 128], AF.Identity, [pb, derb], [hTb],
                              bias=dd[:, bo + dc:bo + dc + 1], scale=dd[:, so + dc:so + dc + 1])
            P.barrier()
            o = 0
            ob = 0
            def carve(n):
                nonlocal o
                a = XR[:, o:o + n]
                o += n
                assert o <= 16384, o
                return a
            def carveB(n):
                nonlocal ob
                a = XB[:, ob:ob + n]
                ob += n
                assert ob <= 4096, ob
                return a
            gc3 = carve(320).rearrange("p (c k) -> p c k", c=10)
            glb3 = carve(320).rearrange("p (c k) -> p c k", c=10)
            ngc3 = carve(320).rearrange("p (c k) -> p c k", c=10)
            bexp3 = carve(320).rearrange("p (c k) -> p c k", c=10)
            beta3 = carve(320).rearrange("p (c k) -> p c k", c=10)
            egl3 = carve(320).rearrange("p (c k) -> p c k", c=10)
            gl3 = carve(320).rearrange("p (c k) -> p c k", c=10)
            o_save = o
            o = 2240 + 7552
            Graw = carve(640).rearrange("p (c k) -> p c k", c=10)
            ones_f = carve(128)
            gtmp = carve(320).rearrange("p (c k) -> p c k", c=10)
            o = o_save
            gatesb = Buf("gates")
            P.memset("dve", ones_f, 1.0, [gatesb])
            P.memset("dve", small[:, 64:65], EPS, [smallb])
            wap, wb, key = wslot()
            P.dma("pool", wap[:, :, 0:64], wab_d[:, :].rearrange("(c p) n -> p c n", p=128), (), [wb], key)
            pg2, pg2b = banks(2)
            for t in range(10):
                for dc in range(16):
                    P.mm(pg2[:, t * 64:(t + 1) * 64], hT[:, dc, t * 128:(t + 1) * 128], wap[:, dc, 0:64], dc == 0, dc == 15,
                         [hTb, wb], [pg2b[(t * 64) // 512]])
            pg3 = pg2[:, 0:640].rearrange("p (c k) -> p c k", c=10)
            dtb_bc = vec[:, V_DTB:V_DTB + 32]
            nA = small[:, 32:64]
            P.act(nA, vec[:, V_ALOG:V_ALOG + 32], AF.Exp, [vecb], [smallb])
            P.ts("dve", nA, nA, -1.0, None, ALU.mult, None, [smallb], [smallb])
            for t in range(10):
                P.tt("dve", Graw[:, t, 0:32], pg3[:, t, 0:32], dtb_bc, ALU.add, pg2b + [vecb], [gatesb])
            P.act(Graw[:, :, 0:32], Graw[:, :, 0:32], AF.Exp, [gatesb], [gatesb])
            P.act(Graw[:, :, 0:32], Graw[:, :, 0:32], AF.Ln, [gatesb], [gatesb], bias=1.0)
            for t in range(10):
                P.tt("dve", Graw[:, t, 0:32], Graw[:, t, 0:32], nA, ALU.mult, [gatesb, smallb], [gatesb])
            P.act(Graw[:, :, 32:64], pg3[:, :, 32:64], AF.Exp, pg2b, [gatesb], scale=-1.0)
            P.act(Graw[:, :, 32:64], Graw[:, :, 32:64], AF.Ln, [gatesb], [gatesb], bias=1.0)
            P.ts("dve", Graw[:, :, 32:64], Graw[:, :, 32:64], -1.0, None, ALU.mult, None, [gatesb], [gatesb])
            pcs_, pcsb = bank()
            ptt_, pttb = bank()
            pc3 = pcs_[:, 0:320].rearrange("p (c k) -> p c k", c=10)
            pt3 = ptt_[:, 0:320].rearrange("p (c k) -> p c k", c=10)
            tri_a = cst[:, C_TA:C_TA + 128]
            tri_d = cst[:, C_TD:C_TD + 128]
            for t in range(10):
                P.mm(pc3[:, t, 0:16], tri_a, Graw[:, t, 0:16], True, True, [cstb, gatesb], [pcsb])
                P.mm(pc3[:, t, 16:32], tri_d, Graw[:, t, 16:32], True, True, [cstb, gatesb], [pcsb])
                P.mm(pt3[:, t, :], ones_f, Graw[:, t, 0:32], True, True, [gatesb], [pttb])
            P.copy("act", gc3, pc3, [pcsb], [gatesb])
            P.ts("dve", ngc3, pc3, -1.0, None, ALU.mult, None, [pcsb], [gatesb])
            P.tt("dve", glb3, pc3, Graw[:, :, 32:64], ALU.add, [pcsb, gatesb], [gatesb])
            P.act(bexp3, glb3, AF.Exp, [gatesb], [gatesb])
            P.act(beta3, Graw[:, :, 32:64], AF.Exp, [gatesb], [gatesb])
            P.tt("dve", gtmp, pt3, gc3, ALU.subtract, [pttb, gatesb], [gatesb])
            P.act(egl3, gtmp, AF.Exp, [gatesb], [gatesb])
            P.act(gl3, pt3, AF.Exp, [pttb], [gatesb])
            P.barrier()
            slots = []
            for i in range(2):
                sl = {}
                sl["kqT"] = carve(1280).bitcast(BF16).rearrange("p (c w t) -> p c w t", c=10, w=2)
                sl["ktok"] = carve(640).bitcast(BF16).rearrange("p (c d) -> p c d", c=10)
                sl["vtok"] = carve(640).bitcast(BF16).rearrange("p (c d) -> p c d", c=10)
                sl["zAs"] = carve(512).bitcast(BF16)
                sl["o1"] = carve(512).bitcast(BF16)
                sl["S"] = carve(128)
                sl["Sb"] = carve(64).bitcast(BF16)
                for nm in ("kqT", "ktok", "vtok", "zAs", "o1", "S", "Sb"):
                    sl[nm + "_b"] = Buf(f"{nm}{i}")
                slots.append(sl)
            oc = 0
            def carveC(n):
                nonlocal oc
                a = aux[:, oc:oc + n]
                oc += n
                assert oc <= 2048, oc
                return a
            CH = 1664
            chain_bases = [carve(CH), carve(CH), carve(CH), carveB(CH), carveC(CH)]
            vn_ring = [Ring([carve(64).bitcast(BF16) for _ in range(2)], f"vn{p}") for p in range(2)]
            fsq = carve(512).bitcast(BF16)
            fsqb = Buf("fsq")
            oacc = carveB(1024)
            oaccb = Buf("oacc")
            Gt = carveB(256)
            Gtb = Buf("Gt")

            def mk_chain_slot(base, nm):
                bfv = lambda a: a.bitcast(BF16)
                cs = dict(
                    em=base[:, 0:384], xyrtA=bfv(base[:, 0:256]), ab=bfv(base[:, 256:384]),
                    F=bfv(base[:, 384:576]), r1t1=bfv(base[:, 384:512]), a2=bfv(base[:, 512:576]),
                    egc=bfv(base[:, 576:640]), kbg=bfv(base[:, 576:640]),
                    y0=bfv(base[:, 640:704]), nw=bfv(base[:, 640:704]),
                    xa=bfv(base[:, 704:832]), msk=bfv(base[:, 832:1152]), xyrtB=bfv(base[:, 1152:1408]),
                    rf=bfv(base[:, 1408:1472]), vb=bfv(base[:, 1472:1536]), kg=bfv(base[:, 1536:1600]),
                    qg=bfv(base[:, 1600:1664]), buf=Buf("chain_" + nm))
                return cs
            chain_slots = [[mk_chain_slot(chain_bases[i], f"p0_{i}") for i in range(3)],
                           [mk_chain_slot(chain_bases[3], "p1_0"), mk_chain_slot(chain_bases[4], "p1_1")]]
            for j, cs_ in enumerate(chain_slots[0] + chain_slots[1]):
                cs_["pcs"] = psum[j // 2][:, (j % 2) * 256:(j % 2) * 256 + 256]
                cs_["pcb"] = pbuf[j // 2]
            cin_d = [nc.dram_tensor(f"cin{h}", [128, 128], F32) for h in range(16)]
            cout_d = [nc.dram_tensor(f"cout{h}", [256, 128], F32) for h in range(16)]
            cinb = [Buf(f"cin{h}") for h in range(16)]
            coutb = [Buf(f"cout{h}") for h in range(16)]
            BLK = ((0, 512), (512, 1024), (1024, 1280))

            cvq, cvk, cvv = chain_bases[0][:, 0:1280], chain_bases[1][:, 0:1280], chain_bases[2][:, 0:1280]
            sqq = chain_bases[3][:, 0:640].bitcast(BF16)
            sqk = chain_bases[3][:, 640:1280].bitcast(BF16)
            vTt = chain_bases[4][:, 0:640].bitcast(BF16)

            head_w = {}

            def load_head_w(h):
                head_w[h] = load_w(ewhd_d, h * D, 16, 0, 512)

            def prep_head(h):
                sl = slots[h % 2]
                wap, wb = head_w.pop(h)

                def proj_conv(which, cv, cvb):
                    ps3, pb3 = banks(3)
                    for bi, (n0, n1) in enumerate(BLK):
                        for dc in range(16):
                            P.mm(ps3[:, n0:n1], wap[:, dc, which * 128:(which + 1) * 128], hT[:, dc, n0:n1], dc == 0, dc == 15,
                                 [wb, hTb], [pb3[bi]])
                    grp = which * 16 + h
                    w0 = vec[:, V_CQ + grp * 3:V_CQ + grp * 3 + 1]
                    w1 = vec[:, V_CQ + grp * 3 + 1:V_CQ + grp * 3 + 2]
                    w2 = vec[:, V_CQ + grp * 3 + 2:V_CQ + grp * 3 + 3]
                    P.act(cv, ps3[:, 0:NTA], AF.Identity, pb3 + [vecb], [cvb], scale=w1)
                    P.stt(cv[:, 1:256], ps3[:, 0:255], w0, cv[:, 1:256], ALU.mult, ALU.add, pb3 + [vecb, cvb], [cvb])
                    P.stt(cv[:, 0:255], ps3[:, 1:256], w2, cv[:, 0:255], ALU.mult, ALU.add, pb3 + [vecb, cvb], [cvb])
                    cvx = cv[:, 256:NTA].rearrange("p (r t) -> p r t", t=64)
                    psx = ps3[:, 256:NTA].rearrange("p (r t) -> p r t", t=64)
                    P.stt(cvx[:, :, 1:64], psx[:, :, 0:63], w0, cvx[:, :, 1:64], ALU.mult, ALU.add, pb3 + [vecb, cvb], [cvb])
                    P.stt(cvx[:, :, 0:63], psx[:, :, 1:64], w2, cvx[:, :, 0:63], ALU.mult, ALU.add, pb3 + [vecb, cvb], [cvb])

                def to_tokmajor(src_T, srcb, dst, dstb):
                    pa, pab = bank()
                    pa_b = pa[:].bitcast(BF16)
                    for c in range(8):
                        P.tr(pa_b[:, c * 128:(c + 1) * 128], src_T(c), ident_b, [srcb, cbfb], [pab])
                    P.copy("act", dst[:, 0:8, :], pa_b[:, 0:1024].rearrange("p (c d) -> p c d", c=8), [pab], [dstb])
                    pa2, pa2b = bank()
                    pa2_b = pa2[:].bitcast(BF16)
                    for c in range(8, 10):
                        P.tr(pa2_b[:, (c - 8) * 128:(c - 7) * 128], src_T(c), ident_b, [srcb, cbfb], [pa2b])
                    P.copy("dve", dst[:, 8:10, :], pa2_b[:, 0:256].rearrange("p (c d) -> p c d", c=2), [pa2b], [dstb])

                def chain_qk(which, cv, sq, cvb, sqb):
                    proj_conv(which, cv, cvb)
                    yield
                    P.act(cv, cv, AF.Silu, [cvb], [cvb])
                    P.tt("pool", sq, cv, cv, ALU.mult, [cvb], [sqb])
                    yield
                    ss3, ssb3 = banks(3)
                    for bi, (n0, n1) in enumerate(BLK):
                        P.mm(ss3[:, n0:n1], ones_b, sq[:, n0:n1], True, True, [cbfb, sqb], [ssb3[bi]])
                    P.act(ss3[:, 0:NTA], ss3[:, 0:NTA], AF.Ln, ssb3 + [smallb], ssb3, bias=small[:, 64:65])
                    P.act(ss3[:, 0:NTA], ss3[:, 0:NTA], AF.Exp, ssb3, ssb3, scale=-0.5)
                    cv3 = cv.rearrange("p (c t) -> p c t", c=10)
                    ri3 = ss3[:, 0:NTA].rearrange("p (c t) -> p c t", c=10)
                    if which == 0:
                        P.stt(sl["kqT"][:, :, 1, :], cv3, 128.0 ** -0.5, ri3, ALU.mult, ALU.mult, [cvb] + ssb3, [sl["kqT_b"]])
                    else:
                        P.tt("dve", sl["kqT"][:, :, 0, :], cv3, ri3, ALU.mult, [cvb] + ssb3, [sl["kqT_b"]])
                        yield
                        to_tokmajor(lambda c: sl["kqT"][:, c, 0, :], sl["kqT_b"], sl["ktok"], sl["ktok_b"])

                def chain_v():
                    cvb, vTb = chain_slots[0][2]["buf"], chain_slots[1][1]["buf"]
                    proj_conv(2, cvv, cvb)
                    yield
                    P.act(vTt, cvv, AF.Silu, [cvb], [vTb])
                    yield
                    to_tokmajor(lambda c: vTt[:, c * 128:(c + 1) * 128], vTb, sl["vtok"], sl["vtok_b"])

                def chain_z():
                    pz2, pz2b = banks(2)
                    for bi in range(2):
                        n0 = 256 + bi * 512
                        for dc in range(16):
                            P.mm(pz2[:, bi * 512:(bi + 1) * 512], wap[:, dc, 384:512], hT[:, dc, n0:n0 + 512], dc == 0, dc == 15,
                                 [wb, hTb], [pz2b[bi]])
                    P.act(sl["zAs"], pz2[:, 0:1024], AF.Silu, pz2b, [sl["zAs_b"]])
                    P.memset("pool", sl["S"], 0.0, [sl["S_b"]])
                    P.memset("pool", sl["Sb"], 0.0, [sl["Sb_b"]])
                    return
                    yield

                gens = [chain_qk(0, cvq, sqq, chain_slots[0][0]["buf"], chain_slots[1][0]["buf"]),
                        chain_qk(1, cvk, sqk, chain_slots[0][1]["buf"], chain_slots[1][0]["buf"]), chain_v(), chain_z()]
                while gens:
                    keep = []
                    for g in gens:
                        try:
                            next(g)
                            keep.append(g)
                        except StopIteration:
                            pass
                    gens = keep

            def intra_gen(h, ph, c, cs):
                sl = slots[h % 2]
                gcol = ph * 16 + h
                kq = sl["kqT"]
                kqb = sl["kqT_b"]
                cb = cs["buf"]
                pcs, pcb = cs["pcs"], cs["pcb"]
                P.mm(pcs[:, 0:256], kq[:, c, 0, :], kq[:, c, :, :].rearrange("p w t -> p (w t)"), True, True, [kqb], [pcb])
                pes, peb = bank()
                P.tr(pes[:, 0:128], glb3[:, c, gcol:gcol + 1].to_broadcast([128, 128]), ident_f, [gatesb, cstb], [peb])
                P.tr(pes[:, 128:256], gc3[:, c, gcol:gcol + 1].to_broadcast([128, 128]), ident_f, [gatesb, cstb], [peb])
                P.tr(pes[:, 256:384], gc3[:, c, gcol:gcol + 1].to_broadcast([128, 128]), ident_f, [gatesb, cstb], [peb])
                mbase = C_MA if ph == 0 else C_MD
                P.act(cs["egc"], pes[:, 128:256], AF.Exp, [peb], [cb])
                P.tt("dve", cs["em"], pes[:, 0:384], cst[:, mbase:mbase + 384], ALU.add, [peb, cstb], [cb])
                P.tt("pool", cs["qg"], kq[:, c, 1, :], cs["egc"], ALU.mult, [kqb], [cb])
                yield
                P.act(cs["F"][:, 0:256], cs["em"][:, 0:256], AF.Exp, [gatesb], [cb], bias=ngc3[:, c, gcol:gcol + 1])
                P.act(cs["F"][:, 256:384], cs["em"][:, 256:384], AF.Exp, [gatesb], [cb], bias=glb3[:, c, gcol:gcol + 1],
                      scale=-1.0)
                yield
                P.tt("dve", cs["xa"], pcs[:, 0:256], cs["F"][:, 0:256], ALU.mult, [pcb], [cb])
                P.tt("dve", cs["y0"], pcs[:, 0:128], cs["F"][:, 256:384], ALU.mult, [pcb], [cb])
                P.tt("pool", cs["vb"], sl["vtok"][:, c, :], beta3[:, c, gcol:gcol + 1].to_broadcast([128, 128]), ALU.mult,
                     [sl["vtok_b"], gatesb], [cb])
                P.tt("pool", cs["kg"], sl["ktok"][:, c, :], egl3[:, c, gcol:gcol + 1].to_broadcast([128, 128]), ALU.mult,
                     [sl["ktok_b"], gatesb], [cb])
                yield
                msk = cs["msk"]
                m1x, m1y, m2y = (bm_b[2], bm_b[1], bm_b[3]) if ph == 0 else (bm_b[1], bm_b[2], bm_b[4])
                P.tt("pool", msk[:, 0:128], cs["xa"][:, 0:128], bm_b[0], ALU.mult, [cbfb], [cb])
                P.tt("pool", msk[:, 128:256], cs["y0"], bm_b[0], ALU.mult, [cbfb], [cb])
                yield
                P.tt("pool", msk[:, 256:384], cs["xa"][:, 0:128], m1x, ALU.mult, [cbfb], [cb])
                P.tt("pool", msk[:, 384:512], cs["y0"], m1y, ALU.mult, [cbfb], [cb])
                P.tt("pool", msk[:, 512:640], cs["y0"], m2y, ALU.mult, [cbfb], [cb])
                Xk, Yk = msk[:, 0:128], msk[:, 128:256]
                Rk = ident_b
                for k in range(5):
                    prs, prb = bank()
                    if k <= 3:
                        P.mm(prs[:, 0:128], Yk, Xk, True, True, [cb], [prb])
                        P.mm(prs[:, 128:256], Xk, Yk, True, True, [cb], [prb])
                    P.mm(prs[:, 256:384], ident_b, Rk, True, False, [cbfb, cb], [prb])
                    P.mm(prs[:, 256:384], Yk, nident_b if k == 0 else Rk, False, True, [cb, cbfb], [prb])
                    nx = cs["xyrtA"] if k % 2 == 0 else cs["xyrtB"]
                    lo = 0 if k <= 3 else 256
                    P.copy("act" if k % 2 == 0 else "dve", nx[:, lo:384], prs[:, lo:384], [prb], [cb])
                    Xk, Yk, Rk = nx[:, 0:128], nx[:, 128:256], nx[:, 256:384]
                    yield
                ptr_, ptrb = bank()
                ptr_b = ptr_[:].bitcast(BF16)
                P.tr(ptr_b[:, 0:128], Rk, ident_b, [cb, cbfb], [ptrb])
                P.copy("dve", cs["xyrtA"][:, 384:512], ptr_b[:, 0:128], [ptrb], [cb])
                Tk = cs["xyrtA"][:, 384:512]
                yield
                pl, plb = bank()
                P.mm(pl[:, 0:128], msk[:, 384:512], Rk, True, True, [cb], [plb])
                P.mm(pl[:, 128:256], msk[:, 256:384], Tk, True, True, [cb], [plb])
                P.copy("act", cs["ab"], pl[:, 0:256], [plb], [cb])
                yield
                pl, plb = bank()
                P.mm(pl[:, 0:128], Tk, cs["ab"][:, 0:128], True, True, [cb], [plb])
                P.mm(pl[:, 128:256], Rk, cs["ab"][:, 128:256], True, True, [cb], [plb])
                P.tt("dve", cs["r1t1"], cs["xyrtA"][:, 256:512], pl[:, 0:256], ALU.subtract, [plb], [cb])
                yield
                pl, plb = bank()
                P.mm(pl[:, 0:128], msk[:, 512:640], cs["r1t1"][:, 0:128], True, True, [cb], [plb])
                P.copy("act", cs["a2"], pl[:, 0:128], [plb], [cb])
                yield
                pl, plb = bank()
                P.mm(pl[:, 0:128], cs["r1t1"][:, 128:256], cs["a2"], True, True, [cb], [plb])
                P.tt("dve", cs["rf"], cs["r1t1"][:, 0:128], pl[:, 0:128], ALU.subtract, [plb], [cb])
                P.tt("pool", cs["kbg"], sl["ktok"][:, c, :], bexp3[:, c, gcol:gcol + 1].to_broadcast([128, 128]), ALU.mult,
                     [sl["ktok_b"], gatesb], [cb])
                yield
                pw, pwb = bank()
                P.mm(pw[:, 0:128], cs["kbg"], cs["rf"], True, True, [cb], [pwb])
                P.act(cs["nw"], pw[:, 0:128], AF.Identity, [pwb], [cb], scale=-1.0)

            def scan_gen(h, ph, c, cs, last):
                sl = slots[h % 2]
                gcol = ph * 16 + h
                cb = cs["buf"]
                S, Sb_ = sl["S"], sl["S_b"]
                Sb, Sbb = sl["Sb"], sl["Sb_b"]
                pv, pvb = bank()
                P.mm(pv[:, 0:128], cs["rf"], cs["vb"], True, False, [cb], [pvb])
                P.mm(pv[:, 0:128], cs["nw"], Sb, False, True, [cb, Sbb], [pvb])
                vn, vnb = vn_ring[ph].next()
                P.copy("act", vn, pv[:, 0:128], [pvb], [vnb])
                yield
                po, pob = bank()
                if c >= 2:
                    P.mm(po[:, 0:128], Sb, cs["qg"], True, False, [Sbb, cb], [pob])
                    P.mm(po[:, 0:128], vn, cs["xa"][:, 128:256], False, True, [vnb, cb], [pob])
                P.mm(po[:, 128:256], cs["kg"], vn, True, True, [cb, vnb], [pob])
                P.stt(S, S, gl3[:, c, gcol:gcol + 1], po[:, 128:256], ALU.mult, ALU.add, [Sb_, gatesb, pob], [Sb_])
                if not last:
                    P.copy("act", Sb, S, [Sb_], [Sbb])
                if c >= 2:
                    tk = (c - 2) * 128
                    if ph == 0:
                        P.copy("dve", sl["o1"][:, tk:tk + 128], po[:, 0:128], [pob], [sl["o1_b"]])
                    else:
                        P.tt("dve", oacc[:, tk:tk + 128], po[:, 0:128], sl["o1"][:, tk:tk + 128], ALU.add,
                             [pob, sl["o1_b"]], [oaccb])

            def run_block(phases):
                st = []
                for (h, ph) in phases:
                    order = list(range(10)) if ph == 0 else list(range(9, 1, -1))
                    st.append(dict(h=h, ph=ph, order=order, istart=0, idone=set(), sdone=0, scan=None, intras=[]))
                while True:
                    progressed = False
                    for p in st:
                        n = len(p["order"])
                        ns = len(chain_slots[p["ph"]])
                        while p["istart"] < n and p["istart"] - p["sdone"] < ns:
                            i = p["istart"]
                            g = intra_gen(p["h"], p["ph"], p["order"][i], chain_slots[p["ph"]][i % ns])
                            p["intras"].append((i, g))
                            p["istart"] += 1
                        if p["scan"] is None and p["sdone"] < n and p["sdone"] in p["idone"]:
                            i = p["sdone"]
                            if i == 0 and p["ph"] == 1:
                                exchange_finish(p["h"])
                            p["scan"] = scan_gen(p["h"], p["ph"], p["order"][i], chain_slots[p["ph"]][i % ns], i == n - 1)
                    for p in st:
                        keep = []
                        for (i, g) in p["intras"]:
                            try:
                                next(g)
                                keep.append((i, g))
                            except StopIteration:
                                p["idone"].add(i)
                            progressed = True
                        p["intras"] = keep
                        if p["scan"] is not None:
                            try:
                                next(p["scan"])
                            except StopIteration:
                                p["scan"] = None
                                p["sdone"] += 1
                            progressed = True
                    if not progressed:
                        break

            def exchange(h):
                sl = slots[h % 2]
                P.dma("sp", cin_d[h][:, :], sl["S"], [sl["S_b"]], [cinb[h]], f"xi{h}")
                P.op("pool", lambda e, h=h: e.collective_compute(
                    "AllGather", ALU.bypass, replica_groups=groups or [[0, 1], [2, 3], [4, 5], [6, 7]],
                    ins=[cin_d[h].ap().opt()], outs=[cout_d[h].ap().opt()]), [cinb[h]], [coutb[h]], key="cc", inc=1)
                P.dma("sp", Gt.rearrange("p (r n) -> p r n", r=2), cout_d[h][:, :].rearrange("(r p) n -> p r n", p=128),
                      [coutb[h]], [Gtb], f"xo{h}")

            def exchange_finish(h):
                sl = slots[h % 2]
                P.ts("dve", sl["S"], Gt[:, 0:128], vec[:, V_SEL:V_SEL + 1], None, ALU.mult, None, [Gtb, vecb], [sl["S_b"]])
                P.stt(sl["S"], Gt[:, 128:256], vec[:, V_SEL + 1:V_SEL + 2], sl["S"], ALU.mult, ALU.add,
                      [Gtb, vecb, sl["S_b"]], [sl["S_b"]])
                P.copy("act", sl["Sb"], sl["S"], [sl["S_b"]], [sl["Sb_b"]])

            def finish_head(h):
                sl = slots[h % 2]
                sq, sqb = fsq, fsqb
                P.tt("pool", sq[:, 0:1024], oacc, oacc, ALU.mult, [oaccb], [sqb])
                ss2, ssb2 = banks(2)
                for bi in range(2):
                    P.mm(ss2[:, bi * 512:(bi + 1) * 512], ones_b, sq[:, bi * 512:(bi + 1) * 512], True, True, [cbfb, sqb],
                         [ssb2[bi]])
                P.act(ss2[:, 0:1024], ss2[:, 0:1024], AF.Ln, ssb2 + [smallb], ssb2, bias=small[:, 64:65], scale=1.0 / 128)
                P.act(ss2[:, 0:1024], ss2[:, 0:1024], AF.Exp, ssb2, ssb2, scale=-0.5)
                P.stt(oacc, oacc, vec[:, V_HN:V_HN + 1], ss2[:, 0:1024], ALU.mult, ALU.mult, [oaccb, vecb] + ssb2, [oaccb])
                P.tt("pool", gat[:, h, :], oacc, sl["zAs"], ALU.mult, [oaccb, sl["zAs_b"]], [gb[h]])

            mod1_gen = emit_mod_gen(1) if with_mod1 else iter(())
            P.nw = 2
            P.nslot = 0
            load_head_w(0)
            for h in range(nheads + 1):
                if h < nheads:
                    prep_head(h)
                if h >= 1:
                    exchange(h - 1)
                next(mod1_gen, None)
                if h + 1 < nheads:
                    load_head_w(h + 1)
                ph_list = ([(h, 0)] if h < nheads else []) + ([(h - 1, 1)] if h >= 1 else [])
                P.bank_lo = 3
                run_block(ph_list)
                P.bank_lo = 0
                if h >= 1:
                    finish_head(h - 1)
                next(mod1_gen, None)
            for _ in mod1_gen:
                pass
            P.barrier()
            P.nw = 3
            P.nslot = 0
            if stop_after == "gdn":
                return
            emit_wout_pass(ewout_d, 0, gat, gb, 0, dd, first=True, ntok=1024, tok0=0)
            P.barrier()
            emit_mixer_B(dd, hT, hTb, gat, gb)
            emit_wout_pass(ewout_d, 2048, gat, gb, 0, dd, first=False, ntok=1024, tok0=0)
            P.barrier()
            stg_ring2 = Ring([G2_t[:, i * 2048:(i + 1) * 2048] for i in range(4)], "xstg2")
            for t in range(8):
                stg, stgb = stg_ring2.next()
                P.dma("sp", stg, xs_d[t * 128:(t + 1) * 128, :], (), [stgb], f"xl{t % 4}")
                for d4 in range(4):
                    ps, pb = bank()
                    for j in range(4):
                        dc = d4 * 4 + j
                        P.tr(ps[:, j * 128:(j + 1) * 128], stg[:, dc * 128:(dc + 1) * 128], ident_f, [stgb, cstb], [pb])
                    for j in range(4):
                        dc = d4 * 4 + j
                        dst = x1T[:, dc, t * 128:(t + 1) * 128]
                        P.tt("dve", dst, ps[:, j * 128:(j + 1) * 128], dst, ALU.add, [pb, x1b[dc][t // 4]], [x1b[dc][t // 4]])
            P.barrier()
            P.nw = 3
            P.nslot = 0

        def emit_mixer_B(dd, hT, hTb, gat, gb):
            BASE = 6144 + 2048 + 64
            for cg in range(16):
                wap, wb = load_w(ewmb_d, cg * D, 16, 0, 512)
                w0 = vec[:, V_CB + cg * 3:V_CB + cg * 3 + 1]
                w1 = vec[:, V_CB + cg * 3 + 1:V_CB + cg * 3 + 2]
                w2 = vec[:, V_CB + cg * 3 + 2:V_CB + cg * 3 + 3]
                for half in range(2):
                    n0 = 256 + half * 512
                    pp = []
                    for j in range(4):
                        ps, pb = bank()
                        for dc in range(16):
                            P.mm(ps[:, :], wap[:, dc, j * 128:(j + 1) * 128], hT[:, dc, n0:n0 + 512], dc == 0, dc == 15,
                                 [wb, hTb], [pb])
                        pp.append((ps, pb))
                    (pbg, pbgb), (pcg, pcgb), (phb, phbb), (pzb, pzbb) = pp
                    cgs, cgsb = mb_ring.next()
                    P.copy("act", cgs, pcg[:, :], [pcgb], [cgsb])
                    t1, t1b = mb_ring.next()
                    P.tt("dve", t1, phb[:, :], cgs, ALU.mult, [phbb, cgsb], [t1b])
                    cv, cvb = mb_ring.next()
                    P.act(cv, t1, AF.Identity, [t1b, vecb], [cvb], scale=w1)
                    cv3 = cv.rearrange("p (r t) -> p r t", t=64)
                    t13 = t1.rearrange("p (r t) -> p r t", t=64)
                    P.stt(cv3[:, :, 1:64], t13[:, :, 0:63], w0, cv3[:, :, 1:64], ALU.mult, ALU.add, [t1b, vecb, cvb], [cvb])
                    P.stt(cv3[:, :, 0:63], t13[:, :, 1:64], w2, cv3[:, :, 0:63], ALU.mult, ALU.add, [t1b, vecb, cvb], [cvb])
                    sz, szb = mb_ring.next()
                    P.act(sz, pzb[:, :], AF.Silu, [pzbb], [szb])
                    P.tt("dve", cv, pbg[:, :], cv, ALU.mult, [pbgb, cvb], [cvb])
                    P.tt("pool", gat[:, cg, half * 512:(half + 1) * 512], cv, sz, ALU.mult, [cvb, szb], [gb[cg]])

        if mode == "L0":
            emit_L0()
            for dc in range(16):
                P.dma("sp", out_d[:, dc * NT:(dc + 1) * NT], x1T[:, dc, :], [x1b[dc][0], x1b[dc][1]], [], f"out{dc % 4}")
        if mode == "full":
            emit_L0(True)
            pre = emit_L1_prelude(True)
            for half in range(2):
                P.barrier()
                emit_L1_half(half, *pre)
            for half in range(2):
                P.barrier()
                emit_final(half)
        if mode == "L1":
            x1all = Buf("x1all")
            for dc in range(16):
                P.dma("sp", x1T[:, dc, :], x1in_d[:, dc * NT:(dc + 1) * NT], (), [x1b[dc][0], x1b[dc][1]], "cld")
            pre = emit_L1_prelude()
            for half in range(2):
                P.barrier()
                emit_L1_half(half, *pre)
            for half in range(2):
                P.barrier()
                emit_final(half)

        with nc.Block() as block:
            P.flush(block)
    return nc


def make_consts():
    c = np.zeros((128, NCST), np.float32)
    idx = np.arange(128)
    c[:, C_ID:C_ID + 128] = np.eye(128)
    c[:, C_TA:C_TA + 128] = (idx[:, None] <= idx[None, :])
    c[:, C_TD:C_TD + 128] = (idx[:, None] >= idx[None, :])
    P_, F_ = idx[:, None], idx[None, :]
    for base, asc in ((C_MA, True), (C_MD, False)):
        if asc:
            m1 = F_ > P_; m2 = F_ >= P_; m3 = P_ > F_
        else:
            m1 = F_ < P_; m2 = F_ <= P_; m3 = P_ < F_
        c[:, base:base + 128] = np.where(m1, 0.0, -BIG)
        c[:, base + 128:base + 256] = np.where(m2, 0.0, -BIG)
        c[:, base + 256:base + 384] = np.where(m3, 0.0, BIG)
    for k, r in enumerate((1, 2, 4, 8)):
        Dm = np.zeros((128, 128), np.float64)
        for i in range(128):
            row = i // 64
            lo = max(i - r, row * 64)
            hi = min(i + r + 1, row * 64 + 64)
            Dm[i, lo:hi] = 1.0 / (hi - lo)
            Dm[i, i] -= 1.0
        c[:, C_BAND + 128 * k:C_BAND + 128 * (k + 1)] = Dm.T
    c[:, C_ONES:C_ONES + 128] = 1.0
    pb, fb = P_ // 32, F_ // 32
    m1_lo = ((pb == 1) & (fb == 0)) | ((pb == 3) & (fb == 2))
    m2_lo = (P_ >= 64) & (F_ < 64)
    for i, m in enumerate((pb == fb, m1_lo, m1_lo.T, m2_lo, m2_lo.T)):
        c[:, C_BM + 128 * i:C_BM + 128 * (i + 1)] = m
    return c


def fm(v, n):
    return np.ascontiguousarray(np.asarray(v, np.float32).reshape(n, 128).T)


def make_vec(inp, b, s):
    v = np.zeros((128, NV), np.float32)
    v[:, V_C:V_C + 16] = fm(inp["c"][b], 16)
    v[:, V_CC:V_CC + 16] = fm(inp["c_ctx"], 16)
    v[:, V_AB0:V_AB0 + 48] = fm(inp["ada_b"][0], 48)
    v[:, V_AB1:V_AB1 + 48] = fm(inp["ada_b"][1], 48)
    v[:, V_NW0:V_NW0 + 16] = fm(inp["norm_w"][0], 16)
    v[:, V_NW1:V_NW1 + 16] = fm(inp["norm_w"][1], 16)
    v[:, V_LNW:V_LNW + 16] = fm(inp["o_ln_w"][0], 16)
    v[:, V_LNB:V_LNB + 16] = fm(inp["o_ln_b"][0], 16)
    v[:, V_PS:V_PS + 16] = fm(inp["o_pool_scale"][0], 16)
    v[:, V_FNW:V_FNW + 16] = fm(inp["final_norm_w"], 16)
    cq = np.asarray(inp["e_conv_qkv"][0], np.float32)
    cb = np.asarray(inp["e_conv_b"][0], np.float32)
    if s == 1:
        cq = cq[::-1]
        cb = cb[::-1]
    v[:, V_CQ:V_CQ + 144] = np.stack([fm(cq[t], 48) for t in range(3)], axis=2).reshape(128, 144)
    v[:, V_CB:V_CB + 48] = np.stack([fm(cb[t], 16) for t in range(3)], axis=2).reshape(128, 48)
    v[:, V_HN] = np.asarray(inp["e_head_norm"][0], np.float32)
    v[:, V_SEL] = 1.0 if s == 1 else 0.0
    v[:, V_SEL + 1] = 1.0 if s == 0 else 0.0
    dirs = (0, 1) if s == 0 else (1, 0)
    dtb = np.asarray(inp["e_dt_bias"][0], np.float32)
    alog = np.asarray(inp["e_a_log"][0], np.float32)
    v[:, V_DTB:V_DTB + 32] = np.concatenate([dtb[dirs[0]], dtb[dirs[1]]])[None, :]
    v[:, V_ALOG:V_ALOG + 32] = np.concatenate([alog[dirs[0]], alog[dirs[1]]])[None, :]
    return v


def common_maps(inp):
    f = lambda a: np.ascontiguousarray(np.asarray(a, np.float32))
    return {
        "cst": make_consts(),
        "ada_w0": f(inp["ada_w"][0]), "ada_w1": f(inp["ada_w"][1]),
        "o_w_in": f(inp["o_w_in"][0]),
        "o_pool_w": f(np.asarray(inp["o_pool_w"][0]).reshape(2048, 512)),
        "o_w_out": f(inp["o_w_out"][0]),
    }


def core_maps_L1(inp, b, s):
    ws = np.asarray(inp["o_w_s"][0], np.float32)
    bs = np.asarray(inp["o_b_s"][0], np.float32)
    if s == 1:
        ws = ws[:, ::-1, ::-1]
        bs = bs[:, ::-1]
    return {
        "vec": make_vec(inp, b, s),
        "ws": np.ascontiguousarray(ws.transpose(1, 0, 2).reshape(128, 2048)),
        "bsb": np.ascontiguousarray(np.broadcast_to(bs.reshape(1, 2048), (128, 2048))),
    }


_NC_CACHE = {}


def kernel(**inputs):
    inp = {k: np.asarray(v) for k, v in inputs.items()}
    if "full" not in _NC_CACHE:
        _NC_CACHE["full"] = build("full")
    nc = _NC_CACHE["full"]
    com = common_maps(inp)
    com.update(common_maps_L0(inp))
    maps = []
    for core in range(8):
        b, s = core // 2, core % 2
        m = dict(com)
        m.update(core_maps_L1(inp, b, s))
        m.update(core_maps_L0(inp, b, s))
        maps.append(m)
    res = run_bass_kernel_spmd(nc, maps, core_ids=list(range(8)))
    out = np.empty((4, 2048, D), np.float32)
    for core in range(8):
        b, s = core // 2, core % 2
        o = np.asarray(res.results[core]["out"], np.float32)
        if s == 1:
            o = o[::-1]
        out[b, s * NT:(s + 1) * NT] = o
    return out


def common_maps_L0(inp):
    f = lambda a: np.ascontiguousarray(np.asarray(a, np.float32))
    w = np.asarray(inp["e_w_in"][0], np.float32)
    hd = np.empty((16, D, 512), np.float32)
    mb = np.empty((16, D, 512), np.float32)
    base = 6144 + 2048 + 64
    for h in range(16):
        for j, c0 in enumerate((h * 128, 2048 + h * 128, 4096 + h * 128, 6144 + h * 128)):
            hd[h, :, j * 128:(j + 1) * 128] = w[:, c0:c0 + 128]
        for j in range(4):
            c0 = base + j * 2048 + h * 128
            mb[h, :, j * 128:(j + 1) * 128] = w[:, c0:c0 + 128]
    return {"e_w_hd": hd.reshape(16 * D, 512), "e_w_mb": mb.reshape(16 * D, 512), "e_w_out": f(inp["e_w_out"][0])}


def core_maps_L0(inp, b, s):
    xs = np.asarray(inp["x"][b, s * NT:(s + 1) * NT], np.float32)
    cx = np.asarray(inp["ctx"][b], np.float32)
    if s == 1:
        xs = xs[::-1]
        cx = cx[::-1]
    d1, d2 = (0, 1) if s == 0 else (1, 0)
    base = 8192
    cols = np.concatenate([np.arange(base + d1 * 16, base + d1 * 16 + 16), np.arange(base + d2 * 16, base + d2 * 16 + 16),
                           np.arange(base + 32 + d1 * 16, base + 32 + d1 * 16 + 16),
                           np.arange(base + 32 + d2 * 16, base + 32 + d2 * 16 + 16)])
    wab = np.asarray(inp["e_w_in"][0], np.float32)[:, cols]
    return {"xs": np.ascontiguousarray(xs), "ctxs": np.ascontiguousarray(cx), "w_ab": np.ascontiguousarray(wab)}
```
